# Optimizing a Trainium2 kernel written in Bass

```python
import math
import jax
import jax.numpy as jnp
from jax import lax
import numpy as np

D_MODEL = 1024
BATCH = 32
SEQ = 256
DEPTH = 4
DEC_BATCH = 2
DEC_SEQ = 1024
PAST_LEN = 256

GRID_W = 64
N_EVEN = (DEPTH + 1) // 2
N_ODD = DEPTH // 2
EPS = 1e-6
CONV_K = 5
NH_A = 4
DK_A = 128
DV_A = 128
CHUNK_A = 64
A_QK = NH_A * DK_A
A_V = NH_A * DV_A
NH_B = 8
HP_B = 64
DSTATE = 128
NG_B = 2
R_B = NH_B // NG_B
CHUNK_B = 64
B_INNER = NH_B * HP_B
B_BC = NG_B * DSTATE
B_XBC = B_INNER + 2 * B_BC
EVEN_SIZES = (2 * A_QK, A_V, A_V, 2 * NH_A, 2 * NH_A, B_INNER, B_XBC, 2 * NH_B)
EVEN_IN = 2 * A_QK + 2 * A_V + 4 * NH_A + B_INNER + B_XBC + 2 * NH_B
EVEN_OUT = A_V + B_INNER
NH_C = 8
NKV_C = 2
G_C = NH_C // NKV_C
HD_C = 64
WINDOW = 128
QBLK = 128
NH_D = 8
Q_RANK = 256
KV_RANK = 128
NOPE_D = 64
ROPE_D = 32
V_D = 64
MLA_SCALE = (NOPE_D + ROPE_D) ** -0.5
ODD_SIZES = (NH_C * HD_C, NKV_C * HD_C, NKV_C * HD_C, Q_RANK, KV_RANK, ROPE_D)
ODD_IN = NH_C * HD_C + 2 * NKV_C * HD_C + Q_RANK + KV_RANK + ROPE_D
ODD_OUT = NH_C * HD_C + NH_D * V_D
D_FF = 4 * D_MODEL
ROPE_BASE = 10000.0

kernel_name = "hybrid_mlstm_ssd_swa_mla_diffusion_step"

F32 = jnp.float32


def _split(x, sizes):
    out, start = [], 0
    for s in sizes:
        out.append(x[..., start:start + s])
        start += s
    return out


def _flip(a):
    return jnp.flip(a, axis=1)


def _chunks(a, L):
    Bsz, T = a.shape[:2]
    return jnp.moveaxis(a.reshape((Bsz, T // L, L) + a.shape[2:]), 1, 0)


def _unchunk(a):
    nc, Bsz, L = a.shape[:3]
    return jnp.moveaxis(a, 0, 1).reshape((Bsz, nc * L) + a.shape[3:])


def rmsnorm(x, w):
    xf = x.astype(F32)
    y = xf * lax.rsqrt(jnp.mean(xf * xf, axis=-1, keepdims=True) + EPS)
    return (y * w.astype(F32)).astype(x.dtype)


def group_rms(x, n_groups, w):
    Bsz, T, C = x.shape
    xg = x.astype(F32).reshape(Bsz, T, n_groups, C // n_groups)
    xg = xg * lax.rsqrt(jnp.mean(xg * xg, axis=-1, keepdims=True) + EPS)
    return xg.reshape(Bsz, T, C) * w.astype(F32)


def dwconv(x, w, b):
    C = x.shape[-1]
    y = lax.conv_general_dilated(x, w[:, None, :].astype(x.dtype), window_strides=(1,),
                                 padding=[(CONV_K // 2, CONV_K // 2)],
                                 dimension_numbers=('NWC', 'WIO', 'NWC'),
                                 feature_group_count=C)
    return y + b.astype(x.dtype)


def axial_rope(rows, rot_dim):
    quarter = rot_dim // 4
    inv = ROPE_BASE ** (-jnp.arange(quarter, dtype=F32) / quarter)
    r = jnp.repeat(jnp.arange(rows, dtype=F32), GRID_W)
    col = jnp.tile(jnp.arange(GRID_W, dtype=F32), rows)
    ang = jnp.concatenate([r[:, None] * inv, col[:, None] * inv], axis=-1)
    return jnp.cos(ang), jnp.sin(ang)


def apply_rope(x, cos, sin):
    half = x.shape[-1] // 2
    x1, x2 = x[..., :half], x[..., half:]
    c = cos[:, None, :].astype(x.dtype)
    s = sin[:, None, :].astype(x.dtype)
    return jnp.concatenate([x1 * c - x2 * s, x1 * s + x2 * c], axis=-1)


def modulation(cond, w_ada, b_ada):
    m = jax.nn.silu(cond) @ w_ada + b_ada
    return jnp.split(m[:, None, :], 6, axis=-1)


def mlstm_scan(q, k, v, log_i, log_f, C0, n0, m0):
    mask = jnp.tril(jnp.ones((CHUNK_A, CHUNK_A), bool))[None, :, :, None]

    def step(carry, inp):
        C, n, m = carry
        qc, kc, vc, ic, fc = inp
        b = jnp.cumsum(fc, axis=1)
        dmat = jnp.where(mask, b[:, :, None, :] - b[:, None, :, :] + ic[:, None, :, :], -jnp.inf)
        inter = b + m[:, None, :]
        m_t = jnp.maximum(inter, jnp.max(dmat, axis=2))
        s = jnp.einsum('bthd,bshd->btsh', qc, kc) * jnp.exp(dmat - m_t[:, :, None, :])
        w_inter = jnp.exp(inter - m_t)
        num = jnp.einsum('btsh,bshv->bthv', s, vc) + w_inter[..., None] * jnp.einsum('bthd,bhdv->bthv', qc, C)
        den = jnp.sum(s, axis=2) + w_inter * jnp.einsum('bthd,bhd->bth', qc, n)
        h = num / jnp.maximum(jnp.abs(den), jnp.exp(-m_t))[..., None]
        b_last = b[:, -1]
        g = b_last[:, None, :] - b + ic
        m_new = jnp.maximum(b_last + m, jnp.max(g, axis=1))
        wg = jnp.exp(g - m_new[:, None, :])
        keep = jnp.exp(b_last + m - m_new)
        C = keep[..., None, None] * C + jnp.einsum('bsh,bshd,bshv->bhdv', wg, kc, vc)
        n = keep[..., None] * n + jnp.einsum('bsh,bshd->bhd', wg, kc)
        return (C, n, m_new), h

    xs = (_chunks(q, CHUNK_A), _chunks(k, CHUNK_A), _chunks(v, CHUNK_A),
          _chunks(log_i, CHUNK_A), _chunks(log_f, CHUNK_A))
    (C, n, m), hs = lax.scan(step, (C0, n0, m0), xs)
    return _unchunk(hs), C, n, m


def ssd_scan(x, dt, A, Bm, Cm, S0):
    mask = jnp.tril(jnp.ones((CHUNK_B, CHUNK_B), bool))[None, :, :, None, None]

    def step(S, inp):
        xc, dtc, Bc, Cc = inp
        acum = jnp.cumsum(dtc * A, axis=1)
        decay = jnp.exp(jnp.where(mask, acum[:, :, None] - acum[:, None, :], -jnp.inf))
        w = jnp.einsum('btgn,bsgn->btsg', Cc, Bc)[..., None] * decay * dtc[:, None]
        y = (jnp.einsum('btsgr,bsgrp->btgrp', w, xc)
             + jnp.einsum('btgn,bgrpn->btgrp', Cc, S) * jnp.exp(acum)[..., None])
        a_last = acum[:, -1]
        wS = jnp.exp(a_last[:, None] - acum) * dtc
        S = jnp.exp(a_last)[..., None, None] * S + jnp.einsum('bsgr,bsgrp,bsgn->bgrpn', wS, xc, Bc)
        return S, y

    xs = (_chunks(x, CHUNK_B), _chunks(dt, CHUNK_B), _chunks(Bm, CHUNK_B), _chunks(Cm, CHUNK_B))
    S, ys = lax.scan(step, S0, xs)
    return _unchunk(ys), S


def even_mixer(h, w_in, conv_a_w, conv_a_b, conv_b_w, conv_b_b, gate_b, a_norm_w,
               dt_bias, a_log, d_skip, b_norm_w, w_out, C0, n0, m0, S0):
    Bsz, T, _ = h.shape
    qk, v, o, ig, fg, z, xbc, dt = _split(h @ w_in, EVEN_SIZES)
    qk = jax.nn.silu(dwconv(qk, conv_a_w, conv_a_b)).astype(F32)
    q = qk[..., :A_QK].reshape(Bsz, T, NH_A, DK_A)
    k = qk[..., A_QK:].reshape(Bsz, T, NH_A, DK_A) * (DK_A ** -0.5)
    v = v.astype(F32).reshape(Bsz, T, NH_A, DV_A)
    gates = (jnp.concatenate([ig, fg], axis=-1) + gate_b).astype(F32)
    log_i = gates[..., :2 * NH_A]
    log_f = jax.nn.log_sigmoid(gates[..., 2 * NH_A:])
    C0, n0, m0 = C0.astype(F32), n0.astype(F32), m0.astype(F32)
    h_f, Cf, nf, mf = mlstm_scan(q, k, v, log_i[..., :NH_A], log_f[..., :NH_A], C0[:, 0], n0[:, 0], m0[:, 0])
    h_b, Cb, nb_, mb = mlstm_scan(_flip(q), _flip(k), _flip(v), _flip(log_i[..., NH_A:]),
                                  _flip(log_f[..., NH_A:]), C0[:, 1], n0[:, 1], m0[:, 1])
    ha = jax.nn.sigmoid(o.astype(F32)) * (h_f + _flip(h_b)).reshape(Bsz, T, A_V)
    ha = group_rms(ha, NH_A, a_norm_w).astype(h.dtype)
    xbc = jax.nn.silu(dwconv(xbc, conv_b_w, conv_b_b)).astype(F32)
    xs = xbc[..., :B_INNER].reshape(Bsz, T, NG_B, R_B, HP_B)
    Bm = xbc[..., B_INNER:B_INNER + B_BC].reshape(Bsz, T, NG_B, DSTATE)
    Cm = xbc[..., B_INNER + B_BC:].reshape(Bsz, T, NG_B, DSTATE)
    dt = jax.nn.softplus(dt.astype(F32).reshape(Bsz, T, 2, NH_B) + dt_bias.astype(F32))
    dt = dt.reshape(Bsz, T, 2, NG_B, R_B)
    A = -jnp.exp(a_log.astype(F32)).reshape(2, NG_B, R_B)
    S0 = S0.astype(F32).reshape(Bsz, 2, NG_B, R_B, HP_B, DSTATE)
    y_f, Sf = ssd_scan(xs, dt[:, :, 0], A[0], Bm, Cm, S0[:, 0])
    y_b, Sb = ssd_scan(_flip(xs), _flip(dt[:, :, 1]), A[1], _flip(Bm), _flip(Cm), S0[:, 1])
    y = y_f + _flip(y_b) + d_skip.astype(F32).reshape(NG_B, R_B)[..., None] * xs
    y = y.reshape(Bsz, T, B_INNER) * jax.nn.silu(z.astype(F32))
    yb = group_rms(y, NG_B, b_norm_w).astype(h.dtype)
    out = jnp.concatenate([ha, yb], axis=-1) @ w_out
    C_new = jnp.stack([Cf, Cb], axis=1)
    n_new = jnp.stack([nf, nb_], axis=1)
    m_new = jnp.stack([mf, mb], axis=1)
    S_new = jnp.stack([Sf, Sb], axis=1).reshape(Bsz, 2, NH_B, HP_B, DSTATE)
    return out, (C_new, n_new, m_new, S_new)


def softmax_sink(s, sink):
    if sink is None:
        return jax.nn.softmax(s, axis=-1)
    sk = jnp.broadcast_to(sink.astype(F32)[None, :, :, None, None], s.shape[:-1] + (1,))
    return jax.nn.softmax(jnp.concatenate([s, sk], axis=-1), axis=-1)[..., :-1]


def blocked_attention(q, k, v, sink, scale):
    Bsz, Tq = q.shape[:2]
    nb = Tq // QBLK
    qb = jnp.moveaxis(q.reshape((Bsz, nb, QBLK) + q.shape[2:]), 1, 0)

    def one(qi):
        s = jnp.einsum('bqhgd,bkhd->bhgqk', qi, k).astype(F32) * scale
        p = softmax_sink(s, sink)
        return jnp.einsum('bhgqk,bkhd->bqhgd', p.astype(v.dtype), v)

    o = lax.map(one, qb)
    return jnp.moveaxis(o, 0, 1).reshape((Bsz, Tq) + o.shape[3:])


def banded_attention(q, k, v, kc, vc, sink, scale):
    Bsz, T = q.shape[:2]
    nb = T // QBLK
    pad = ((0, 0), (QBLK, QBLK), (0, 0), (0, 0))
    kp = jnp.pad(k, pad).reshape((Bsz, nb + 2, QBLK) + k.shape[2:])
    vp = jnp.pad(v, pad).reshape((Bsz, nb + 2, QBLK) + v.shape[2:])
    kw = jnp.concatenate([kp[:, :-2], kp[:, 1:-1], kp[:, 2:]], axis=2)
    vw = jnp.concatenate([vp[:, :-2], vp[:, 1:-1], vp[:, 2:]], axis=2)
    qpos = jnp.arange(T).reshape(nb, QBLK)
    kpos = (jnp.arange(nb)[:, None] - 1) * QBLK + jnp.arange(3 * QBLK)[None, :]
    valid = ((kpos[:, None, :] >= 0) & (kpos[:, None, :] < T)
             & (jnp.abs(qpos[:, :, None] - kpos[:, None, :]) <= WINDOW))
    qb = q.reshape((Bsz, nb, QBLK) + q.shape[2:])
    n_loc = 3 * QBLK

    def one(args):
        qi, kwi, vwi, vi = args
        s_loc = jnp.einsum('bqhgd,bkhd->bhgqk', qi, kwi).astype(F32) * scale
        s_loc = jnp.where(vi, s_loc, -jnp.inf)
        s_ctx = jnp.einsum('bqhgd,bkhd->bhgqk', qi, kc).astype(F32) * scale
        p = softmax_sink(jnp.concatenate([s_loc, s_ctx], axis=-1), sink).astype(v.dtype)
        return (jnp.einsum('bhgqk,bkhd->bqhgd', p[..., :n_loc], vwi)
                + jnp.einsum('bhgqk,bkhd->bqhgd', p[..., n_loc:], vc))

    o = lax.map(one, (jnp.moveaxis(qb, 1, 0), jnp.moveaxis(kw, 1, 0), jnp.moveaxis(vw, 1, 0), valid))
    return jnp.moveaxis(o, 0, 1).reshape((Bsz, T) + o.shape[3:])


def odd_project(h, w_in, q_a_norm, kv_a_norm, w_q_b):
    Bsz, T, _ = h.shape
    qc, kc, vc, qa, kva, kpe = _split(h @ w_in, ODD_SIZES)
    qc = qc.reshape(Bsz, T, NH_C, HD_C)
    kc = kc.reshape(Bsz, T, NKV_C, HD_C)
    vc = vc.reshape(Bsz, T, NKV_C, HD_C)
    qd = (rmsnorm(qa, q_a_norm) @ w_q_b).reshape(Bsz, T, NH_D, NOPE_D + ROPE_D)
    ckv = rmsnorm(kva, kv_a_norm)
    return qc, kc, vc, qd, ckv, kpe


def mla_keys_values(ckv, kpe, w_kv_b):
    Bsz, S, _ = ckv.shape
    kv = (ckv @ w_kv_b).reshape(Bsz, S, NH_D, NOPE_D + V_D)
    k = jnp.concatenate([kv[..., :NOPE_D],
                         jnp.broadcast_to(kpe[:, :, None, :], (Bsz, S, NH_D, ROPE_D))], axis=-1)
    return k, kv[..., NOPE_D:]


def odd_mixer_ctx(h, w_in, sink, q_a_norm, kv_a_norm, w_q_b, w_kv_b, w_out):
    Bsz, T, _ = h.shape
    qc, kc, vc, qd, ckv, kpe = odd_project(h, w_in, q_a_norm, kv_a_norm, w_q_b)
    oc = blocked_attention(qc.reshape(Bsz, T, NKV_C, G_C, HD_C), kc, vc, sink.reshape(NKV_C, G_C), HD_C ** -0.5)
    k, v = mla_keys_values(ckv, kpe, w_kv_b)
    od = blocked_attention(qd[:, :, :, None], k, v, None, MLA_SCALE)
    y = jnp.concatenate([oc.reshape(Bsz, T, -1), od.reshape(Bsz, T, -1)], axis=-1) @ w_out
    return y, (kc, vc, ckv, kpe)


def odd_mixer_lat(h, k_ctx, v_ctx, ckv_ctx, kpe_ctx, cos_c, sin_c, cos_d, sin_d,
                  w_in, sink, q_a_norm, kv_a_norm, w_q_b, w_kv_b, w_out):
    Bsz, T, _ = h.shape
    qc, kc, vc, qd, ckv, kpe = odd_project(h, w_in, q_a_norm, kv_a_norm, w_q_b)
    qc = apply_rope(qc, cos_c, sin_c)
    kc = apply_rope(kc, cos_c, sin_c)
    oc = banded_attention(qc.reshape(Bsz, T, NKV_C, G_C, HD_C), kc, vc, k_ctx, v_ctx,
                          sink.reshape(NKV_C, G_C), HD_C ** -0.5)
    qd = jnp.concatenate([qd[..., :NOPE_D], apply_rope(qd[..., NOPE_D:], cos_d, sin_d)], axis=-1)
    kpe = apply_rope(kpe[:, :, None, :], cos_d, sin_d)[:, :, 0]
    k, v = mla_keys_values(jnp.concatenate([ckv_ctx, ckv], axis=1),
                           jnp.concatenate([kpe_ctx, kpe], axis=1), w_kv_b)
    od = blocked_attention(qd[:, :, :, None], k, v, None, MLA_SCALE)
    return jnp.concatenate([oc.reshape(Bsz, T, -1), od.reshape(Bsz, T, -1)], axis=-1) @ w_out


def sq_relu_mlp(h, w_up, w_down):
    return jnp.square(jax.nn.relu(h @ w_up)) @ w_down


def setup_inputs(seed: int = 0) -> dict:
    key = jax.random.key(seed)
    ks = iter(jax.random.split(key, 64))
    D = D_MODEL

    def nrm(shape, scale):
        return scale * jax.random.normal(next(ks), shape, F32)

    x_prompt = nrm((BATCH, SEQ, D), 1.0)
    x_sample = nrm((DEC_BATCH, DEC_SEQ, D), 1.0)
    c = nrm((DEC_BATCH, D), 1.0)
    state_mlstm_C = nrm((DEC_BATCH, N_EVEN, 2, NH_A, DK_A, DV_A), 0.1)
    state_mlstm_n = nrm((DEC_BATCH, N_EVEN, 2, NH_A, DK_A), 0.1)
    state_mlstm_m = nrm((DEC_BATCH, N_EVEN, 2, NH_A), 0.5)
    state_ssd = nrm((DEC_BATCH, N_EVEN, 2, NH_B, HP_B, DSTATE), 0.1)
    cache_gqa_k = nrm((DEC_BATCH, N_ODD, PAST_LEN, NKV_C, HD_C), 1.0)
    cache_gqa_v = nrm((DEC_BATCH, N_ODD, PAST_LEN, NKV_C, HD_C), 1.0)
    cache_mla_ckv = nrm((DEC_BATCH, N_ODD, PAST_LEN, KV_RANK), 1.0)
    cache_mla_kpe = nrm((DEC_BATCH, N_ODD, PAST_LEN, ROPE_D), 1.0)
    c_ctx = nrm((D,), 1.0)
    w_ada = nrm((DEPTH, D, 6 * D), 0.5 * D ** -0.5)
    b_ada = nrm((DEPTH, 6 * D), 0.02)
    norm_g = 1.0 + nrm((DEPTH, 4, D), 0.05)
    w_up = nrm((DEPTH, D, D_FF), D ** -0.5)
    w_down = nrm((DEPTH, D_FF, D), D_FF ** -0.5)
    w_in_even = nrm((N_EVEN, D, EVEN_IN), D ** -0.5)
    conv_a_w = nrm((N_EVEN, CONV_K, 2 * A_QK), CONV_K ** -0.5)
    conv_a_b = nrm((N_EVEN, 2 * A_QK), 0.02)
    conv_b_w = nrm((N_EVEN, CONV_K, B_XBC), CONV_K ** -0.5)
    conv_b_b = nrm((N_EVEN, B_XBC), 0.02)
    gate_b = jnp.concatenate([nrm((N_EVEN, 2 * NH_A), 0.1),
                              3.0 + nrm((N_EVEN, 2 * NH_A), 0.5)], axis=-1)
    a_norm_w = 1.0 + nrm((N_EVEN, A_V), 0.05)
    dt0 = jnp.exp(jax.random.uniform(next(ks), (N_EVEN, 2, NH_B), F32, math.log(1e-3), math.log(1e-1)))
    dt_bias = dt0 + jnp.log(-jnp.expm1(-dt0))
    a_log = jnp.log(jax.random.uniform(next(ks), (N_EVEN, 2, NH_B), F32, 1.0, 16.0))
    d_skip = 1.0 + nrm((N_EVEN, NH_B), 0.1)
    b_norm_w = 1.0 + nrm((N_EVEN, B_INNER), 0.05)
    w_out_even = nrm((N_EVEN, EVEN_OUT, D), EVEN_OUT ** -0.5)
    w_in_odd = nrm((N_ODD, D, ODD_IN), D ** -0.5)
    sink = nrm((N_ODD, NH_C), 0.5)
    q_a_norm = 1.0 + nrm((N_ODD, Q_RANK), 0.05)
    kv_a_norm = 1.0 + nrm((N_ODD, KV_RANK), 0.05)
    w_q_b = nrm((N_ODD, Q_RANK, NH_D * (NOPE_D + ROPE_D)), Q_RANK ** -0.5)
    w_kv_b = nrm((N_ODD, KV_RANK, NH_D * (NOPE_D + V_D)), KV_RANK ** -0.5)
    w_out_odd = nrm((N_ODD, ODD_OUT, D), ODD_OUT ** -0.5)
    return {"x_prompt": x_prompt, "x_sample": x_sample, "c": c,
            "state_mlstm_C": state_mlstm_C, "state_mlstm_n": state_mlstm_n,
            "state_mlstm_m": state_mlstm_m, "state_ssd": state_ssd,
            "cache_gqa_k": cache_gqa_k, "cache_gqa_v": cache_gqa_v,
            "cache_mla_ckv": cache_mla_ckv, "cache_mla_kpe": cache_mla_kpe,
            "c_ctx": c_ctx, "w_ada": w_ada, "b_ada": b_ada, "norm_g": norm_g,
            "w_up": w_up, "w_down": w_down, "w_in_even": w_in_even,
            "conv_a_w": conv_a_w, "conv_a_b": conv_a_b, "conv_b_w": conv_b_w, "conv_b_b": conv_b_b,
            "gate_b": gate_b, "a_norm_w": a_norm_w, "dt_bias": dt_bias, "a_log": a_log,
            "d_skip": d_skip, "b_norm_w": b_norm_w, "w_out_even": w_out_even,
            "w_in_odd": w_in_odd, "sink": sink, "q_a_norm": q_a_norm, "kv_a_norm": kv_a_norm,
            "w_q_b": w_q_b, "w_kv_b": w_kv_b, "w_out_odd": w_out_odd}


def reference(x_prompt, x_sample, c, state_mlstm_C, state_mlstm_n, state_mlstm_m, state_ssd,
              cache_gqa_k, cache_gqa_v, cache_mla_ckv, cache_mla_kpe, c_ctx, w_ada, b_ada, norm_g,
              w_up, w_down, w_in_even, conv_a_w, conv_a_b, conv_b_w, conv_b_b, gate_b, a_norm_w,
              dt_bias, a_log, d_skip, b_norm_w, w_out_even, w_in_odd, sink, q_a_norm, kv_a_norm,
              w_q_b, w_kv_b, w_out_odd):
    xp, xs = x_prompt, x_sample
    nbp = xp.shape[0]
    rows = xs.shape[1] // GRID_W
    cos_c, sin_c = axial_rope(rows, HD_C)
    cos_d, sin_d = axial_rope(rows, ROPE_D)
    zC = jnp.zeros((nbp, 2, NH_A, DK_A, DV_A), F32)
    zn = jnp.zeros((nbp, 2, NH_A, DK_A), F32)
    zm = jnp.zeros((nbp, 2, NH_A), F32)
    zS = jnp.zeros((nbp, 2, NH_B, HP_B, DSTATE), F32)
    l_C, l_n, l_m, l_S, l_k, l_v, l_ckv, l_kpe = [], [], [], [], [], [], [], []
    for l in range(DEPTH):
        j = l // 2
        sh1p, sc1p, g1p, sh2p, sc2p, g2p = modulation(c_ctx[None, :], w_ada[l], b_ada[l])
        sh1s, sc1s, g1s, sh2s, sc2s, g2s = modulation(c, w_ada[l], b_ada[l])
        hp = rmsnorm(xp, norm_g[l, 0]) * (1.0 + sc1p) + sh1p
        hs = rmsnorm(xs, norm_g[l, 0]) * (1.0 + sc1s) + sh1s
        if l % 2 == 0:
            ew = (w_in_even[j], conv_a_w[j], conv_a_b[j], conv_b_w[j], conv_b_b[j], gate_b[j],
                  a_norm_w[j], dt_bias[j], a_log[j], d_skip[j], b_norm_w[j], w_out_even[j])
            yp, (Cn, nn_, mn, Sn) = even_mixer(hp, *ew, zC, zn, zm, zS)
            ys, _ = even_mixer(hs, *ew, state_mlstm_C[:, j], state_mlstm_n[:, j],
                               state_mlstm_m[:, j], state_ssd[:, j])
            l_C.append(Cn)
            l_n.append(nn_)
            l_m.append(mn)
            l_S.append(Sn)
        else:
            ow = (w_in_odd[j], sink[j], q_a_norm[j], kv_a_norm[j], w_q_b[j], w_kv_b[j], w_out_odd[j])
            yp, (kn, vn, ckvn, kpen) = odd_mixer_ctx(hp, *ow)
            ys = odd_mixer_lat(hs, cache_gqa_k[:, j], cache_gqa_v[:, j], cache_mla_ckv[:, j],
                               cache_mla_kpe[:, j], cos_c, sin_c, cos_d, sin_d, *ow)
            l_k.append(kn)
            l_v.append(vn)
            l_ckv.append(ckvn)
            l_kpe.append(kpen)
        xp = xp + g1p * rmsnorm(yp, norm_g[l, 1])
        xs = xs + g1s * rmsnorm(ys, norm_g[l, 1])
        hp = rmsnorm(xp, norm_g[l, 2]) * (1.0 + sc2p) + sh2p
        hs = rmsnorm(xs, norm_g[l, 2]) * (1.0 + sc2s) + sh2s
        xp = xp + g2p * rmsnorm(sq_relu_mlp(hp, w_up[l], w_down[l]), norm_g[l, 3])
        xs = xs + g2s * rmsnorm(sq_relu_mlp(hs, w_up[l], w_down[l]), norm_g[l, 3])
    new_mlstm_C = jnp.stack(l_C, axis=1)
    new_mlstm_n = jnp.stack(l_n, axis=1)
    new_mlstm_m = jnp.stack(l_m, axis=1)
    new_ssd = jnp.stack(l_S, axis=1)
    new_gqa_k = jnp.stack(l_k, axis=1)
    new_gqa_v = jnp.stack(l_v, axis=1)
    new_mla_ckv = jnp.stack(l_ckv, axis=1)
    new_mla_kpe = jnp.stack(l_kpe, axis=1)
    return (xp, xs, new_mlstm_C, new_mlstm_n, new_mlstm_m, new_ssd,
            new_gqa_k, new_gqa_v, new_mla_ckv, new_mla_kpe)
```

```python
import numpy as np
import concourse.bass as bass
import concourse.mybir as mybir
from concourse.bass_utils import run_bass_kernel_spmd

F32 = mybir.dt.float32
BF16 = mybir.dt.bfloat16
AF = mybir.ActivationFunctionType
ALU = mybir.AluOpType
AX = mybir.AxisListType

D = 1024
DFF = 4096
DEPTH = 4
EPS = 1e-6
NCORES = 8


class Buf:
    __slots__ = ("name", "w", "r")

    def __init__(self, name):
        self.name = name
        self.w = None
        self.r = []


class Prog:
    NDMA = 6

    def __init__(self, nc, stack):
        self.nc = nc
        self.stack = stack
        self.eng = {"pe": nc.tensor, "act": nc.scalar, "dve": nc.vector,
                    "pool": nc.gpsimd, "sp": nc.sync}
        self.sem = {}
        self.cnt = {}
        for e in self.eng:
            self.sem[e] = stack.enter_context(nc.semaphore("s_" + e))
            self.cnt[e] = 0
        self.dq = {}
        for q in ("sp", "pool", "act"):
            sems = []
            for i in range(self.NDMA):
                k = "d_%s%d" % (q, i)
                self.sem[k] = stack.enter_context(nc.semaphore(k))
                self.cnt[k] = 0
                sems.append(k)
            self.dq[q] = [sems, 0]
        self.waited = {}
        self.pe_pending = []
        self.nbuf = 0
        self.ninstr = 0
        self.npe = 0
        self.marks = []

    def sb(self, name, shape, dt=F32):
        self.nbuf += 1
        name = "%s_s%d" % (name, self.nbuf)
        t = self.stack.enter_context(self.nc.sbuf_tensor(name, list(shape), dt))
        return t

    def ps(self, name, shape, dt=F32):
        t = self.stack.enter_context(self.nc.psum_tensor(name, list(shape), dt))
        return t

    def buf(self, name=None):
        self.nbuf += 1
        return Buf(name or ("b%d" % self.nbuf))

    def _wait(self, e, key, val):
        if val <= 0:
            return
        if key == e and e == "pe":
            return
        k = (e, key)
        if self.waited.get(k, 0) >= val:
            return
        self.waited[k] = val
        self.eng[e].wait_ge(self.sem[key], val)

    def _deps(self, e, reads, writes):
        for b in reads:
            if b.w is not None:
                self._wait(e, b.w[0], b.w[1])
        for b in writes:
            if b.w is not None:
                self._wait(e, b.w[0], b.w[1])
            for (k, v) in b.r:
                self._wait(e, k, v)

    def op(self, e, fn, reads=(), writes=(), inc=True):
        self._deps(e, reads, writes)
        ins = fn(self.eng[e])
        self.ninstr += 1
        if e == "pe":
            self.npe += 1
        if e == "pe" and not inc:
            for b in reads:
                self.pe_pending.append(("r", b))
            for b in writes:
                self.pe_pending.append(("w", b))
            return ins
        self.cnt[e] += 1
        ins.then_inc(self.sem[e], 1)
        me = (e, self.cnt[e])
        if e == "pe" and self.pe_pending:
            for kind, b in self.pe_pending:
                if kind == "r":
                    b.r.append(me)
                else:
                    b.w = me
                    b.r = []
            self.pe_pending = []
        for b in reads:
            b.r.append(me)
            if len(b.r) > 24:
                b.r = b.r[-24:] if False else self._compact(b.r)
        for b in writes:
            b.w = me
            b.r = []
        return ins

    @staticmethod
    def _compact(rl):
        best = {}
        for k, v in rl:
            if best.get(k, 0) < v:
                best[k] = v
        return list(best.items())

    def dma(self, q, out_ap, in_ap, reads=(), writes=(), **kw):
        sems, idx = self.dq[q]
        key = sems[idx % self.NDMA]
        self.dq[q][1] = idx + 1
        self._wait(q, key, self.cnt[key])
        self._deps(q, reads, writes)
        ins = self.eng[q].dma_start(out=out_ap, in_=in_ap, **kw)
        self.ninstr += 1
        self.cnt[key] += 16
        ins.then_inc(self.sem[key], 16)
        me = (key, self.cnt[key])
        for b in reads:
            b.r.append(me)
            if len(b.r) > 24:
                b.r = self._compact(b.r)
        for b in writes:
            b.w = me
            b.r = []
        return ins

    def mark(self, label):
        self.marks.append((label, self.npe))

    def barrier(self):
        for e in ("pe", "act", "dve", "pool", "sp"):
            for key in self.cnt:
                if key == e and e == "pe":
                    continue
                self._wait(e, key, self.cnt[key])

    def phase(self):
        from contextlib import contextmanager, ExitStack

        @contextmanager
        def cm():
            old = self.stack
            with ExitStack() as sub:
                self.stack = sub
                try:
                    yield
                finally:
                    assert not self.pe_pending
                    self.barrier()
                    self.stack = old
        return cm()

    def finish(self):
        for q in ("sp",):
            for key in self.cnt:
                self._wait(q, key, self.cnt[key])


class K:
    pass


def ap3(t, off, dims):
    return bass.AP(t, off, [list(d) for d in dims])


def build_program(opts=None):
    from contextlib import ExitStack
    opts = opts or {}
    nlayers = opts.get("nlayers", DEPTH)
    do_mixer = opts.get("mixer", True)
    nc = bass.Bass("TRN2", target_bir_lowering=False)
    dr = {}

    def din(name, shape, dt=F32):
        dr[name] = nc.dram_tensor(name, list(shape), dt, kind="ExternalInput").ap()
        return dr[name]

    def dout(name, shape, dt=F32):
        dr[name] = nc.dram_tensor(name, list(shape), dt, kind="ExternalOutput").ap()
        return dr[name]

    din("xp", [1024, D]); din("xs", [1024, D]); din("cond2", [2, D])
    din("w_ada", [DEPTH, D, 6 * D]); din("b_ada", [DEPTH, 6 * D]); din("norm_g", [DEPTH * 4, D])
    din("w_up", [DEPTH, D, DFF]); din("w_down", [DEPTH, DFF, D])
    din("ident", [128, 128]); din("maskF", [128, 128]); din("maskB", [128, 128]); din("nmF", [128, 128]); din("nmB", [128, 128])
    din("w_in_even", [2, D, 3616]); din("conv_a_w", [2, 5, 1024]); din("conv_a_b", [2, 1024]); din("conv_b_w", [2, 5, 1024]); din("conv_b_b", [2, 1024])
    din("gate_b", [2, 16]); din("a_norm_w", [2, 512]); din("dt_bias", [2, 16]); din("a_log", [2, 16]); din("d_skip", [2, 8]); din("b_norm_w", [2, 512])
    din("w_out_even", [2, D, D]); din("w_out_odd", [2, D, D])
    din("w_in_odd", [2, D, 1184]); din("sink", [2, 8]); din("q_a_norm", [2, 256]); din("kv_a_norm", [2, 128])
    din("w_q_b", [2, 256, 768]); din("w_kv_b", [2, 128, 1024]); din("esel", [32, 96]); din("rope_tab", [6, 128, 1024]); din("rope_perm", [3, 128, 128])
    din("c_k", [2, 256, 2, 64]); din("c_v", [2, 256, 2, 64]); din("c_ckv", [2, 256, 128]); din("c_kpe", [2, 256, 32])
    dout("o_k", [4, 2, 256, 2, 64]); dout("o_v", [4, 2, 256, 2, 64]); dout("o_ckv", [4, 2, 256, 128]); dout("o_kpe", [4, 2, 256, 32])
    din("st_C", [2, 2, 4, 128, 128]); din("st_n", [2, 2, 4, 128]); din("st_m", [2, 2, 4]); din("st_S", [2, 2, 8, 64, 128])
    dout("o_C", [4, 2, 2, 4, 128, 128]); dout("o_n", [4, 2, 2, 4, 128]); dout("o_m", [4, 2, 2, 4]); dout("o_S", [4, 2, 2, 8, 64, 128])
    dout("yp", [1024, D]); dout("ys", [1024, D])

    with ExitStack() as st:
        P = Prog(nc, st)
        k = K()
        k.P, k.nc, k.dr, k.opts = P, nc, dr, opts
        k.idf = P.sb("idf", [128, 128]); k.b_idf = P.buf("idf")
        k.idb = P.sb("idb", [128, 128], BF16); k.b_idb = P.buf("idb")
        k.onesf = P.sb("onesf", [128, 128]); k.b_ones = P.buf("ones")
        P.dma("sp", k.idf[:], dr["ident"], writes=[k.b_idf])
        P.op("dve", lambda e: e.tensor_copy(out=k.idb[:], in_=k.idf[:]), reads=[k.b_idf], writes=[k.b_idb])
        P.op("dve", lambda e: e.memset(k.onesf[:], 1.0), writes=[k.b_ones])
        k.ps_mm = [(P.ps("psmm%d" % i, [128, 512]), P.buf("psmm%d" % i)) for i in range(4)]
        k.ps_tr = [(P.ps("pstr%d" % i, [128, 1024], BF16), P.buf("pstr%d" % i)) for i in range(2)]
        k.ps_x = [(P.ps("psx%d" % i, [128, 512]), P.buf("psx%d" % i)) for i in range(2)]
        k.rr = {"mm": 0, "tr": 0, "x": 0, "w": 0}

        def nxt(pool):
            lst = {"mm": k.ps_mm, "tr": k.ps_tr, "x": k.ps_x}[pool]
            i = k.rr[pool]
            k.rr[pool] = i + 1
            return lst[i % len(lst)]
        k.nxt = nxt
        k.wring = [(P.sb("wr%d" % i, [128, 4096], BF16), P.buf("wr%d" % i)) for i in range(3)]

        def wnext():
            i = k.rr["w"]
            k.rr["w"] = i + 1
            return k.wring[i % len(k.wring)]
        k.wnext = wnext

        k.x = P.sb("x", [128, 8, D]); k.b_x = [P.buf("x%d" % i) for i in range(8)]
        k.hT = P.sb("hT", [128, 8, 1024], BF16); k.b_hT = [P.buf("hT%d" % i) for i in range(8)]
        k.mixT = P.sb("mixT", [128, 8, 1024], BF16); k.b_mixT = [P.buf("mixT%d" % i) for i in range(8)]

        init_persistent(k)
        P.mark("prologue")
        with P.phase():
            emit_prologue(k)
        for bi, (xin, xout) in enumerate((("xp", "yp"), ("xs", "ys"))):
            if bi in opts.get("skipbatch", ()):
                continue
            for t in range(8):
                P.dma("sp", k.x[:, t, :], dr[xin][t * 128:(t + 1) * 128, :], writes=[k.b_x[t]])
            for l in range(nlayers):
                P.mark("b%d l%d norm1" % (bi, l))
                emit_norm(k, l, bi, 0)
                if do_mixer:
                    P.mark("b%d l%d mixer" % (bi, l))
                    if bi == 0 and l + 1 < nlayers and not opts.get("eager_mod", False):
                        k.ada_todo = [(l + 1, ch) for ch in range(48)]
                    emit_mixer(k, l, bi)
                    ada_tick(k, 48)
                    P.mark("b%d l%d outproj" % (bi, l))
                    emit_outproj_resid(k, l, bi)
                P.mark("b%d l%d norm2" % (bi, l))
                emit_norm(k, l, bi, 1)
                P.mark("b%d l%d mlp" % (bi, l))
                with P.phase():
                    emit_mlp(k, l, bi)
            P.mark("b%d end" % bi)
            for t in range(8):
                P.dma("sp", dr[xout][t * 128:(t + 1) * 128, :], k.x[:, t, :], reads=[k.b_x[t]])
        P.finish()
        k.ninstr = P.ninstr
    return nc, k


def transpose_rows(k, dst_ap_fn, src_sb, rows, nchunk, src_buf, dst_buf):
    P = k.P
    for c0 in range(0, nchunk, 4):
        ps, pb = k.nxt("x")
        n = min(4, nchunk - c0)
        for c in range(n):
            P.op("pe", lambda e: e.transpose(out=ps[:, c * rows:(c + 1) * rows], in_=src_sb[0:rows, (c0 + c) * 128:(c0 + c + 1) * 128],
                                             identity=k.idf[0:rows, 0:rows]),
                 reads=[src_buf, k.b_idf], writes=[pb], inc=(c == n - 1))
        for c in range(n):
            P.op("dve", lambda e: e.tensor_copy(out=dst_ap_fn(c0 + c), in_=ps[:, c * rows:(c + 1) * rows]), reads=[pb], writes=[dst_buf])


def emit_prologue(k):
    P, dr = k.P, k.dr
    stg = P.sb("stg", [16, 8192])
    cond = stg[0:2, 0:1024]; b_cond = P.buf()
    scT = P.sb("scT", [128, 8, 2]); b_scT = P.buf()
    scTb, b_scTb = k.scTb, k.b_scTb
    ng = stg[0:16, 1024:2048]; b_ng = P.buf()
    bada = stg[0:4, 2048:8192]; b_bada = P.buf()
    P.dma("sp", cond, dr["cond2"], writes=[b_cond])
    P.dma("sp", ng, dr["norm_g"], writes=[b_ng])
    P.dma("sp", bada, dr["b_ada"], writes=[b_bada])
    P.op("act", lambda e: e.activation(out=cond, in_=cond, func=AF.Silu), reads=[b_cond], writes=[b_cond])
    transpose_rows(k, lambda c: scT[:, c, :], cond, 2, 8, b_cond, b_scT)
    P.op("dve", lambda e: e.tensor_copy(out=scTb[:], in_=scT[:]), reads=[b_scT], writes=[b_scTb])
    transpose_rows(k, lambda c: k.ngT[:, c, :], ng, 16, 8, b_ng, k.b_ngT)
    transpose_rows(k, lambda c: k.badaT[:, c, :], bada, 4, 48, b_bada, k.b_badaT)
    for ch in range(48):
        mod_block(k, 0, ch)
    if k.opts.get("eager_mod", False):
        for l in range(1, DEPTH):
            for ch in range(48):
                mod_block(k, l, ch)


def mod_block(k, l, ch):
    P, dr = k.P, k.dr
    ps, pb = k.nxt("x")
    i = k.rr.get("ada", 0)
    k.rr["ada"] = i + 1
    wt, wb = k.adaring[i % 2]
    P.dma("pool", wt[:], dr["w_ada"][l, :, ch * 128:(ch + 1) * 128].rearrange("(c p) f -> p c f", p=128), writes=[wb])
    for kc in range(8):
        P.op("pe", lambda e: e.matmul(ps[:, 0:2], lhsT=wt[:, kc, :], rhs=k.scTb[:, kc, :], start=(kc == 0), stop=(kc == 7)),
             reads=[wb, k.b_scTb], writes=[pb], inc=(kc == 7))
    P.op("dve", lambda e: e.tensor_scalar(out=k.mod[:, l, ch, :], in0=ps[:, 0:2], scalar1=k.badaT[:, ch, l:l + 1], scalar2=None, op0=ALU.add),
         reads=[pb, k.b_badaT], writes=[k.b_mod[l]])


def ada_tick(k, n=1):
    for _ in range(n):
        if k.ada_todo:
            l, blk = k.ada_todo.pop(0)
            mod_block(k, l, blk)


def emit_norm(k, l, bi, which):
    P = k.P
    n = k.nrm
    shc, scc = (0, 8) if which == 0 else (24, 32)
    gi = l * 4 + (0 if which == 0 else 2)
    P.op("dve", lambda e: e.scalar_tensor_tensor(out=n["A"][:], in0=k.mod[:, l, scc:scc + 8, bi], scalar=1.0, in1=k.ngT[:, :, gi],
                                                 op0=ALU.add, op1=ALU.mult),
         reads=[k.b_mod[l], k.b_ngT], writes=[n["bA"]])
    bss = n["bss"][0]
    for t in range(8):
        P.op("act", lambda e: e.activation(out=n["junk"][:], in_=k.x[:, t, :], func=AF.Square, accum_out=n["ss"][:, t:t + 1]),
             reads=[k.b_x[t]], writes=[n["bj"], bss])
    P.op("dve", lambda e: e.tensor_scalar(out=n["ss"][:], in0=n["ss"][:], scalar1=1.0 / D, scalar2=EPS, op0=ALU.mult, op1=ALU.add), reads=[bss], writes=[bss])
    P.op("act", lambda e: e.activation(out=n["ss"][:], in_=n["ss"][:], func=AF.Sqrt), reads=[bss], writes=[bss])
    P.op("dve", lambda e: e.reciprocal(out=n["ss"][:], in_=n["ss"][:]), reads=[bss], writes=[bss])
    for t in range(8):
        xn, bxn = n["xn"][t % 2], n["bxn"][t % 2]
        P.op("act", lambda e: e.activation(out=xn[:], in_=k.x[:, t, :], func=AF.Copy, scale=n["ss"][:, t:t + 1]), reads=[k.b_x[t], bss], writes=[bxn])
        ps, pb = k.nxt("tr")
        for c in range(8):
            P.op("pe", lambda e: e.transpose(out=ps[:, c * 128:(c + 1) * 128], in_=xn[:, c * 128:(c + 1) * 128], identity=k.idb[:]),
                 reads=[bxn, k.b_idb], writes=[pb], inc=(c == 7))
        for c in range(8):
            if t % 2 == 0:
                P.op("act", lambda e: e.activation(out=k.hT[:, c, t * 128:(t + 1) * 128], in_=ps[:, c * 128:(c + 1) * 128], func=AF.Identity,
                                                   scale=n["A"][:, c:c + 1], bias=k.mod[:, l, shc + c, bi:bi + 1]),
                     reads=[pb, n["bA"], k.b_mod[l]], writes=[k.b_hT[t]])
            else:
                P.op("dve", lambda e: e.tensor_scalar(out=k.hT[:, c, t * 128:(t + 1) * 128], in0=ps[:, c * 128:(c + 1) * 128],
                                                      scalar1=n["A"][:, c:c + 1], scalar2=k.mod[:, l, shc + c, bi:bi + 1],
                                                      op0=ALU.mult, op1=ALU.add),
                     reads=[pb, n["bA"], k.b_mod[l]], writes=[k.b_hT[t]])


def row_bcast(k, vec_ap_fn, reads, dst, dst_buf):
    P = k.P
    for c in range(8):
        P.op("dve", lambda e: e.tensor_scalar(out=k.rb_diag[:, c * 128:(c + 1) * 128], in0=k.idf[:], scalar1=vec_ap_fn(c), scalar2=None, op0=ALU.mult),
             reads=list(reads) + [k.b_idf], writes=[k.b_rbdiag])
    for h in range(2):
        ps, pb = k.nxt("x")
        P.op("pe", lambda e: e.matmul(ps[:], lhsT=k.onesf[:], rhs=k.rb_diag[:, h * 512:(h + 1) * 512], start=True, stop=True),
             reads=[k.b_ones, k.b_rbdiag], writes=[pb])
        P.op("act", lambda e: acopy(e, out=dst[:, h * 512:(h + 1) * 512], in_=ps[:]), reads=[pb], writes=[dst_buf])


def emit_gate_vec(k, l, bi, which):
    P = k.P
    gc = 16 if which == 0 else 40
    gi = l * 4 + (1 if which == 0 else 3)
    P.op("dve", lambda e: e.tensor_tensor(out=k.ggv[:], in0=k.mod[:, l, gc:gc + 8, bi], in1=k.ngT[:, :, gi], op=ALU.mult),
         reads=[k.b_mod[l], k.b_ngT], writes=[k.b_ggv])
    row_bcast(k, lambda c: k.ggv[:, c:c + 1], [k.b_ggv], k.gg, k.b_gg)


def emit_resid_all(k, ys):
    P = k.P
    r = k.rs
    ss, bss = r["ss2"], r["bss"]
    for t in range(8):
        for h in range(2):
            P.op("act", lambda e: e.activation(out=r["junk"][:], in_=ys[t][0][h], func=AF.Square, accum_out=ss[:, t * 2 + h:t * 2 + h + 1]),
                 reads=[ys[t][1][h]], writes=[r["bj"], bss])
    P.op("dve", lambda e: e.tensor_reduce(out=ss[:, 16:24], in_=ss[:, 0:16].rearrange("p (t h) -> p t h", h=2), axis=AX.X, op=ALU.add), reads=[bss], writes=[bss])
    P.op("dve", lambda e: e.tensor_scalar(out=ss[:, 16:24], in0=ss[:, 16:24], scalar1=1.0 / D, scalar2=EPS, op0=ALU.mult, op1=ALU.add), reads=[bss], writes=[bss])
    P.op("act", lambda e: e.activation(out=ss[:, 16:24], in_=ss[:, 16:24], func=AF.Sqrt), reads=[bss], writes=[bss])
    P.op("dve", lambda e: e.reciprocal(out=ss[:, 24:32], in_=ss[:, 16:24]), reads=[bss], writes=[bss])
    for t in range(8):
        for h in range(2):
            P.op("dve", lambda e: e.scalar_tensor_tensor(out=r["tmp"][:, h * 512:(h + 1) * 512], in0=ys[t][0][h], scalar=ss[:, 24 + t:25 + t],
                                                         in1=k.gg[:, h * 512:(h + 1) * 512], op0=ALU.mult, op1=ALU.mult),
                 reads=[ys[t][1][h], bss, k.b_gg], writes=[r["btmp"]])
        P.op("pool", lambda e: e.tensor_tensor(out=k.x[:, t, :], in0=k.x[:, t, :], in1=r["tmp"][:], op=ALU.add),
             reads=[r["btmp"], k.b_x[t]], writes=[k.b_x[t]])


def emit_mlp(k, l, bi):
    P, dr = k.P, k.dr
    emit_gate_vec(k, l, bi, 1)
    k.hid = [(P.sb("hid%d" % i, [128, 4, 1024], BF16), P.buf("hid%d" % i)) for i in range(2)]
    yacc = P.sb("yacc", [128, 8, D])
    k.b_yacc = [P.buf("yacc%d" % i) for i in range(16)]
    for blk in range(8):
        wu, wub = k.wnext()
        wuv = wu[:].rearrange("p (c f) -> p c f", c=8)
        P.dma("pool", wuv, dr["w_up"][l, :, blk * 512:(blk + 1) * 512].rearrange("(c p) f -> p c f", p=128), writes=[wub])
        wd, wdb = k.wnext()
        wdv = wd[:].rearrange("p (c f) -> p c f", c=4)
        P.dma("pool", wdv, dr["w_down"][l, blk * 512:(blk + 1) * 512, :].rearrange("(c p) f -> p c f", p=128), writes=[wdb])
        hid, hb = k.hid[blk % 2]
        for g in range(2):
            for j in range(4):
                ps, pb = k.nxt("mm")
                for kc in range(8):
                    P.op("pe", lambda e: e.matmul(ps[:], lhsT=wuv[:, kc, j * 128:(j + 1) * 128], rhs=k.hT[:, kc, g * 512:(g + 1) * 512],
                                                  start=(kc == 0), stop=(kc == 7)),
                         reads=[wub] + k.b_hT[g * 4:(g + 1) * 4], writes=[pb], inc=(kc == 7))
                rl, rb = k.relu[(g * 4 + j) % 2]
                P.op("act", lambda e: e.activation(out=rl[:], in_=ps[:], func=AF.Relu), reads=[pb], writes=[rb])
                P.op("dve", lambda e: e.tensor_tensor(out=hid[:, j, g * 512:(g + 1) * 512], in0=rl[:], in1=rl[:], op=ALU.mult), reads=[rb], writes=[hb])
        for t in range(8):
            for h in range(2):
                ps, pb = k.nxt("mm")
                for j in range(4):
                    P.op("pe", lambda e: e.matmul(ps[:], lhsT=hid[:, j, t * 128:(t + 1) * 128], rhs=wdv[:, j, h * 512:(h + 1) * 512],
                                                  start=(j == 0), stop=(j == 3)),
                         reads=[hb, wdb], writes=[pb], inc=(j == 3))
                yb = k.b_yacc[t * 2 + h]
                if blk == 0:
                    P.op("act", lambda e: acopy(e, out=yacc[:, t, h * 512:(h + 1) * 512], in_=ps[:]), reads=[pb], writes=[yb])
                else:
                    P.op("dve", lambda e: e.tensor_tensor(out=yacc[:, t, h * 512:(h + 1) * 512], in0=ps[:], in1=yacc[:, t, h * 512:(h + 1) * 512], op=ALU.add),
                         reads=[pb, yb], writes=[yb])
    emit_resid_all(k, [([yacc[:, t, 0:512], yacc[:, t, 512:1024]], [k.b_yacc[t * 2], k.b_yacc[t * 2 + 1]]) for t in range(8)])


_CACHE = {}


def host_consts():
    i = np.arange(128)
    mF = (i[:, None] <= i[None, :]).astype(np.float32)
    mB = (i[:, None] >= i[None, :]).astype(np.float32)
    T, GW = 1024, 64

    def tabs(rot):
        q = rot // 4
        inv = (10000.0 ** (-np.arange(q, dtype=np.float32) / q)).astype(np.float32)
        r = np.repeat(np.arange(T // GW, dtype=np.float32), GW)
        cl = np.tile(np.arange(GW, dtype=np.float32), T // GW)
        ang = np.concatenate([r[:, None] * inv, cl[:, None] * inv], -1).astype(np.float32)
        return np.cos(ang).astype(np.float32), np.sin(ang).astype(np.float32)
    cc, sc = tabs(64)
    cd, sd = tabs(32)
    rt = np.zeros((6, 128, T), np.float32)
    p = np.arange(128)
    rt[0] = cc[:, p % 32].T
    rt[1] = sc[:, p % 32].T
    rt[2, :64] = 1.0
    rt[2, 64:96] = cd[:, np.arange(32) % 16].T
    rt[3, 64:96] = sd[:, np.arange(32) % 16].T
    rt[4, :32] = cd[:, np.arange(32) % 16].T
    rt[5, :32] = sd[:, np.arange(32) % 16].T
    pm = np.zeros((3, 128, 128), np.float32)
    for m in range(128):
        if m % 64 < 32:
            pm[0, m + 32, m] = -1.0
        else:
            pm[0, m - 32, m] = 1.0
    for m in range(64, 80):
        pm[1, m + 16, m] = -1.0
    for m in range(80, 96):
        pm[1, m - 16, m] = 1.0
    for m in range(16):
        pm[2, m + 16, m] = -1.0
    for m in range(16, 32):
        pm[2, m - 16, m] = 1.0
    es = np.zeros((32, 96), np.float32)
    es[np.arange(32), 64 + np.arange(32)] = 1.0
    return {"ident": np.eye(128, dtype=np.float32), "maskF": mF, "maskB": mB,
            "nmF": (mF - 1.0) * 30000.0, "nmB": (mB - 1.0) * 30000.0,
            "rope_tab": rt, "rope_perm": pm, "esel": es}


def make_in_maps(inp):
    f = lambda a: np.ascontiguousarray(np.asarray(a, dtype=np.float32))
    consts = host_consts()
    in_maps = []
    for c in range(NCORES):
        b = c // 4
        m = dict(consts)
        m["xp"] = f(inp["x_prompt"][4 * c:4 * c + 4]).reshape(1024, D)
        m["xs"] = f(inp["x_sample"][b]).reshape(1024, D)
        m["cond2"] = f(np.stack([np.asarray(inp["c_ctx"]), np.asarray(inp["c"])[b]], 0))
        m["w_ada"] = f(inp["w_ada"]); m["b_ada"] = f(inp["b_ada"])
        m["norm_g"] = f(inp["norm_g"]).reshape(DEPTH * 4, D)
        m["w_up"] = f(inp["w_up"]); m["w_down"] = f(inp["w_down"])
        for nm in ("w_in_even", "conv_a_w", "conv_a_b", "conv_b_w", "conv_b_b", "gate_b", "a_norm_w", "d_skip", "b_norm_w", "w_out_even", "w_out_odd"):
            m[nm] = f(inp[nm])
        m["dt_bias"] = f(inp["dt_bias"]).reshape(2, 16); m["a_log"] = f(inp["a_log"]).reshape(2, 16)
        m["st_C"] = f(inp["state_mlstm_C"][b]); m["st_n"] = f(inp["state_mlstm_n"][b]); m["st_m"] = f(inp["state_mlstm_m"][b])
        m["st_S"] = f(inp["state_ssd"][b])
        for nm in ("w_in_odd", "sink", "q_a_norm", "kv_a_norm", "w_q_b", "w_kv_b"):
            m[nm] = f(inp[nm])
        m["c_k"] = f(inp["cache_gqa_k"][b]); m["c_v"] = f(inp["cache_gqa_v"][b])
        m["c_ckv"] = f(inp["cache_mla_ckv"][b]); m["c_kpe"] = f(inp["cache_mla_kpe"][b])
        in_maps.append(m)
    return in_maps


def kernel(**inp):
    opts = inp.pop("_opts", None)
    key = repr(opts)
    if key not in _CACHE:
        _CACHE[key] = build_program(opts)
    nc, k = _CACHE[key]
    in_maps = make_in_maps(inp)
    if opts and "cores" in opts:
        in_maps = in_maps[:opts["cores"]]
        res = run_bass_kernel_spmd(nc, in_maps, core_ids=list(range(opts["cores"])))
        return [res.results[0][n] for n in ("yp", "ys", "o_C", "o_n", "o_m", "o_S", "o_k", "o_v", "o_ckv", "o_kpe")]
    res = run_bass_kernel_spmd(nc, in_maps, core_ids=list(range(NCORES)))
    R = res.results
    yp = np.concatenate([R[c]["yp"].reshape(4, 256, D) for c in range(NCORES)], 0)
    ys = np.stack([R[0]["ys"], R[4]["ys"]], 0)
    cat = lambda nm: np.concatenate([R[c][nm] for c in range(NCORES)], 0)
    return (yp, ys, cat("o_C"), cat("o_n"), cat("o_m"), cat("o_S"), cat("o_k"), cat("o_v"), cat("o_ckv"), cat("o_kpe"))


def acopy(e, out, in_):
    return e.activation(out=out, in_=in_, func=AF.Copy)


def V(t, p0, npart, off, dims):
    row = 1
    for d in t.shape[1:]:
        row *= d
    return bass.AP(t, p0 * row + off, [[row, npart]] + [list(d) for d in dims])


def init_persistent(k):
    P = k.P
    k.nrm = dict(
        junk=P.sb("njunk", [128, D], BF16), bj=P.buf(),
        ss=P.sb("nss", [128, 8]), bss=[P.buf() for _ in range(8)],
        xn=[P.sb("nxn%d" % i, [128, D], BF16) for i in range(2)], bxn=[P.buf() for _ in range(2)],
        A=P.sb("nA", [128, 8]), bA=P.buf(), )
    k.rs = dict(junk=P.sb("rjunk", [128, 512], BF16), bj=P.buf(), ss2=P.sb("rss", [128, 32]), bss=P.buf(),
                tmp=P.sb("rtmp", [128, D]), btmp=P.buf())
    k.gg = P.sb("gg", [128, D]); k.b_gg = P.buf()
    k.ggv = P.sb("ggv", [128, 8]); k.b_ggv = P.buf()
    k.rb_diag = k.rs["tmp"]; k.b_rbdiag = k.rs["btmp"]
    k.adaring = [(P.sb("adar%d" % i, [128, 8, 128], BF16), P.buf()) for i in range(2)]
    k.relu = [(P.sb("relu%d" % i, [128, 512], BF16), P.buf()) for i in range(2)]
    k.mod = P.sb("mod", [128, DEPTH, 48, 2]); k.b_mod = [P.buf("mod%d" % l) for l in range(DEPTH)]
    k.ngT = P.sb("ngT", [128, 8, 16]); k.b_ngT = P.buf("ngT")
    k.badaT = P.sb("badaT", [128, 48, 4]); k.b_badaT = P.buf("badaT")
    k.stg = P.sb("stg2", [16, 1024]); k.b_stg = P.buf()
    k.scTb = P.sb("scTb", [128, 8, 2], BF16); k.b_scTb = P.buf()
    k.ada_todo = []
    k.onesb = P.sb("onesb", [128, 8], BF16); k.b_onesb = P.buf()
    P.op("dve", lambda e: e.memset(k.onesb[:], 1.0), writes=[k.b_onesb])
    k.maskF = P.sb("maskF", [128, 128]); k.maskB = P.sb("maskB", [128, 128]); k.b_mask = P.buf()
    k.nmF = P.sb("nmF", [128, 128]); k.nmB = P.sb("nmB", [128, 128])
    P.dma("sp", k.maskF[:], k.dr["maskF"], writes=[k.b_mask])
    P.dma("sp", k.maskB[:], k.dr["maskB"], writes=[k.b_mask])
    P.dma("sp", k.nmF[:], k.dr["nmF"], writes=[k.b_mask])
    P.dma("sp", k.nmB[:], k.dr["nmB"], writes=[k.b_mask])


def load_w(k, name, idx, col0, ncols, kch=8):
    P = k.P
    wt, wb = k.wnext()
    wv = wt[:, 0:kch * ncols].rearrange("p (c f) -> p c f", c=kch)
    P.dma("pool", wv, k.dr[name][idx, :, col0:col0 + ncols].rearrange("(c p) f -> p c f", p=128), writes=[wb])
    return wv, wb


def load_rows_fm(k, row_aps, nchunk, dst, dst_buf):
    P = k.P
    for r, a in enumerate(row_aps):
        P.dma("sp", k.stg[r:r + 1, 0:nchunk * 128], a, writes=[k.b_stg])
    transpose_rows(k, lambda c: dst[:, c, :], k.stg, len(row_aps), nchunk, k.b_stg, dst_buf)


def bcast_rows(k, dst, dram_ap_1d_off, tensor, n, dst_buf):
    src = bass.AP(tensor, dram_ap_1d_off, [[0, 128], [1, n]])
    k.P.dma("sp", dst, src, writes=[dst_buf])


def conv_silu(k, pre, b_pre, acc, b_acc, cw, b_cw, c, seqs, out_ap, out_buf, post_scale=None):
    P = k.P
    T = seqs[0][1]
    ns = len(seqs)
    P.op("dve", lambda e: e.tensor_scalar(out=acc[:], in0=pre[:], scalar1=cw[:, c, 2:3], scalar2=cw[:, c, 5:6], op0=ALU.mult, op1=ALU.add),
         reads=[b_pre, b_cw], writes=[b_acc])
    for jj, eng in ((0, "dve"), (1, "dve"), (3, "dve"), (4, "dve")):
        s = jj - 2
        a, b = max(0, -s), T - max(0, s)
        o = V(acc, 0, 128, a, [[T, ns], [1, b - a]])
        i = V(pre, 0, 128, a + s, [[T, ns], [1, b - a]])
        P.op(eng, lambda e: e.scalar_tensor_tensor(out=o, in0=i, scalar=cw[:, c, jj:jj + 1], in1=o, op0=ALU.mult, op1=ALU.add),
             reads=[b_pre, b_cw, b_acc], writes=[b_acc])
    ada_tick(k)
    if post_scale is None:
        P.op("act", lambda e: e.activation(out=out_ap, in_=acc[:], func=AF.Silu), reads=[b_acc], writes=[out_buf])
    else:
        P.op("act", lambda e: e.activation(out=acc[:], in_=acc[:], func=AF.Silu), reads=[b_acc], writes=[b_acc])
        P.op("dve", lambda e: e.tensor_scalar(out=out_ap, in0=acc[:], scalar1=post_scale, scalar2=None, op0=ALU.mult), reads=[b_acc], writes=[out_buf])


def proj_fm_chunk(k, wv, wb, j, evac):
    P = k.P
    for g in range(2):
        ps, pb = k.nxt("mm")
        for kc in range(8):
            P.op("pe", lambda e: e.matmul(ps[:], lhsT=wv[:, kc, j * 128:(j + 1) * 128], rhs=k.hT[:, kc, g * 512:(g + 1) * 512],
                                          start=(kc == 0), stop=(kc == 7)),
                 reads=[wb] + k.b_hT[g * 4:(g + 1) * 4], writes=[pb], inc=(kc == 7))
        evac(g, ps, pb)


def proj_tm(k, wv, wb, c0, ncols, evac):
    P = k.P
    pend = None
    for t in range(8):
        ps, pb = k.nxt("mm")
        for kc in range(8):
            P.op("pe", lambda e: e.matmul(ps[:, 0:ncols], lhsT=k.hT[:, kc, t * 128:(t + 1) * 128], rhs=wv[:, kc, c0:c0 + ncols],
                                          start=(kc == 0), stop=(kc == 7)),
                 reads=[wb, k.b_hT[t]], writes=[pb], inc=(kc == 7))
        if pend is not None:
            pend()
        pend = evac(t, ps, pb)
        if not callable(pend):
            pend = None
        ada_tick(k)
    if pend is not None:
        pend()


def log_sigmoid_inplace(k, x, bx, tmp, btmp, tmp2):
    P = k.P
    P.op("act", lambda e: e.activation(out=tmp, in_=x, func=AF.Abs), reads=[bx], writes=[btmp])
    P.op("act", lambda e: e.activation(out=tmp, in_=tmp, func=AF.Exp, scale=-1.0), reads=[btmp], writes=[btmp])
    P.op("act", lambda e: e.activation(out=tmp, in_=tmp, func=AF.Ln, bias=1.0), reads=[btmp], writes=[btmp])
    P.op("dve", lambda e: e.tensor_scalar_min(out=tmp2, in0=x, scalar1=0.0), reads=[bx], writes=[btmp])
    P.op("dve", lambda e: e.tensor_tensor(out=x, in0=tmp2, in1=tmp, op=ALU.subtract), reads=[btmp], writes=[bx])


def softplus_inplace(k, x, bx, tmp, btmp, tmp2):
    P = k.P
    P.op("act", lambda e: e.activation(out=tmp, in_=x, func=AF.Abs), reads=[bx], writes=[btmp])
    P.op("act", lambda e: e.activation(out=tmp, in_=tmp, func=AF.Exp, scale=-1.0), reads=[btmp], writes=[btmp])
    P.op("act", lambda e: e.activation(out=tmp, in_=tmp, func=AF.Ln, bias=1.0), reads=[btmp], writes=[btmp])
    P.op("dve", lambda e: e.tensor_scalar_max(out=tmp2, in0=x, scalar1=0.0), reads=[bx], writes=[btmp])
    P.op("dve", lambda e: e.tensor_tensor(out=x, in0=tmp2, in1=tmp, op=ALU.add), reads=[btmp], writes=[bx])


def group_norm_to_mixT(k, src, b_src, t, ngroups, gsize, nw_fm, b_nw, chunk0, scr):
    P = k.P
    sq, ss, hn, bs = scr["sq"], scr["ss"], scr["hn"], scr["b"]
    width = ngroups * gsize
    P.op("pool", lambda e: e.tensor_tensor(out=sq[:, 0:width], in0=src, in1=src, op=ALU.mult), reads=[b_src], writes=[bs])
    P.op("dve", lambda e: e.tensor_reduce(out=ss[:, 0:ngroups], in_=sq[:, 0:width].rearrange("p (g f) -> p g f", g=ngroups), axis=AX.X, op=ALU.add),
         reads=[bs], writes=[bs])
    P.op("dve", lambda e: e.tensor_scalar(out=ss[:, 0:ngroups], in0=ss[:, 0:ngroups], scalar1=1.0 / gsize, scalar2=EPS, op0=ALU.mult, op1=ALU.add), reads=[bs], writes=[bs])
    P.op("act", lambda e: e.activation(out=ss[:, 0:ngroups], in_=ss[:, 0:ngroups], func=AF.Sqrt), reads=[bs], writes=[bs])
    P.op("dve", lambda e: e.reciprocal(out=ss[:, 0:ngroups], in_=ss[:, 0:ngroups]), reads=[bs], writes=[bs])
    P.op("dve", lambda e: e.tensor_tensor(out=hn[:, 0:width].rearrange("p (g f) -> p g f", g=ngroups), in0=src.rearrange("p (g f) -> p g f", g=ngroups),
                                          in1=V(ss, 0, 128, 0, [[1, ngroups], [0, gsize]]), op=ALU.mult), reads=[b_src, bs], writes=[bs])
    nch = width // 128

    def tail():
        ps, pb = k.nxt("tr")
        for c in range(nch):
            P.op("pe", lambda e: e.transpose(out=ps[:, c * 128:(c + 1) * 128], in_=hn[:, c * 128:(c + 1) * 128], identity=k.idb[:]),
                 reads=[bs, k.b_idb], writes=[pb], inc=(c == nch - 1))
        for c in range(nch):
            P.op("act", lambda e: e.activation(out=k.mixT[:, chunk0 + c, t * 128:(t + 1) * 128], in_=ps[:, c * 128:(c + 1) * 128], func=AF.Copy, scale=nw_fm[:, c:c + 1]),
                 reads=[pb, b_nw], writes=[k.b_mixT[t]])
    return tail


def emit_even(k, l, bi):
    P, dr = k.P, k.dr
    j = l // 2
    seqs = [(s * 256, 256) for s in range(4)] if bi == 0 else [(0, 1024)]
    skip = k.opts.get("skip", ())
    if "mlstm" in skip:
        for t in range(8):
            P.op("pool", lambda e: e.memset(k.mixT[:, 0:4, t * 128:(t + 1) * 128], 0.0), writes=[k.b_mixT[t]])
    else:
        emit_mlstm(k, l, j, bi, seqs)
    if "ssd" in skip:
        for t in range(8):
            P.op("pool", lambda e: e.memset(k.mixT[:, 4:8, t * 128:(t + 1) * 128], 0.0), writes=[k.b_mixT[t]])
    else:
        emit_ssd(k, l, j, bi, seqs)


def emit_mlstm(k, l, j, bi, seqs):
    P, dr = k.P, k.dr
    with P.phase():
        cw = P.sb("cwA", [128, 8, 6]); b_cw = P.buf()
        load_rows_fm(k, [dr["conv_a_w"][j, r:r + 1, :] for r in range(5)] + [dr["conv_a_b"][j:j + 1, :]], 8, cw, b_cw)
        anw = P.sb("anw", [128, 4, 1]); b_anw = P.buf()
        load_rows_fm(k, [dr["a_norm_w"][j:j + 1, :]], 4, anw, b_anw)
        gb = P.sb("gb", [128, 16]); b_gb = P.buf()
        bcast_rows(k, gb[:], j * 16, dr["gate_b"].tensor, 16, b_gb)
        qkT = P.sb("qkT", [128, 8, 1024], BF16); b_qkT = [P.buf() for _ in range(8)]
        ktm = P.sb("ktm", [128, 8, 512], BF16); b_ktm = [P.buf() for _ in range(8)]
        vtm = P.sb("vtm", [128, 8, 512], BF16); b_vtm = [P.buf() for _ in range(8)]
        gt = P.sb("gates", [128, 8, 16]); b_gt = P.buf()
        bb = P.sb("bb", [128, 8, 8]); b_bb = P.buf()
        aa = P.sb("aa", [128, 8, 8]); b_aa = P.buf()
        gtmp = P.sb("gtmp", [128, 2, 64]); b_gtmp = P.buf()
        with P.phase():
            pre = [(P.sb("pre%d" % i, [128, 1024]), P.buf()) for i in range(2)]
            acc = [(P.sb("acc%d" % i, [128, 1024]), P.buf()) for i in range(2)]
            for blk in range(2):
                wv, wb = load_w(k, "w_in_even", j, blk * 512, 512)
                for jj in range(4):
                    c = blk * 4 + jj
                    pr, bpr = pre[c % 2]
                    ac, bac = acc[c % 2]
                    proj_fm_chunk(k, wv, wb, jj, lambda g, ps, pb: P.op("act", lambda e: acopy(e, out=pr[:, g * 512:(g + 1) * 512], in_=ps[:]), reads=[pb], writes=[bpr]))
                    conv_silu(k, pr, bpr, ac, bac, cw, b_cw, c, seqs, qkT[:, c, :], b_qkT[c], post_scale=(128.0 ** -0.5 if c >= 4 else None))
            for t in range(8):
                ps, pb = k.nxt("tr")
                for h in range(4):
                    P.op("pe", lambda e: e.transpose(out=ps[:, h * 128:(h + 1) * 128], in_=qkT[:, 4 + h, t * 128:(t + 1) * 128], identity=k.idb[:]),
                         reads=[b_qkT[4 + h], k.b_idb], writes=[pb], inc=(h == 3))
                P.op("dve", lambda e: e.tensor_copy(out=ktm[:, t, :], in_=ps[:, 0:512]), reads=[pb], writes=[b_ktm[t]])
            wv, wb = load_w(k, "w_in_even", j, 1024, 512)
            proj_tm(k, wv, wb, 0, 512, lambda t, ps, pb: P.op("act", lambda e: acopy(e, out=vtm[:, t, :], in_=ps[:]), reads=[pb], writes=[b_vtm[t]]))
            wv, wb = load_w(k, "w_in_even", j, 2048, 16)
            proj_tm(k, wv, wb, 0, 16, lambda t, ps, pb: P.op("dve", lambda e: e.tensor_tensor(out=gt[:, t, :], in0=ps[:, 0:16], in1=gb[:], op=ALU.add),
                                                              reads=[pb, b_gb], writes=[b_gt]))
            lfv = gt[:, :, 8:16]
            log_sigmoid_inplace(k, lfv, b_gt, gtmp[:, 0, :].rearrange("p (t f) -> p t f", t=8), b_gtmp, gtmp[:, 1, :].rearrange("p (t f) -> p t f", t=8))
            ps, pb = k.nxt("x")
            for t in range(8):
                P.op("pe", lambda e: e.matmul(ps[:, t * 8:t * 8 + 4], lhsT=k.maskF[:], rhs=gt[:, t, 8:12], start=True, stop=True), reads=[k.b_mask, b_gt], writes=[pb], inc=False)
                P.op("pe", lambda e: e.matmul(ps[:, t * 8 + 4:t * 8 + 8], lhsT=k.maskB[:], rhs=gt[:, t, 12:16], start=True, stop=True), reads=[k.b_mask, b_gt], writes=[pb], inc=(t == 7))
            P.op("dve", lambda e: e.tensor_copy(out=bb[:], in_=ps[:, 0:64].rearrange("p (t f) -> p t f", t=8)), reads=[pb], writes=[b_bb])
            P.op("dve", lambda e: e.tensor_tensor(out=aa[:], in0=gt[:, :, 0:8], in1=bb[:], op=ALU.subtract), reads=[b_gt, b_bb], writes=[b_aa])
        with P.phase():
            hacc = P.sb("hacc", [128, 8, 512]); b_hacc = [P.buf() for _ in range(8)]
            seen = set()
            ch = []
            for d in range(2):
                ch.append(dict(C=P.sb("C%d" % d, [128, 4, 128]), Cb=P.sb("Cb%d" % d, [128, 4, 128], BF16), n=P.sb("n%d" % d, [128, 4]),
                               nb=P.sb("nb%d" % d, [128, 4], BF16), m=P.sb("m%d" % d, [128, 4]), bC=P.buf(), bCb=P.buf(), bn=P.buf(), bm=P.buf(),
                               sm=P.sb("sm%d" % d, [128, 8, 4]), bsm=P.buf(),
                               dg=P.sb("dg%d" % d, [128, 512]), bdg=P.buf(), eam=P.sb("eam%d" % d, [128, 512]), beam=P.buf(),
                               PT=P.sb("PT%d" % d, [128, 4, 128], BF16), bPT=P.buf(), kea=P.sb("kea%d" % d, [128, 4, 128], BF16), bkea=P.buf(),
                               tmp=P.sb("htmp%d" % d, [128, 512]), btmp=P.buf()))
            for si, (t0, T) in enumerate(seqs):
                nt = T // 128
                tb = t0 // 128
                for d in range(2):
                    c = ch[d]
                    if bi == 0:
                        P.op("pool", lambda e: e.memset(c["C"][:], 0.0), writes=[c["bC"]])
                        P.op("pool", lambda e: e.memset(c["n"][:], 0.0), writes=[c["bn"]])
                        P.op("pool", lambda e: e.memset(c["m"][:], 0.0), writes=[c["bm"]])
                    else:
                        P.dma("sp", c["C"][:], dr["st_C"][j, d].rearrange("h d v -> d h v"), writes=[c["bC"]])
                        P.dma("sp", c["n"][:], dr["st_n"][j, d].rearrange("h d -> d h"), writes=[c["bn"]], allow_slow_non_contiguous=True)
                        bcast_rows(k, c["m"][:], (j * 2 + d) * 4, dr["st_m"].tensor, 4, c["bm"])
                for i in range(nt):
                    for d in range(2):
                        t = tb + (i if d == 0 else nt - 1 - i)
                        first = t not in seen
                        seen.add(t)
                        mlstm_step(k, ch[d], d, first, t, qkT, b_qkT, ktm, b_ktm, vtm, b_vtm, gt, b_gt, bb, b_bb, aa, b_aa, hacc, b_hacc)
                if bi == 0:
                    for d in range(2):
                        c = ch[d]
                        P.dma("sp", dr["o_C"][si, j, d].rearrange("h d v -> d h v"), c["C"][:], reads=[c["bC"]])
                        P.dma("sp", dr["o_n"][si, j, d].rearrange("h d -> d h"), c["n"][:], reads=[c["bn"]], allow_slow_non_contiguous=True)
                        P.dma("sp", dr["o_m"][si, j, d:d + 1, :], c["m"][0:1, :], reads=[c["bm"]])
            otms = [(P.sb("otm%d" % i, [128, 512], BF16), P.buf()) for i in range(2)]
            scrs = [dict(sq=P.sb("fsq%d" % i, [128, 512]), ss=P.sb("fss%d" % i, [128, 4]), hn=P.sb("fhn%d" % i, [128, 512], BF16), b=P.buf()) for i in range(2)]
            wv, wb = load_w(k, "w_in_even", j, 1536, 512)

            def fin(t, ps, pb):
                otm, b_otm = otms[t % 2]
                P.op("act", lambda e: e.activation(out=otm[:], in_=ps[:], func=AF.Sigmoid), reads=[pb], writes=[b_otm])
                P.op("dve", lambda e: e.tensor_tensor(out=hacc[:, t, :], in0=hacc[:, t, :], in1=otm[:], op=ALU.mult), reads=[b_otm, b_hacc[t]], writes=[b_hacc[t]])
                return group_norm_to_mixT(k, hacc[:, t, :], b_hacc[t], t, 4, 128, anw[:, :, 0], b_anw, 0, scrs[t % 2])
            proj_tm(k, wv, wb, 0, 512, fin)


def mlstm_step(k, c, d, first, t, qkT, b_qkT, ktm, b_ktm, vtm, b_vtm, gt, b_gt, bb, b_bb, aa, b_aa, hacc, b_hacc):
    P = k.P
    mask = k.maskF if d == 0 else k.maskB
    a_ap = aa[:, t, d * 4:d * 4 + 4]
    b_ap = bb[:, t, d * 4:d * 4 + 4]
    lf_ap = gt[:, t, 8 + d * 4:12 + d * 4]
    sm = c["sm"]
    bsm = c["bsm"]
    tok = slice(t * 128, (t + 1) * 128)
    P.op("dve", lambda e: e.tensor_tensor(out=c["dg"][:].rearrange("p (h s) -> p h s", h=4), in0=V(k.idf, 0, 128, 0, [[0, 4], [1, 128]]),
                                          in1=V(aa, 0, 128, t * 8 + d * 4, [[1, 4], [0, 128]]), op=ALU.mult),
         reads=[k.b_idf, b_aa], writes=[c["bdg"]])
    ps_a, pb_a = k.nxt("mm")
    P.op("pe", lambda e: e.matmul(ps_a[:], lhsT=k.onesf[:], rhs=c["dg"][:], start=True, stop=True), reads=[k.b_ones, c["bdg"]], writes=[pb_a])
    ps_b, pb_b = k.nxt("x")
    P.op("pe", lambda e: e.matmul(ps_b[:, 0:4], lhsT=k.onesf[:], rhs=lf_ap, start=True, stop=True), reads=[k.b_ones, b_gt], writes=[pb_b])
    P.op("dve", lambda e: e.tensor_reduce(out=sm[:, 0, :], in_=ps_a[:].rearrange("p (h s) -> p h s", h=4), axis=AX.X, op=ALU.max), reads=[pb_a], writes=[bsm])
    P.op("dve", lambda e: e.tensor_tensor(out=sm[:, 1, :], in0=sm[:, 0, :], in1=c["m"][:], op=ALU.max), reads=[bsm, c["bm"]], writes=[bsm])
    P.op("dve", lambda e: e.tensor_tensor(out=sm[:, 2, :], in0=c["m"][:], in1=sm[:, 1, :], op=ALU.subtract), reads=[bsm, c["bm"]], writes=[bsm])
    P.op("act", lambda e: e.activation(out=sm[:, 2, :], in_=sm[:, 2, :], func=AF.Exp), reads=[bsm], writes=[bsm])
    P.op("dve", lambda e: e.tensor_tensor(out=c["m"][:], in0=ps_b[:, 0:4], in1=sm[:, 1, :], op=ALU.add), reads=[pb_b, bsm], writes=[c["bm"]])
    P.op("dve", lambda e: e.tensor_tensor(out=sm[:, 3, :], in0=a_ap, in1=sm[:, 1, :], op=ALU.subtract), reads=[b_aa, bsm], writes=[bsm])
    P.op("act", lambda e: e.activation(out=sm[:, 3, :], in_=sm[:, 3, :], func=AF.Exp), reads=[bsm], writes=[bsm])
    P.op("dve", lambda e: e.tensor_tensor(out=sm[:, 4, :], in0=b_ap, in1=sm[:, 1, :], op=ALU.add), reads=[b_bb, bsm], writes=[bsm])
    P.op("act", lambda e: e.activation(out=sm[:, 4, :], in_=sm[:, 4, :], func=AF.Exp, scale=-1.0), reads=[bsm], writes=[bsm])
    P.op("dve", lambda e: e.tensor_tensor(out=c["C"][:], in0=c["C"][:], in1=V(sm, 0, 128, 8, [[1, 4], [0, 128]]), op=ALU.mult), reads=[bsm, c["bC"]], writes=[c["bC"]])
    P.op("dve", lambda e: e.tensor_tensor(out=c["n"][:], in0=c["n"][:], in1=sm[:, 2, :], op=ALU.mult), reads=[bsm, c["bn"]], writes=[c["bn"]])
    P.op("act", lambda e: acopy(e, out=c["Cb"][:], in_=c["C"][:]), reads=[c["bC"]], writes=[c["bCb"]])
    P.op("act", lambda e: acopy(e, out=c["nb"][:], in_=c["n"][:]), reads=[c["bn"]], writes=[c["bCb"]])
    ps_s, pb_s = k.nxt("mm")
    for h in range(4):
        P.op("pe", lambda e: e.matmul(ps_s[:, h * 128:(h + 1) * 128], lhsT=qkT[:, 4 + h, tok], rhs=qkT[:, h, tok], start=True, stop=True),
             reads=[b_qkT[4 + h], b_qkT[h]], writes=[pb_s], inc=(h == 3))
    P.op("pool", lambda e: e.tensor_tensor(out=c["eam"][:].rearrange("p (h s) -> p h s", h=4), in0=V(mask, 0, 128, 0, [[0, 4], [1, 128]]),
                                           in1=V(sm, 0, 128, 12, [[1, 4], [0, 128]]), op=ALU.mult), reads=[k.b_mask, bsm], writes=[c["beam"]])
    P.op("dve", lambda e: e.tensor_tensor(out=c["PT"][:].rearrange("p h s -> p (h s)"), in0=ps_s[:], in1=c["eam"][:], op=ALU.mult), reads=[pb_s, c["beam"]], writes=[c["bPT"]])
    ps_n, pb_n = k.nxt("mm")
    for h in range(4):
        P.op("pe", lambda e: e.matmul(ps_n[:, h * 128:(h + 1) * 128], lhsT=c["PT"][:, h, :], rhs=vtm[:, t, h * 128:(h + 1) * 128], start=True, stop=False),
             reads=[c["bPT"], b_vtm[t]], writes=[pb_n], inc=False)
        P.op("pe", lambda e: e.matmul(ps_n[:, h * 128:(h + 1) * 128], lhsT=qkT[:, h, tok], rhs=c["Cb"][:, h, :], start=False, stop=True),
             reads=[b_qkT[h], c["bCb"]], writes=[pb_n], inc=False)
    for h in range(4):
        P.op("pe", lambda e: e.matmul(ps_b[:, 8 + h:9 + h], lhsT=c["PT"][:, h, :], rhs=k.onesb[:, 0:1], start=True, stop=False),
             reads=[c["bPT"], k.b_onesb], writes=[pb_b], inc=False)
        P.op("pe", lambda e: e.matmul(ps_b[:, 8 + h:9 + h], lhsT=qkT[:, h, tok], rhs=c["nb"][:, h:h + 1], start=False, stop=True),
             reads=[b_qkT[h], c["bCb"]], writes=[pb_b], inc=(h == 3))
    P.op("act", lambda e: e.activation(out=sm[:, 5, :], in_=ps_b[:, 8:12], func=AF.Abs), reads=[pb_b], writes=[bsm])
    P.op("dve", lambda e: e.tensor_tensor(out=sm[:, 5, :], in0=sm[:, 5, :], in1=sm[:, 4, :], op=ALU.max), reads=[bsm], writes=[bsm])
    P.op("dve", lambda e: e.reciprocal(out=sm[:, 6, :], in_=sm[:, 5, :]), reads=[bsm], writes=[bsm])
    rd_bc = V(sm, 0, 128, 24, [[1, 4], [0, 128]])
    if first:
        P.op("dve", lambda e: e.tensor_tensor(out=hacc[:, t, :].rearrange("p (h s) -> p h s", h=4), in0=ps_n[:].rearrange("p (h s) -> p h s", h=4), in1=rd_bc, op=ALU.mult),
             reads=[pb_n, bsm], writes=[b_hacc[t]])
    else:
        P.op("dve", lambda e: e.tensor_tensor(out=c["tmp"][:].rearrange("p (h s) -> p h s", h=4), in0=ps_n[:].rearrange("p (h s) -> p h s", h=4), in1=rd_bc, op=ALU.mult),
             reads=[pb_n, bsm], writes=[c["btmp"]])
        P.op("pool", lambda e: e.tensor_tensor(out=hacc[:, t, :], in0=hacc[:, t, :], in1=c["tmp"][:], op=ALU.add), reads=[c["btmp"], b_hacc[t]], writes=[b_hacc[t]])
    P.op("pool", lambda e: e.tensor_tensor(out=c["kea"][:], in0=ktm[:, t, :].rearrange("p (h s) -> p h s", h=4), in1=V(sm, 0, 128, 12, [[1, 4], [0, 128]]), op=ALU.mult),
         reads=[b_ktm[t], bsm], writes=[c["bkea"]])
    ps_c, pb_c = k.nxt("mm")
    for h in range(4):
        P.op("pe", lambda e: e.matmul(ps_c[:, h * 128:(h + 1) * 128], lhsT=c["kea"][:, h, :], rhs=vtm[:, t, h * 128:(h + 1) * 128], start=True, stop=True),
             reads=[c["bkea"], b_vtm[t]], writes=[pb_c], inc=False)
    for h in range(4):
        P.op("pe", lambda e: e.matmul(ps_b[:, 16 + h:17 + h], lhsT=c["kea"][:, h, :], rhs=k.onesb[:, 0:1], start=True, stop=True),
             reads=[c["bkea"], k.b_onesb], writes=[pb_b, pb_c], inc=(h == 3))
    P.op("dve", lambda e: e.tensor_tensor(out=c["C"][:].rearrange("p h s -> p (h s)"), in0=ps_c[:], in1=c["C"][:].rearrange("p h s -> p (h s)"), op=ALU.add),
         reads=[pb_c, c["bC"]], writes=[c["bC"]])
    P.op("dve", lambda e: e.tensor_tensor(out=c["n"][:], in0=ps_b[:, 16:20], in1=c["n"][:], op=ALU.add), reads=[pb_b, c["bn"]], writes=[c["bn"]])


def emit_ssd(k, l, j, bi, seqs):
    P, dr = k.P, k.dr
    base = 2576
    with P.phase():
        cw = P.sb("cwB", [128, 8, 6]); b_cw = P.buf()
        load_rows_fm(k, [dr["conv_b_w"][j, r:r + 1, :] for r in range(5)] + [dr["conv_b_b"][j:j + 1, :]], 8, cw, b_cw)
        bnw = P.sb("bnw", [128, 4, 1]); b_bnw = P.buf()
        load_rows_fm(k, [dr["b_norm_w"][j:j + 1, :]], 4, bnw, b_bnw)
        dtb = P.sb("dtb", [128, 16]); b_dtb = P.buf()
        bcast_rows(k, dtb[:], j * 16, dr["dt_bias"].tensor, 16, b_dtb)
        Aneg = P.sb("Aneg", [128, 16]); b_A = P.buf()
        bcast_rows(k, Aneg[:], j * 16, dr["a_log"].tensor, 16, b_A)
        P.op("act", lambda e: e.activation(out=Aneg[:], in_=Aneg[:], func=AF.Exp), reads=[b_A], writes=[b_A])
        dsk = P.sb("dsk", [128, 8]); b_dsk = P.buf()
        bcast_rows(k, dsk[:], j * 8, dr["d_skip"].tensor, 8, b_dsk)
        xbcT = P.sb("xbcT", [128, 8, 1024], BF16); b_xbcT = [P.buf() for _ in range(8)]
        xtm = P.sb("xtm", [128, 8, 512], BF16); b_xtm = [P.buf() for _ in range(8)]
        Btm = P.sb("Btm", [128, 8, 256], BF16); b_Btm = [P.buf() for _ in range(8)]
        dt = P.sb("dt", [128, 8, 16]); b_dt = P.buf()
        dtA = P.sb("dtA", [128, 8, 16]); b_dtA = P.buf()
        acum = P.sb("acum", [128, 8, 16]); b_acum = P.buf()
        gtmp = P.sb("gtmp2", [128, 2, 128]); b_gtmp = P.buf()
        stage = k.opts.get("ssd_stage", 99)
        if stage <= 1:
            return
        with P.phase():
            pre = [(P.sb("pre%d" % i, [128, 1024]), P.buf()) for i in range(2)]
            acc = [(P.sb("acc%d" % i, [128, 1024]), P.buf()) for i in range(2)]
            for blk in range(2):
                wv, wb = load_w(k, "w_in_even", j, base + blk * 512, 512)
                for jj in range(4):
                    c = blk * 4 + jj
                    pr, bpr = pre[c % 2]
                    ac, bac = acc[c % 2]
                    proj_fm_chunk(k, wv, wb, jj, lambda g, ps, pb: P.op("act", lambda e: acopy(e, out=pr[:, g * 512:(g + 1) * 512], in_=ps[:]), reads=[pb], writes=[bpr]))
                    conv_silu(k, pr, bpr, ac, bac, cw, b_cw, c, seqs, xbcT[:, c, :], b_xbcT[c])
            ntr = k.opts.get("ssd_tr", 6)
            for t in range(8 if stage > 2 else 0):
                ps, pb = k.nxt("tr")
                for c in range(ntr):
                    P.op("pe", lambda e: e.transpose(out=ps[:, c * 128:(c + 1) * 128], in_=xbcT[:, c, t * 128:(t + 1) * 128], identity=k.idb[:]),
                         reads=[b_xbcT[c], k.b_idb], writes=[pb], inc=(c == ntr - 1))
                P.op("dve", lambda e: e.tensor_copy(out=xtm[:, t, :], in_=ps[:, 0:512]), reads=[pb], writes=[b_xtm[t]])
                bt = k.opts.get("ssd_btm", 2)
                if bt == 1:
                    P.op("act", lambda e: acopy(e, out=Btm[:, t, :], in_=ps[:, 512:768]), reads=[pb], writes=[b_Btm[t]])
                elif bt == 2:
                    P.op("dve", lambda e: e.tensor_copy(out=Btm[:, t, :], in_=ps[:, 512:768]), reads=[pb], writes=[b_Btm[t]])
                elif bt == 4:
                    P.op("act", lambda e: e.activation(out=Btm[:, t, :], in_=ps[:, 512:768], func=AF.Identity, scale=k.onesf[:, 0:1]), reads=[pb, k.b_ones], writes=[b_Btm[t]])
                elif bt == 3:
                    for q in range(2):
                        P.op("act", lambda e: acopy(e, out=Btm[:, t, q * 128:(q + 1) * 128], in_=ps[:, 512 + q * 128:640 + q * 128]), reads=[pb], writes=[b_Btm[t]])
            if stage <= 3:
                return
            wv, wb = load_w(k, "w_in_even", j, 3600, 16)
            proj_tm(k, wv, wb, 0, 16, lambda t, ps, pb: P.op("dve", lambda e: e.tensor_tensor(out=dt[:, t, :], in0=ps[:, 0:16], in1=dtb[:], op=ALU.add),
                                                              reads=[pb, b_dtb], writes=[b_dt]))
            softplus_inplace(k, dt[:], b_dt, gtmp[:, 0, :].rearrange("p (t f) -> p t f", t=8), b_gtmp, gtmp[:, 1, :].rearrange("p (t f) -> p t f", t=8))
            P.op("dve", lambda e: e.scalar_tensor_tensor(out=dtA[:], in0=dt[:], scalar=-1.0, in1=V(Aneg, 0, 128, 0, [[0, 8], [1, 16]]), op0=ALU.mult, op1=ALU.mult),
                 reads=[b_dt, b_A], writes=[b_dtA])
            ps, pb = k.nxt("x")
            for t in range(8):
                P.op("pe", lambda e: e.matmul(ps[:, t * 16:t * 16 + 8], lhsT=k.maskF[:], rhs=dtA[:, t, 0:8], start=True, stop=True), reads=[k.b_mask, b_dtA], writes=[pb], inc=False)
                P.op("pe", lambda e: e.matmul(ps[:, t * 16 + 8:t * 16 + 16], lhsT=k.maskB[:], rhs=dtA[:, t, 8:16], start=True, stop=True), reads=[k.b_mask, b_dtA], writes=[pb], inc=(t == 7))
            P.op("dve", lambda e: e.tensor_copy(out=acum[:], in_=ps[:, 0:128].rearrange("p (t f) -> p t f", t=8)), reads=[pb], writes=[b_acum])
        if stage <= 4:
            return
        with P.phase():
            yss = P.sb("yss", [128, 8, 512]); b_yss = [P.buf() for _ in range(8)]
            seen = set()
            ch = []
            for d in range(2):
                ch.append(dict(S=P.sb("S%d" % d, [128, 8, 64]), Sb=P.sb("Sb%d" % d, [128, 8, 64], BF16), bS=P.buf(), bSb=P.buf(),
                               sm=P.sb("ssm%d" % d, [128, 4, 8]), bsm=P.buf()))
            tp = dict(dgA=P.sb("dgA", [128, 1024]), bdgA=P.buf(), Dm=P.sb("Dm", [128, 1024]), bDm=P.buf(),
                      GT=P.sb("GT", [128, 8, 128], BF16), bGT=P.buf(), xdt=P.sb("xdt", [128, 8, 64], BF16), bxdt=P.buf(),
                      xw=P.sb("xw", [128, 8, 64], BF16), bxw=P.buf(), tmp=P.sb("stmp", [128, 512]), btmp=P.buf())
            for si, (t0, T) in enumerate(seqs):
                nt = T // 128
                tb = t0 // 128
                for d in range(2):
                    c = ch[d]
                    if bi == 0:
                        P.op("pool", lambda e: e.memset(c["S"][:], 0.0), writes=[c["bS"]])
                    else:
                        for h2 in range(4):
                            P.dma("sp", tp["Dm"][:, h2 * 128:(h2 + 1) * 128], dr["st_S"][j, d, 2 * h2:2 * h2 + 2].rearrange("h p n -> (h p) n"), writes=[tp["bDm"]])
                        ps, pb = k.nxt("x")
                        for h2 in range(4):
                            P.op("pe", lambda e: e.transpose(out=ps[:, h2 * 128:(h2 + 1) * 128], in_=tp["Dm"][:, h2 * 128:(h2 + 1) * 128], identity=k.idf[:]),
                                 reads=[tp["bDm"], k.b_idf], writes=[pb], inc=(h2 == 3))
                        P.op("dve", lambda e: e.tensor_copy(out=c["S"][:].rearrange("p h s -> p (h s)"), in_=ps[:]), reads=[pb], writes=[c["bS"]])
                    P.op("act", lambda e: acopy(e, out=c["Sb"][:], in_=c["S"][:]), reads=[c["bS"]], writes=[c["bSb"]])
                for i in range(nt):
                    for d in range(2):
                        t = tb + (i if d == 0 else nt - 1 - i)
                        first = t not in seen
                        seen.add(t)
                        ssd_step(k, ch[d], tp, d, first, t, xbcT, b_xbcT, xtm, b_xtm, Btm, b_Btm, dt, b_dt, dtA, b_dtA, acum, b_acum, yss, b_yss)
                if bi == 0:
                    for d in range(2):
                        c = ch[d]
                        ps, pb = k.nxt("x")
                        for h2 in range(4):
                            P.op("pe", lambda e: e.transpose(out=ps[:, h2 * 128:(h2 + 1) * 128], in_=c["S"][:, 2 * h2:2 * h2 + 2, :].rearrange("p h s -> p (h s)"), identity=k.idf[:]),
                                 reads=[c["bS"], k.b_idf], writes=[pb], inc=(h2 == 3))
                        P.op("dve", lambda e: e.tensor_copy(out=tp["Dm"][:, 0:512], in_=ps[:]), reads=[pb], writes=[tp["bDm"]])
                        for h2 in range(4):
                            P.dma("sp", dr["o_S"][si, j, d, 2 * h2:2 * h2 + 2].rearrange("h p n -> (h p) n"), tp["Dm"][:, h2 * 128:(h2 + 1) * 128], reads=[tp["bDm"]])
            if stage <= 5:
                return
            ztms = [(P.sb("ztm%d" % i, [128, 512], BF16), P.buf()) for i in range(2)]
            scrs = [dict(sq=P.sb("gsq%d" % i, [128, 512]), ss=P.sb("gss%d" % i, [128, 4]), hn=P.sb("ghn%d" % i, [128, 512], BF16), b=P.buf()) for i in range(2)]
            wv, wb = load_w(k, "w_in_even", j, 2064, 512)

            def fin(t, ps, pb):
                ztm, b_ztm = ztms[t % 2]
                scr = scrs[t % 2]
                P.op("act", lambda e: e.activation(out=ztm[:], in_=ps[:], func=AF.Silu), reads=[pb], writes=[b_ztm])
                P.op("dve", lambda e: e.tensor_tensor(out=scr["sq"][:].rearrange("p (h s) -> p h s", h=8), in0=xtm[:, t, :].rearrange("p (h s) -> p h s", h=8),
                                                      in1=V(dsk, 0, 128, 0, [[1, 8], [0, 64]]), op=ALU.mult), reads=[b_xtm[t], b_dsk], writes=[scr["b"]])
                P.op("dve", lambda e: e.tensor_tensor(out=yss[:, t, :], in0=yss[:, t, :], in1=scr["sq"][:], op=ALU.add), reads=[scr["b"], b_yss[t]], writes=[b_yss[t]])
                P.op("dve", lambda e: e.tensor_tensor(out=yss[:, t, :], in0=yss[:, t, :], in1=ztm[:], op=ALU.mult), reads=[b_ztm, b_yss[t]], writes=[b_yss[t]])
                return group_norm_to_mixT(k, yss[:, t, :], b_yss[t], t, 2, 256, bnw[:, :, 0], b_bnw, 4, scr)
            proj_tm(k, wv, wb, 0, 512, fin)


def ssd_step(k, c, tp, d, first, t, xbcT, b_xbcT, xtm, b_xtm, Btm, b_Btm, dt, b_dt, dtA, b_dtA, acum, b_acum, yss, b_yss):
    P = k.P
    nm = k.nmF if d == 0 else k.nmB
    tok = slice(t * 128, (t + 1) * 128)
    sm, bsm = c["sm"], c["bsm"]
    ac_off = t * 16 + d * 8
    P.op("dve", lambda e: e.tensor_tensor(out=tp["dgA"][:].rearrange("p (h s) -> p h s", h=8), in0=V(k.idf, 0, 128, 0, [[0, 8], [1, 128]]),
                                          in1=V(acum, 0, 128, ac_off, [[1, 8], [0, 128]]), op=ALU.mult), reads=[k.b_idf, b_acum], writes=[tp["bdgA"]])
    psD = []
    for hh in range(2):
        ps, pb = k.nxt("mm")
        P.op("pe", lambda e: e.matmul(ps[:], lhsT=k.onesf[:], rhs=tp["dgA"][:, hh * 512:(hh + 1) * 512], start=True, stop=False), reads=[k.b_ones, tp["bdgA"]], writes=[pb], inc=False)
        P.op("pe", lambda e: e.matmul(ps[:].rearrange("p (h s) -> p h s", h=4), lhsT=k.idf[:], rhs=V(nm, 0, 128, 0, [[0, 4], [1, 128]]), start=False, stop=True),
             reads=[k.b_idf, k.b_mask], writes=[pb])
        psD.append((ps, pb))
    for hh in range(2):
        ps, pb = psD[hh]
        P.op("dve", lambda e: e.tensor_tensor(out=tp["Dm"][:, hh * 512:(hh + 1) * 512].rearrange("p (h s) -> p h s", h=4), in0=ps[:].rearrange("p (h s) -> p h s", h=4),
                                              in1=V(acum, 0, 128, ac_off + hh * 4, [[1, 4], [0, 128]]), op=ALU.subtract), reads=[pb, b_acum], writes=[tp["bDm"]])
    P.op("act", lambda e: e.activation(out=tp["Dm"][:], in_=tp["Dm"][:], func=AF.Exp), reads=[tp["bDm"]], writes=[tp["bDm"]])
    sub = k.opts.get("ssd_sub", 99)
    if sub <= 1:
        return
    ps_cb, pb_cb = k.nxt("x")
    for g in range(2):
        P.op("pe", lambda e: e.matmul(ps_cb[:, g * 128:(g + 1) * 128], lhsT=xbcT[:, 4 + g, tok], rhs=xbcT[:, 6 + g, tok], start=True, stop=True),
             reads=[b_xbcT[4 + g], b_xbcT[6 + g]], writes=[pb_cb], inc=False)
    P.op("pe", lambda e: e.matmul(ps_cb[:, 256:264], lhsT=k.onesf[:], rhs=dtA[:, t, d * 8:d * 8 + 8], start=True, stop=True), reads=[k.b_ones, b_dtA], writes=[pb_cb])
    for g in range(2):
        P.op("dve", lambda e: e.tensor_tensor(out=tp["GT"][:, g * 4:(g + 1) * 4, :], in0=tp["Dm"][:, g * 512:(g + 1) * 512].rearrange("p (h s) -> p h s", h=4),
                                              in1=V(ps_cb, 0, 128, g * 128, [[0, 4], [1, 128]]), op=ALU.mult), reads=[tp["bDm"], pb_cb], writes=[tp["bGT"]])
    if sub <= 2:
        return
    P.op("act", lambda e: e.activation(out=sm[:, 0, :], in_=acum[:, t, d * 8:d * 8 + 8], func=AF.Exp), reads=[b_acum], writes=[bsm])
    P.op("dve", lambda e: e.tensor_tensor(out=sm[:, 1, :], in0=ps_cb[:, 256:264], in1=acum[:, t, d * 8:d * 8 + 8], op=ALU.subtract), reads=[pb_cb, b_acum], writes=[bsm])
    P.op("act", lambda e: e.activation(out=sm[:, 1, :], in_=sm[:, 1, :], func=AF.Exp), reads=[bsm], writes=[bsm])
    P.op("dve", lambda e: e.tensor_tensor(out=sm[:, 1, :], in0=sm[:, 1, :], in1=dt[:, t, d * 8:d * 8 + 8], op=ALU.mult), reads=[bsm, b_dt], writes=[bsm])
    P.op("dve", lambda e: e.tensor_copy(out=sm[:, 3, :], in_=ps_cb[:, 256:264]), reads=[pb_cb], writes=[bsm])
    P.op("act", lambda e: e.activation(out=sm[:, 2, :], in_=sm[:, 3, :], func=AF.Exp), reads=[bsm], writes=[bsm])
    xv = xtm[:, t, :].rearrange("p (h s) -> p h s", h=8)
    P.op("pool", lambda e: e.tensor_tensor(out=tp["xdt"][:], in0=xv, in1=V(dt, 0, 128, t * 16 + d * 8, [[1, 8], [0, 64]]), op=ALU.mult), reads=[b_xtm[t], b_dt], writes=[tp["bxdt"]])
    P.op("pool", lambda e: e.tensor_tensor(out=tp["xw"][:], in0=xv, in1=V(sm, 0, 128, 8, [[1, 8], [0, 64]]), op=ALU.mult), reads=[b_xtm[t], bsm], writes=[tp["bxw"]])
    if sub <= 3:
        return
    ps_y, pb_y = k.nxt("mm")
    ps_z, pb_z = k.nxt("mm")
    for hd in range(8):
        P.op("pe", lambda e: e.matmul(ps_y[:, hd * 64:(hd + 1) * 64], lhsT=tp["GT"][:, hd, :], rhs=tp["xdt"][:, hd, :], start=True, stop=True),
             reads=[tp["bGT"], tp["bxdt"]], writes=[pb_y], inc=(hd == 7))
    for hd in range(8):
        P.op("pe", lambda e: e.matmul(ps_z[:, hd * 64:(hd + 1) * 64], lhsT=xbcT[:, 6 + hd // 4, tok], rhs=c["Sb"][:, hd, :], start=True, stop=True),
             reads=[b_xbcT[6 + hd // 4], c["bSb"]], writes=[pb_z], inc=(hd == 7))
    P.op("dve", lambda e: e.tensor_tensor(out=tp["tmp"][:].rearrange("p (h s) -> p h s", h=8), in0=ps_z[:].rearrange("p (h s) -> p h s", h=8),
                                          in1=V(sm, 0, 128, 0, [[1, 8], [0, 64]]), op=ALU.mult), reads=[pb_z, bsm], writes=[tp["btmp"]])
    if first:
        P.op("dve", lambda e: e.tensor_tensor(out=yss[:, t, :], in0=ps_y[:], in1=tp["tmp"][:], op=ALU.add), reads=[pb_y, tp["btmp"]], writes=[b_yss[t]])
    else:
        P.op("dve", lambda e: e.tensor_tensor(out=tp["tmp"][:], in0=ps_y[:], in1=tp["tmp"][:], op=ALU.add), reads=[pb_y, tp["btmp"]], writes=[tp["btmp"]])
        P.op("pool", lambda e: e.tensor_tensor(out=yss[:, t, :], in0=yss[:, t, :], in1=tp["tmp"][:], op=ALU.add), reads=[tp["btmp"], b_yss[t]], writes=[b_yss[t]])
    if sub <= 4:
        return
    ps_s, pb_s = k.nxt("mm")
    for hd in range(8):
        g = hd // 4
        P.op("pe", lambda e: e.matmul(ps_s[:, hd * 64:(hd + 1) * 64], lhsT=Btm[:, t, g * 128:(g + 1) * 128], rhs=tp["xw"][:, hd, :], start=True, stop=True),
             reads=[b_Btm[t], tp["bxw"]], writes=[pb_s], inc=(hd == 7))
    P.op("dve", lambda e: e.tensor_tensor(out=c["S"][:], in0=c["S"][:], in1=V(sm, 0, 128, 16, [[1, 8], [0, 64]]), op=ALU.mult), reads=[bsm, c["bS"]], writes=[c["bS"]])
    P.op("dve", lambda e: e.tensor_tensor(out=c["S"][:].rearrange("p h s -> p (h s)"), in0=ps_s[:], in1=c["S"][:].rearrange("p h s -> p (h s)"), op=ALU.add),
         reads=[pb_s, c["bS"]], writes=[c["bS"]])
    P.op("act", lambda e: acopy(e, out=c["Sb"][:], in_=c["S"][:]), reads=[c["bS"]], writes=[c["bSb"]])


def emit_outproj_resid(k, l, bi):
    P = k.P
    j = l // 2
    name = "w_out_even" if l % 2 == 0 else "w_out_odd"
    emit_gate_vec(k, l, bi, 0)
    with P.phase():
        ybuf = P.sb("ybuf", [128, 8, D]); b_y = [P.buf() for _ in range(16)]
        for h in range(2):
            wv, wb = load_w(k, name, j, h * 512, 512)
            for t in range(8):
                ps, pb = k.nxt("mm")
                for kc in range(8):
                    P.op("pe", lambda e: e.matmul(ps[:], lhsT=k.mixT[:, kc, t * 128:(t + 1) * 128], rhs=wv[:, kc, :], start=(kc == 0), stop=(kc == 7)),
                         reads=[wb, k.b_mixT[t]], writes=[pb], inc=(kc == 7))
                P.op("act", lambda e: acopy(e, out=ybuf[:, t, h * 512:(h + 1) * 512], in_=ps[:]), reads=[pb], writes=[b_y[t * 2 + h]])
        emit_resid_all(k, [([ybuf[:, t, 0:512], ybuf[:, t, 512:1024]], [b_y[t * 2], b_y[t * 2 + 1]]) for t in range(8)])


def emit_mixer(k, l, bi):
    if l % 2 == 0:
        emit_even(k, l, bi)
    else:
        emit_odd(k, l, bi)


def emit_odd(k, l, bi):
    raise NotImplementedError


def attn_unit(k, A, qT, ksegs, vlist, scale, sinkcol, out_ap, out_buf, reads):
    P = k.P
    u = A["rr"]
    A["rr"] = u + 1
    Ssb, bS = A["Ssb"][u % len(A["Ssb"])]
    Pb, bP = A["Pb"][u % len(A["Pb"])]
    PT, bPT = A["PT"][u % len(A["PT"])]
    sm, bsm = A["sm"][u % 2]
    col = 0
    for si, (kT, n, masks) in enumerate(ksegs):
        ps, pb = k.nxt("mm")
        P.op("pe", lambda e: e.matmul(ps[:, 0:n], lhsT=qT, rhs=kT, start=True, stop=(len(masks) == 0)), reads=reads, writes=[pb], inc=(len(masks) == 0))
        for mi, (c0, nm) in enumerate(masks):
            P.op("pe", lambda e: e.matmul(ps[:, c0:c0 + 128], lhsT=k.idb[:], rhs=nm, start=False, stop=(mi == len(masks) - 1)),
                 reads=[k.b_idb, A["b_nm"]], writes=[pb], inc=(mi == len(masks) - 1))
        eng = "act" if si % 2 == 0 else "dve"
        if eng == "act":
            P.op("act", lambda e: acopy(e, out=Ssb[:, col:col + n], in_=ps[:, 0:n]), reads=[pb], writes=[bS])
        else:
            P.op("dve", lambda e: e.tensor_copy(out=Ssb[:, col:col + n], in_=ps[:, 0:n]), reads=[pb], writes=[bS])
        col += n
    N = col
    nblk = N // 128
    assert nblk == len(vlist)
    P.op("dve", lambda e: e.tensor_reduce(out=sm[:, 0:1], in_=Ssb[:, 0:N], axis=AX.X, op=ALU.max), reads=[bS], writes=[bsm])
    if sinkcol is not None:
        P.op("dve", lambda e: e.tensor_scalar(out=sm[:, 1:2], in0=sm[:, 0:1], scalar1=-scale, scalar2=sinkcol, op0=ALU.mult, op1=ALU.min), reads=[bsm, A["b_sink"]], writes=[bsm])
    else:
        P.op("dve", lambda e: e.tensor_scalar(out=sm[:, 1:2], in0=sm[:, 0:1], scalar1=-scale, scalar2=None, op0=ALU.mult), reads=[bsm], writes=[bsm])
    P.op("act", lambda e: e.activation(out=Pb[:, 0:N], in_=Ssb[:, 0:N], func=AF.Exp, scale=scale, bias=sm[:, 1:2], accum_out=sm[:, 2:3]), reads=[bS, bsm], writes=[bP, bsm])
    has_sink = sinkcol is not None
    if has_sink:
        P.op("act", lambda e: e.activation(out=sm[:, 3:4], in_=sinkcol, func=AF.Exp, scale=-1.0, bias=sm[:, 1:2]), reads=[bsm, A["b_sink"]], writes=[bsm])

    def part2():
        _attn_part2(k, A, u, Pb, bP, PT, bPT, sm, bsm, nblk, vlist, reads, out_ap, out_buf, has_sink)
    prev = A.get("pend")
    A["pend"] = part2
    if prev is not None:
        prev()
    ada_tick(k)


def attn_flush(A):
    if A.get("pend") is not None:
        A["pend"]()
        A["pend"] = None


def _attn_part2(k, A, u, Pb, bP, PT, bPT, sm, bsm, nblk, vlist, reads, out_ap, out_buf, has_sink):
    P = k.P
    for b0 in range(0, nblk, 8):
        nb_ = min(8, nblk - b0)
        ps, pb = k.nxt("tr")
        for b in range(nb_):
            P.op("pe", lambda e: e.transpose(out=ps[:, b * 128:(b + 1) * 128], in_=Pb[:, (b0 + b) * 128:(b0 + b + 1) * 128], identity=k.idb[:]),
                 reads=[bP, k.b_idb], writes=[pb], inc=(b == nb_ - 1))
        if False:
            P.op("act", lambda e: acopy(e, out=PT[:, b0 * 128:(b0 + nb_) * 128], in_=ps[:, 0:nb_ * 128]), reads=[pb], writes=[bPT])
        else:
            P.op("dve", lambda e: e.tensor_copy(out=PT[:, b0 * 128:(b0 + nb_) * 128], in_=ps[:, 0:nb_ * 128]), reads=[pb], writes=[bPT])
    pso, pbo = k.nxt("x")
    for b in range(nblk):
        P.op("pe", lambda e: e.matmul(pso[:, 0:64], lhsT=PT[:, b * 128:(b + 1) * 128], rhs=vlist[b], start=(b == 0), stop=(b == nblk - 1)),
             reads=[bPT] + reads, writes=[pbo], inc=(b == nblk - 1))
    if has_sink:
        P.op("dve", lambda e: e.tensor_tensor(out=sm[:, 2:3], in0=sm[:, 2:3], in1=sm[:, 3:4], op=ALU.add), reads=[bsm], writes=[bsm])
    P.op("dve", lambda e: e.reciprocal(out=sm[:, 4:5], in_=sm[:, 2:3]), reads=[bsm], writes=[bsm])
    P.op("dve", lambda e: e.tensor_scalar(out=out_ap, in0=pso[:, 0:64], scalar1=sm[:, 4:5], scalar2=None, op0=ALU.mult), reads=[pbo, bsm], writes=[out_buf])


def rope_fm(k, A, ps, pb, nrows, perm, cosT, sinT, g, dst, dst_buf):
    P = k.P
    u = A["rrr"]
    A["rrr"] = u + 1
    raw, braw = A["raw"][u % 2]
    t1, bt1 = A["t1"][u % len(A["t1"])]
    cs = slice(g * 512, (g + 1) * 512)
    P.op("act", lambda e: acopy(e, out=raw[0:nrows, :], in_=ps[0:nrows, :]), reads=[pb], writes=[braw])
    ps2, pb2 = k.nxt("mm")
    P.op("pe", lambda e: e.matmul(ps2[0:nrows, :], lhsT=perm[0:nrows, 0:nrows], rhs=raw[0:nrows, :], start=True, stop=True), reads=[braw, A["b_rope"]], writes=[pb2])
    P.op("pool", lambda e: e.tensor_tensor(out=t1[0:nrows, :], in0=raw[0:nrows, :], in1=cosT[0:nrows, cs], op=ALU.mult), reads=[braw, A["b_rope"]], writes=[bt1])
    P.op("dve", lambda e: e.tensor_tensor(out=raw[0:nrows, :], in0=ps2[0:nrows, :], in1=sinT[0:nrows, cs], op=ALU.mult), reads=[pb2, A["b_rope"]], writes=[braw])
    P.op("dve", lambda e: e.tensor_tensor(out=dst, in0=t1[0:nrows, :], in1=raw[0:nrows, :], op=ALU.add), reads=[bt1, braw], writes=[dst_buf])


def emit_odd(k, l, bi):
    P, dr = k.P, k.dr
    j = l // 2
    sample = (bi == 1)
    NK = 1280 if sample else 1024
    nkt = NK // 128
    with P.phase():
        A = dict(rr=0, rrr=0)
        sinkneg = P.sb("sinkneg", [128, 8]); A["b_sink"] = P.buf()
        bcast_rows(k, sinkneg[:], j * 8, dr["sink"].tensor, 8, A["b_sink"])
        P.op("dve", lambda e: e.tensor_scalar(out=sinkneg[:], in0=sinkneg[:], scalar1=-1.0, scalar2=None, op0=ALU.mult), reads=[A["b_sink"]], writes=[A["b_sink"]])
        qanw = P.sb("qanw", [128, 2, 1]); b_qanw = P.buf()
        load_rows_fm(k, [dr["q_a_norm"][j:j + 1, :]], 2, qanw, b_qanw)
        kvnw = P.sb("kvnw", [128, 128]); b_kvnw = P.buf()
        bcast_rows(k, kvnw[:], j * 128, dr["kv_a_norm"].tensor, 128, b_kvnw)
        wqb = P.sb("wqb", [128, 2, 768], BF16); b_wqb = P.buf()
        P.dma("pool", wqb[:], dr["w_q_b"][j].rearrange("(c p) f -> p c f", p=128), writes=[b_wqb])
        wkvK = P.sb("wkvK", [128, 8, 96], BF16); b_wkv = P.buf()
        wkvV = P.sb("wkvV", [128, 8, 64], BF16)
        P.op("pool", lambda e: e.memset(wkvK[:], 0.0), writes=[b_wkv])
        wkv3 = dr["w_kv_b"][j].rearrange("p (h f) -> p h f", h=8)
        P.dma("pool", wkvK[:, :, 0:64], wkv3[:, :, 0:64], writes=[b_wkv])
        P.dma("pool", wkvV[:], wkv3[:, :, 64:128], writes=[b_wkv])
        esel = P.sb("esel", [32, 96], BF16)
        nmb = P.sb("nmb", [128, 2, 128], BF16); A["b_nm"] = P.buf()
        P.dma("pool", esel[:], dr["esel"], writes=[b_wkv])
        P.dma("pool", nmb[:, 0, :], dr["nmF"], writes=[A["b_nm"]])
        P.dma("pool", nmb[:, 1, :], dr["nmB"], writes=[A["b_nm"]])
        if sample:
            ropeD = P.sb("ropeD", [96, 2, 1024], BF16); A["b_rope"] = P.buf()
            for i in range(2):
                P.dma("pool", ropeD[:, i, :], dr["rope_tab"][2 + i, 0:96, :], writes=[A["b_rope"]])
            perms = P.sb("perms", [128, 3, 128], BF16)
            for i in range(3):
                P.dma("pool", perms[:, i, :], dr["rope_perm"][i], writes=[A["b_rope"]])
            A["raw"] = [(P.sb("rraw%d" % i, [128, 512], BF16), P.buf()) for i in range(2)]
            A["t1"] = [(P.sb("rt1%d" % i, [128, 512]), P.buf()) for i in range(1)]
        qcT = P.sb("qcT", [128, 4, 1024], BF16); b_qcT = [P.buf() for _ in range(4)]
        kdup = [P.sb("kdup%d" % g, [128, NK], BF16) for g in range(2)]; b_kdup = [P.buf() for _ in range(2)]
        vc = P.sb("vc", [128, nkt, 128], BF16); b_vc = P.buf()
        qanT = P.sb("qanT", [128, 2, 1024], BF16); b_qanT = P.buf()
        ckvT = P.sb("ckvT", [128, NK], BF16); b_ckvT = P.buf()
        kpeT = P.sb("kpeT", [32, NK], BF16); b_kpeT = P.buf()
        oall = P.sb("oall", [128, 8, 1024], BF16); b_oall = [P.buf() for _ in range(8)]
        ssm = P.sb("ossm", [128, 8, 4]); b_ssm = [P.buf() for _ in range(8)]
        from contextlib import ExitStack as _ES
        inner = _ES()
        inner.enter_context(P.phase())
        st = [(P.sb("ost%d" % i, [128, 416]), P.buf()) for i in range(2)]
        stb = [(P.sb("ostb%d" % i, [128, 416], BF16), P.buf()) for i in range(2)]
        if sample:
            rope = P.sb("ropeCK", [128, 6, 1024], BF16)
            for i in (0, 1, 4, 5):
                P.dma("pool", rope[:, i, :], dr["rope_tab"][i], writes=[A["b_rope"]])
        ost = k.opts.get("odd_stage", 99)
        if ost <= 1:
            inner.close(); return
        wv, wb = load_w(k, "w_in_odd", j, 0, 512)
        for c in range(4):
            if sample:
                proj_fm_chunk(k, wv, wb, c, lambda g, ps, pb: rope_fm(k, A, ps, pb, 128, perms[:, 0, :], rope[:, 0, :], rope[:, 1, :], g, qcT[:, c, g * 512:(g + 1) * 512], b_qcT[c]))
            else:
                proj_fm_chunk(k, wv, wb, c, lambda g, ps, pb: P.op("act", lambda e: acopy(e, out=qcT[:, c, g * 512:(g + 1) * 512], in_=ps[:]), reads=[pb], writes=[b_qcT[c]]))
        if ost <= 2:
            inner.close(); return
        wt, wb = k.wnext()
        wdup = wt[:, 0:8 * 256].rearrange("p (c f) -> p c f", c=8)
        for g in range(2):
            for dup in range(2):
                P.dma("pool", wdup[:, :, (g * 2 + dup) * 64:(g * 2 + dup + 1) * 64],
                      dr["w_in_odd"][j, :, 512 + g * 64:512 + (g + 1) * 64].rearrange("(c p) f -> p c f", p=128), writes=[wb])
        for g in range(2):
            if sample:
                proj_fm_chunk(k, wdup, wb, g, lambda gg, ps, pb: rope_fm(k, A, ps, pb, 128, perms[:, 0, :], rope[:, 0, :], rope[:, 1, :], gg, kdup[g][:, gg * 512:(gg + 1) * 512], b_kdup[g]))
            else:
                proj_fm_chunk(k, wdup, wb, g, lambda gg, ps, pb: P.op("act", lambda e: acopy(e, out=kdup[g][:, gg * 512:(gg + 1) * 512], in_=ps[:]), reads=[pb], writes=[b_kdup[g]]))
        if ost <= 3:
            inner.close(); return
        wv, wb = load_w(k, "w_in_odd", j, 512, 256)

        def evac_kv(t, ps, pb):
            s_, bs_ = st[t % 2]
            P.op("dve", lambda e: e.tensor_copy(out=s_[:, 0:256], in_=ps[:, 0:256]), reads=[pb], writes=[bs_])
            P.op("pool", lambda e: e.tensor_copy(out=vc[:, t, :], in_=s_[:, 128:256]), reads=[bs_], writes=[b_vc])
            if not sample:
                rows = slice((t % 2) * 128, (t % 2) * 128 + 128)
                P.dma("sp", dr["o_k"][t // 2, j, rows].rearrange("t g d -> t (g d)"), s_[:, 0:128], reads=[bs_])
                P.dma("sp", dr["o_v"][t // 2, j, rows].rearrange("t g d -> t (g d)"), s_[:, 128:256], reads=[bs_])
        proj_tm(k, wv, wb, 0, 256, evac_kv)
        if ost <= 4:
            inner.close(); return
        wv, wb = load_w(k, "w_in_odd", j, 768, 416)

        def evac_lat(t, ps, pb):
            s_, bs_ = st[t % 2]
            sb_, bsb_ = stb[t % 2]
            sm = ssm[:, t, :]
            tok = slice(t * 128, (t + 1) * 128)
            rows = slice((t % 2) * 128, (t % 2) * 128 + 128)
            P.op("act", lambda e: acopy(e, out=s_[:], in_=ps[:, 0:416]), reads=[pb], writes=[bs_])
            P.op("act", lambda e: e.activation(out=sb_[:, 0:256], in_=s_[:, 0:256], func=AF.Square, accum_out=sm[:, 0:1]), reads=[bs_], writes=[bsb_, b_ssm[t]])
            P.op("act", lambda e: e.activation(out=sb_[:, 256:384], in_=s_[:, 256:384], func=AF.Square, accum_out=sm[:, 1:2]), reads=[bs_], writes=[bsb_, b_ssm[t]])
            P.op("dve", lambda e: e.tensor_scalar(out=sm[:, 0:1], in0=sm[:, 0:1], scalar1=1.0 / 256, scalar2=EPS, op0=ALU.mult, op1=ALU.add), reads=[b_ssm[t]], writes=[b_ssm[t]])
            P.op("dve", lambda e: e.tensor_scalar(out=sm[:, 1:2], in0=sm[:, 1:2], scalar1=1.0 / 128, scalar2=EPS, op0=ALU.mult, op1=ALU.add), reads=[b_ssm[t]], writes=[b_ssm[t]])
            P.op("act", lambda e: e.activation(out=sm[:, 0:2], in_=sm[:, 0:2], func=AF.Sqrt), reads=[b_ssm[t]], writes=[b_ssm[t]])
            P.op("dve", lambda e: e.reciprocal(out=sm[:, 2:4], in_=sm[:, 0:2]), reads=[b_ssm[t]], writes=[b_ssm[t]])
            P.op("dve", lambda e: e.tensor_scalar(out=sb_[:, 0:256], in0=s_[:, 0:256], scalar1=sm[:, 2:3], scalar2=None, op0=ALU.mult), reads=[bs_, b_ssm[t]], writes=[bsb_])
            P.op("dve", lambda e: e.scalar_tensor_tensor(out=s_[:, 256:384], in0=s_[:, 256:384], scalar=sm[:, 3:4], in1=kvnw[:], op0=ALU.mult, op1=ALU.mult),
                 reads=[bs_, b_ssm[t], b_kvnw], writes=[bs_])
            P.op("pool", lambda e: e.tensor_copy(out=sb_[:, 256:416], in_=s_[:, 256:416]), reads=[bs_], writes=[bsb_])
            if not sample:
                P.dma("sp", dr["o_ckv"][t // 2, j, rows], s_[:, 256:384], reads=[bs_])
                P.dma("sp", dr["o_kpe"][t // 2, j, rows], s_[:, 384:416], reads=[bs_])
            def tail():
                ps2, pb2 = k.nxt("tr")
                for c in range(3):
                    P.op("pe", lambda e: e.transpose(out=ps2[:, c * 128:(c + 1) * 128], in_=sb_[:, c * 128:(c + 1) * 128], identity=k.idb[:]), reads=[bsb_, k.b_idb], writes=[pb2], inc=False)
                P.op("pe", lambda e: e.transpose(out=ps2[0:32, 384:512], in_=sb_[:, 384:416], identity=k.idb[:]), reads=[bsb_, k.b_idb], writes=[pb2])
                for c in range(2):
                    P.op("dve", lambda e: e.tensor_scalar(out=qanT[:, c, tok], in0=ps2[:, c * 128:(c + 1) * 128], scalar1=qanw[:, c, 0:1], scalar2=None, op0=ALU.mult), reads=[pb2, b_qanw], writes=[b_qanT])
                P.op("dve", lambda e: e.tensor_copy(out=ckvT[:, tok], in_=ps2[:, 256:384]), reads=[pb2], writes=[b_ckvT])
                P.op("dve", lambda e: e.tensor_copy(out=kpeT[:, tok], in_=ps2[0:32, 384:512]), reads=[pb2], writes=[b_kpeT])
            return tail
        proj_tm(k, wv, wb, 0, 416, evac_lat)
        if sample:
            for g in range(2):
                raw, braw = A["raw"][g]
                t1, bt1 = A["t1"][0]
                cs = slice(g * 512, (g + 1) * 512)
                ps2, pb2 = k.nxt("mm")
                P.op("pe", lambda e: e.matmul(ps2[0:32, :], lhsT=perms[0:32, 2, 0:32], rhs=kpeT[:, cs], start=True, stop=True), reads=[b_kpeT, A["b_rope"]], writes=[pb2])
                P.op("pool", lambda e: e.tensor_tensor(out=t1[0:32, :], in0=kpeT[:, cs], in1=rope[0:32, 4, cs], op=ALU.mult), reads=[b_kpeT, A["b_rope"]], writes=[bt1])
                P.op("dve", lambda e: e.tensor_tensor(out=raw[0:32, :], in0=ps2[0:32, :], in1=rope[0:32, 5, cs], op=ALU.mult), reads=[pb2, A["b_rope"]], writes=[braw])
                P.op("dve", lambda e: e.tensor_tensor(out=kpeT[:, cs], in0=t1[0:32, :], in1=raw[0:32, :], op=ALU.add), reads=[bt1, braw], writes=[b_kpeT])
            s_, bs_ = st[0]
            sb_, bsb_ = stb[0]
            for tt in range(2):
                rows = slice(tt * 128, (tt + 1) * 128)
                for g in range(2):
                    for dup in range(2):
                        P.dma("sp", s_[:, (g * 2 + dup) * 64:(g * 2 + dup + 1) * 64], dr["c_k"][j, rows, g, :], writes=[bs_])
                P.dma("sp", s_[:, 256:384], dr["c_v"][j, rows].rearrange("t g d -> t (g d)"), writes=[bs_])
                P.op("dve", lambda e: e.tensor_copy(out=sb_[:, 0:256], in_=s_[:, 0:256]), reads=[bs_], writes=[bsb_])
                P.op("act", lambda e: acopy(e, out=vc[:, 8 + tt, :], in_=s_[:, 256:384]), reads=[bs_], writes=[b_vc])
                ps2, pb2 = k.nxt("tr")
                for g in range(2):
                    P.op("pe", lambda e: e.transpose(out=ps2[:, g * 128:(g + 1) * 128], in_=sb_[:, g * 128:(g + 1) * 128], identity=k.idb[:]), reads=[bsb_, k.b_idb], writes=[pb2], inc=(g == 1))
                for g in range(2):
                    P.op("dve", lambda e: e.tensor_copy(out=kdup[g][:, 1024 + tt * 128:1024 + (tt + 1) * 128], in_=ps2[:, g * 128:(g + 1) * 128]), reads=[pb2], writes=[b_kdup[g]])
                s2, bs2 = st[1]
                sb2, bsb2 = stb[1]
                P.dma("sp", s2[:, 0:128], dr["c_ckv"][j, rows], writes=[bs2])
                P.dma("sp", s2[:, 128:160], dr["c_kpe"][j, rows], writes=[bs2])
                P.op("dve", lambda e: e.tensor_copy(out=sb2[:, 0:160], in_=s2[:, 0:160]), reads=[bs2], writes=[bsb2])
                ps3, pb3 = k.nxt("tr")
                P.op("pe", lambda e: e.transpose(out=ps3[:, 0:128], in_=sb2[:, 0:128], identity=k.idb[:]), reads=[bsb2, k.b_idb], writes=[pb3], inc=False)
                P.op("pe", lambda e: e.transpose(out=ps3[0:32, 128:256], in_=sb2[:, 128:160], identity=k.idb[:]), reads=[bsb2, k.b_idb], writes=[pb3])
                P.op("dve", lambda e: e.tensor_copy(out=ckvT[:, 1024 + tt * 128:1024 + (tt + 1) * 128], in_=ps3[:, 0:128]), reads=[pb3], writes=[b_ckvT])
                P.op("dve", lambda e: e.tensor_copy(out=kpeT[:, 1024 + tt * 128:1024 + (tt + 1) * 128], in_=ps3[0:32, 128:256]), reads=[pb3], writes=[b_kpeT])
        inner.close()
        if ost <= 6:
            return
        A["Ssb"] = [(P.sb("Ssb%d" % i, [128, NK]), P.buf()) for i in range(2)]
        A["Pb"] = [(P.sb("Pb%d" % i, [128, NK], BF16), P.buf()) for i in range(2)]
        A["PT"] = [(P.sb("PTa%d" % i, [128, NK], BF16), P.buf()) for i in range(2)]
        A["sm"] = [(P.sb("asm%d" % i, [128, 8]), P.buf()) for i in range(2)]
        for qt in range(8):
            tok = slice(qt * 128, (qt + 1) * 128)
            for h in range(8):
                g, c, half = h // 4, h // 2, h % 2
                rows = slice(half * 64, half * 64 + 64)
                if sample:
                    k0, k1 = max(0, qt - 1), min(7, qt + 1)
                    masks = []
                    if qt - 1 >= 0:
                        masks.append((0, nmb[:, 0, :]))
                    if qt + 1 <= 7:
                        masks.append(((k1 - k0) * 128, nmb[:, 1, :]))
                    ksegs = [(kdup[g][rows, k0 * 128:(k1 + 1) * 128], (k1 - k0 + 1) * 128, masks), (kdup[g][rows, 1024:1280], 256, [])]
                    kts = list(range(k0, k1 + 1)) + [8, 9]
                else:
                    s0 = (qt // 2) * 2
                    ksegs = [(kdup[g][rows, s0 * 128:(s0 + 2) * 128], 256, [])]
                    kts = [s0, s0 + 1]
                vlist = [vc[:, kt, g * 64:(g + 1) * 64] for kt in kts]
                attn_unit(k, A, qcT[rows, c, tok], ksegs, vlist, 64.0 ** -0.5, sinkneg[:, h:h + 1], oall[:, qt, h * 64:(h + 1) * 64], b_oall[qt],
                          [b_qcT[c], b_kdup[g], b_vc])
        attn_flush(A)
        if ost <= 7:
            return
        QdT = [(P.sb("QdT%d" % i, [96, 1024], BF16), P.buf()) for i in range(2)]
        KTh = [(P.sb("KTh%d" % i, [96, NK], BF16), P.buf()) for i in range(2)]
        Vh = [(P.sb("Vh%d" % i, [128, nkt, 64], BF16), P.buf()) for i in range(2)]
        for h in range(8):
            qd, bqd = QdT[h % 2]
            kt_, bkt = KTh[h % 2]
            for g in range(2):
                ps, pb = k.nxt("mm")
                for kc in range(2):
                    P.op("pe", lambda e: e.matmul(ps[0:96, :], lhsT=wqb[:, kc, h * 96:(h + 1) * 96], rhs=qanT[:, kc, g * 512:(g + 1) * 512], start=(kc == 0), stop=(kc == 1)),
                         reads=[b_wqb, b_qanT], writes=[pb], inc=(kc == 1))
                if sample:
                    rope_fm(k, A, ps, pb, 96, perms[:, 1, :], ropeD[:, 0, :], ropeD[:, 1, :], g, qd[:, g * 512:(g + 1) * 512], bqd)
                else:
                    P.op("act", lambda e: acopy(e, out=qd[:, g * 512:(g + 1) * 512], in_=ps[0:96, :]), reads=[pb], writes=[bqd])
            for c0 in range(0, NK, 512):
                n = min(512, NK - c0)
                ps, pb = k.nxt("mm")
                P.op("pe", lambda e: e.matmul(ps[0:96, 0:n], lhsT=wkvK[:, h, :], rhs=ckvT[:, c0:c0 + n], start=True, stop=False), reads=[b_wkv, b_ckvT], writes=[pb], inc=False)
                P.op("pe", lambda e: e.matmul(ps[0:96, 0:n], lhsT=esel[:], rhs=kpeT[:, c0:c0 + n], start=False, stop=True), reads=[b_wkv, b_kpeT], writes=[pb])
                P.op("dve", lambda e: e.tensor_copy(out=kt_[:, c0:c0 + n], in_=ps[0:96, 0:n]), reads=[pb], writes=[bkt])
            vh, bvh = Vh[h % 2]
            for kt0 in range(0, nkt, 8):
                nk_ = min(8, nkt - kt0)
                ps, pb = k.nxt("mm")
                for kk in range(nk_):
                    P.op("pe", lambda e: e.matmul(ps[:, kk * 64:(kk + 1) * 64], lhsT=ckvT[:, (kt0 + kk) * 128:(kt0 + kk + 1) * 128], rhs=wkvV[:, h, :], start=True, stop=True),
                         reads=[b_ckvT, b_wkv], writes=[pb], inc=(kk == nk_ - 1))
                P.op("act", lambda e: acopy(e, out=vh[:, kt0:kt0 + nk_, :], in_=ps[:, 0:nk_ * 64].rearrange("p (a f) -> p a f", a=nk_)), reads=[pb], writes=[bvh])
            for qt in range(8):
                tok = slice(qt * 128, (qt + 1) * 128)
                if sample:
                    ksegs = [(kt_[:, 0:512], 512, []), (kt_[:, 512:1024], 512, []), (kt_[:, 1024:1280], 256, [])]
                    kts = list(range(10))
                else:
                    s0 = (qt // 2) * 2
                    ksegs = [(kt_[:, s0 * 128:(s0 + 2) * 128], 256, [])]
                    kts = [s0, s0 + 1]
                vlist = [vh[:, kt, :] for kt in kts]
                attn_unit(k, A, qd[:, tok], ksegs, vlist, 96.0 ** -0.5, None, oall[:, qt, 512 + h * 64:512 + (h + 1) * 64], b_oall[qt], [bqd, bkt, bvh])
        attn_flush(A)
        if ost <= 8:
            return
        for qt in range(8):
            ps, pb = k.nxt("tr")
            for c in range(8):
                P.op("pe", lambda e: e.transpose(out=ps[:, c * 128:(c + 1) * 128], in_=oall[:, qt, c * 128:(c + 1) * 128], identity=k.idb[:]), reads=[b_oall[qt], k.b_idb], writes=[pb], inc=(c == 7))
            P.op("dve", lambda e: e.tensor_copy(out=k.mixT[:, 0:4, qt * 128:(qt + 1) * 128], in_=ps[:, 0:512].rearrange("p (c t) -> p c t", c=4)), reads=[pb], writes=[k.b_mixT[qt]])
            P.op("dve", lambda e: e.tensor_copy(out=k.mixT[:, 4:8, qt * 128:(qt + 1) * 128], in_=ps[:, 512:1024].rearrange("p (c t) -> p c t", c=4)), reads=[pb], writes=[k.b_mixT[qt]])
```

```python
import numpy as np
import concourse.bass as bass
import concourse.mybir as mybir
from concourse.bass_utils import run_bass_kernel_spmd

F32 = mybir.dt.float32
BF16 = mybir.dt.bfloat16
AF = mybir.ActivationFunctionType
ALU = mybir.AluOpType
AX = mybir.AxisListType

D = 1024
DFF = 4096
DEPTH = 4
EPS = 1e-6
NCORES = 8


class Buf:
    __slots__ = ("name", "w", "r")

    def __init__(self, name):
        self.name = name
        self.w = None
        self.r = []


class Prog:
    NDMA = 6

    def __init__(self, nc, stack):
        self.nc = nc
        self.stack = stack
        self.eng = {"pe": nc.tensor, "act": nc.scalar, "dve": nc.vector,
                    "pool": nc.gpsimd, "sp": nc.sync}
        self.sem = {}
        self.cnt = {}
        for e in self.eng:
            self.sem[e] = stack.enter_context(nc.semaphore("s_" + e))
            self.cnt[e] = 0
        self.dq = {}
        for q in ("sp", "pool", "act"):
            sems = []
            for i in range(self.NDMA):
                k = "d_%s%d" % (q, i)
                self.sem[k] = stack.enter_context(nc.semaphore(k))
                self.cnt[k] = 0
                sems.append(k)
            self.dq[q] = [sems, 0]
        self.waited = {}
        self.pe_pending = []
        self.nbuf = 0
        self.ninstr = 0
        self.npe = 0
        self.marks = []

    def sb(self, name, shape, dt=F32):
        self.nbuf += 1
        name = "%s_s%d" % (name, self.nbuf)
        t = self.stack.enter_context(self.nc.sbuf_tensor(name, list(shape), dt))
        return t

    def ps(self, name, shape, dt=F32):
        t = self.stack.enter_context(self.nc.psum_tensor(name, list(shape), dt))
        return t

    def buf(self, name=None):
        self.nbuf += 1
        return Buf(name or ("b%d" % self.nbuf))

    def _wait(self, e, key, val):
        if val <= 0:
            return
        if key == e and e == "pe":
            return
        k = (e, key)
        if self.waited.get(k, 0) >= val:
            return
        self.waited[k] = val
        self.eng[e].wait_ge(self.sem[key], val)

    def _deps(self, e, reads, writes):
        for b in reads:
            if b.w is not None:
                self._wait(e, b.w[0], b.w[1])
        for b in writes:
            if b.w is not None:
                self._wait(e, b.w[0], b.w[1])
            for (k, v) in b.r:
                self._wait(e, k, v)

    def op(self, e, fn, reads=(), writes=(), inc=True):
        self._deps(e, reads, writes)
        ins = fn(self.eng[e])
        self.ninstr += 1
        if e == "pe":
            self.npe += 1
        if e == "pe" and not inc:
            for b in reads:
                self.pe_pending.append(("r", b))
            for b in writes:
                self.pe_pending.append(("w", b))
            return ins
        self.cnt[e] += 1
        ins.then_inc(self.sem[e], 1)
        me = (e, self.cnt[e])
        if e == "pe" and self.pe_pending:
            for kind, b in self.pe_pending:
                if kind == "r":
                    b.r.append(me)
                else:
                    b.w = me
                    b.r = []
            self.pe_pending = []
        for b in reads:
            b.r.append(me)
            if len(b.r) > 24:
                b.r = b.r[-24:] if False else self._compact(b.r)
        for b in writes:
            b.w = me
            b.r = []
        return ins

    @staticmethod
    def _compact(rl):
        best = {}
        for k, v in rl:
            if best.get(k, 0) < v:
                best[k] = v
        return list(best.items())

    def dma(self, q, out_ap, in_ap, reads=(), writes=(), **kw):
        sems, idx = self.dq[q]
        key = sems[idx % self.NDMA]
        self.dq[q][1] = idx + 1
        self._wait(q, key, self.cnt[key])
        self._deps(q, reads, writes)
        ins = self.eng[q].dma_start(out=out_ap, in_=in_ap, **kw)
        self.ninstr += 1
        self.cnt[key] += 16
        ins.then_inc(self.sem[key], 16)
        me = (key, self.cnt[key])
        for b in reads:
            b.r.append(me)
            if len(b.r) > 24:
                b.r = self._compact(b.r)
        for b in writes:
            b.w = me
            b.r = []
        return ins

    def mark(self, label):
        self.marks.append((label, self.npe))

    def barrier(self):
        for e in ("pe", "act", "dve", "pool", "sp"):
            for key in self.cnt:
                if key == e and e == "pe":
                    continue
                self._wait(e, key, self.cnt[key])

    def phase(self):
        from contextlib import contextmanager, ExitStack

        @contextmanager
        def cm():
            old = self.stack
            with ExitStack() as sub:
                self.stack = sub
                try:
                    yield
                finally:
                    assert not self.pe_pending
                    self.barrier()
                    self.stack = old
        return cm()

    def finish(self):
        for q in ("sp",):
            for key in self.cnt:
                self._wait(q, key, self.cnt[key])


class K:
    pass


def ap3(t, off, dims):
    return bass.AP(t, off, [list(d) for d in dims])


def build_program(opts=None):
    from contextlib import ExitStack
    opts = opts or {}
    nlayers = opts.get("nlayers", DEPTH)
    do_mixer = opts.get("mixer", True)
    nc = bass.Bass("TRN2", target_bir_lowering=False)
    dr = {}

    def din(name, shape, dt=F32):
        dr[name] = nc.dram_tensor(name, list(shape), dt, kind="ExternalInput").ap()
        return dr[name]

    def dout(name, shape, dt=F32):
        dr[name] = nc.dram_tensor(name, list(shape), dt, kind="ExternalOutput").ap()
        return dr[name]

    din("xp", [1024, D]); din("xs", [1024, D]); din("cond2", [2, D])
    din("w_ada", [DEPTH, D, 6 * D]); din("b_ada", [DEPTH, 6 * D]); din("norm_g", [DEPTH * 4, D])
    din("w_up", [DEPTH, D, DFF]); din("w_down", [DEPTH, DFF, D])
    din("ident", [128, 128]); din("maskF", [128, 128]); din("maskB", [128, 128]); din("nmF", [128, 128]); din("nmB", [128, 128])
    din("w_in_even", [2, D, 3616]); din("conv_a_w", [2, 5, 1024]); din("conv_a_b", [2, 1024]); din("conv_b_w", [2, 5, 1024]); din("conv_b_b", [2, 1024])
    din("gate_b", [2, 16]); din("a_norm_w", [2, 512]); din("dt_bias", [2, 16]); din("a_log", [2, 16]); din("d_skip", [2, 8]); din("b_norm_w", [2, 512])
    din("w_out_even", [2, D, D]); din("w_out_odd", [2, D, D])
    din("w_in_odd", [2, D, 1184]); din("sink", [2, 8]); din("q_a_norm", [2, 256]); din("kv_a_norm", [2, 128])
    din("w_q_b", [2, 256, 768]); din("w_kv_b", [2, 128, 1024]); din("esel", [32, 96]); din("rope_tab", [6, 128, 1024]); din("rope_perm", [3, 128, 128])
    din("c_k", [2, 256, 2, 64]); din("c_v", [2, 256, 2, 64]); din("c_ckv", [2, 256, 128]); din("c_kpe", [2, 256, 32])
    dout("o_k", [4, 2, 256, 2, 64]); dout("o_v", [4, 2, 256, 2, 64]); dout("o_ckv", [4, 2, 256, 128]); dout("o_kpe", [4, 2, 256, 32])
    din("st_C", [2, 2, 4, 128, 128]); din("st_n", [2, 2, 4, 128]); din("st_m", [2, 2, 4]); din("st_S", [2, 2, 8, 64, 128])
    dout("o_C", [4, 2, 2, 4, 128, 128]); dout("o_n", [4, 2, 2, 4, 128]); dout("o_m", [4, 2, 2, 4]); dout("o_S", [4, 2, 2, 8, 64, 128])
    dout("yp", [1024, D]); dout("ys", [1024, D])

    with ExitStack() as st:
        P = Prog(nc, st)
        k = K()
        k.P, k.nc, k.dr, k.opts = P, nc, dr, opts
        k.idf = P.sb("idf", [128, 128]); k.b_idf = P.buf("idf")
        k.idb = P.sb("idb", [128, 128], BF16); k.b_idb = P.buf("idb")
        k.onesf = P.sb("onesf", [128, 128]); k.b_ones = P.buf("ones")
        P.dma("sp", k.idf[:], dr["ident"], writes=[k.b_idf])
        P.op("dve", lambda e: e.tensor_copy(out=k.idb[:], in_=k.idf[:]), reads=[k.b_idf], writes=[k.b_idb])
        P.op("dve", lambda e: e.memset(k.onesf[:], 1.0), writes=[k.b_ones])
        k.ps_mm = [(P.ps("psmm%d" % i, [128, 512]), P.buf("psmm%d" % i)) for i in range(4)]
        k.ps_tr = [(P.ps("pstr%d" % i, [128, 1024], BF16), P.buf("pstr%d" % i)) for i in range(2)]
        k.ps_x = [(P.ps("psx%d" % i, [128, 512]), P.buf("psx%d" % i)) for i in range(2)]
        k.rr = {"mm": 0, "tr": 0, "x": 0, "w": 0}

        def nxt(pool):
            lst = {"mm": k.ps_mm, "tr": k.ps_tr, "x": k.ps_x}[pool]
            i = k.rr[pool]
            k.rr[pool] = i + 1
            return lst[i % len(lst)]
        k.nxt = nxt
        k.wring = [(P.sb("wr%d" % i, [128, 4096], BF16), P.buf("wr%d" % i)) for i in range(3)]

        def wnext():
            i = k.rr["w"]
            k.rr["w"] = i + 1
            return k.wring[i % len(k.wring)]
        k.wnext = wnext

        k.x = P.sb("x", [128, 8, D]); k.b_x = [P.buf("x%d" % i) for i in range(8)]
        k.hT = P.sb("hT", [128, 8, 1024], BF16); k.b_hT = [P.buf("hT%d" % i) for i in range(8)]
        k.mixT = P.sb("mixT", [128, 8, 1024], BF16); k.b_mixT = [P.buf("mixT%d" % i) for i in range(8)]

        init_persistent(k)
        P.mark("prologue")
        with P.phase():
            emit_prologue(k)
        for bi, (xin, xout) in enumerate((("xp", "yp"), ("xs", "ys"))):
            if bi in opts.get("skipbatch", ()):
                continue
            for t in range(8):
                P.dma("sp", k.x[:, t, :], dr[xin][t * 128:(t + 1) * 128, :], writes=[k.b_x[t]])
            for l in range(nlayers):
                P.mark("b%d l%d norm1" % (bi, l))
                emit_norm(k, l, bi, 0)
                if do_mixer:
                    P.mark("b%d l%d mixer" % (bi, l))
                    emit_mixer(k, l, bi)
                    P.mark("b%d l%d outproj" % (bi, l))
                    emit_outproj_resid(k, l, bi)
                P.mark("b%d l%d norm2" % (bi, l))
                emit_norm(k, l, bi, 1)
                P.mark("b%d l%d mlp" % (bi, l))
                with P.phase():
                    emit_mlp(k, l, bi)
            P.mark("b%d end" % bi)
            for t in range(8):
                P.dma("sp", dr[xout][t * 128:(t + 1) * 128, :], k.x[:, t, :], reads=[k.b_x[t]])
        P.finish()
        k.ninstr = P.ninstr
    return nc, k


def transpose_rows(k, dst_ap_fn, src_sb, rows, nchunk, src_buf, dst_buf):
    P = k.P
    for c0 in range(0, nchunk, 4):
        ps, pb = k.nxt("x")
        n = min(4, nchunk - c0)
        for c in range(n):
            P.op("pe", lambda e: e.transpose(out=ps[:, c * rows:(c + 1) * rows], in_=src_sb[0:rows, (c0 + c) * 128:(c0 + c + 1) * 128],
                                             identity=k.idf[0:rows, 0:rows]),
                 reads=[src_buf, k.b_idf], writes=[pb], inc=(c == n - 1))
        for c in range(n):
            P.op("dve", lambda e: e.tensor_copy(out=dst_ap_fn(c0 + c), in_=ps[:, c * rows:(c + 1) * rows]), reads=[pb], writes=[dst_buf])


def emit_prologue(k):
    P, dr = k.P, k.dr
    stg = P.sb("stg", [16, 8192])
    cond = stg[0:2, 0:1024]; b_cond = P.buf()
    scT = P.sb("scT", [128, 8, 2]); b_scT = P.buf()
    scTb = P.sb("scTb", [128, 8, 2], BF16); b_scTb = P.buf()
    ng = stg[0:16, 1024:2048]; b_ng = P.buf()
    bada = stg[0:4, 2048:8192]; b_bada = P.buf()
    P.dma("sp", cond, dr["cond2"], writes=[b_cond])
    P.dma("sp", ng, dr["norm_g"], writes=[b_ng])
    P.dma("sp", bada, dr["b_ada"], writes=[b_bada])
    P.op("act", lambda e: e.activation(out=cond, in_=cond, func=AF.Silu), reads=[b_cond], writes=[b_cond])
    transpose_rows(k, lambda c: scT[:, c, :], cond, 2, 8, b_cond, b_scT)
    P.op("dve", lambda e: e.tensor_copy(out=scTb[:], in_=scT[:]), reads=[b_scT], writes=[b_scTb])
    transpose_rows(k, lambda c: k.ngT[:, c, :], ng, 16, 8, b_ng, k.b_ngT)
    transpose_rows(k, lambda c: k.badaT[:, c, :], bada, 4, 48, b_bada, k.b_badaT)
    for l in range(DEPTH):
        ps, pb = k.nxt("x")
        for blk in range(12):
            wt, wb = k.wnext()
            wv = wt[:].rearrange("p (c f) -> p c f", c=8)
            P.dma("pool", wv, dr["w_ada"][l, :, blk * 512:(blk + 1) * 512].rearrange("(c p) f -> p c f", p=128), writes=[wb])
            for j in range(4):
                ch = blk * 4 + j
                for kc in range(8):
                    P.op("pe", lambda e: e.matmul(ps[:, ch * 2:ch * 2 + 2], lhsT=wv[:, kc, j * 128:(j + 1) * 128], rhs=scTb[:, kc, :],
                                                  start=(kc == 0), stop=(kc == 7)),
                         reads=[wb, b_scTb], writes=[pb], inc=(kc == 7))
        P.op("dve", lambda e: e.tensor_tensor(out=k.mod[:, l, :, :], in0=ps[:, 0:96].rearrange("p (c t) -> p c t", t=2),
                                              in1=ap3(k.badaT, l, [[48 * 4, 128], [4, 48], [0, 2]]), op=ALU.add),
             reads=[pb, k.b_badaT], writes=[k.b_mod[l]])


def emit_norm(k, l, bi, which):
    P = k.P
    n = k.nrm
    shc, scc = (0, 8) if which == 0 else (24, 32)
    gi = l * 4 + (0 if which == 0 else 2)
    P.op("dve", lambda e: e.scalar_tensor_tensor(out=n["A"][:], in0=k.mod[:, l, scc:scc + 8, bi], scalar=1.0, in1=k.ngT[:, :, gi],
                                                 op0=ALU.add, op1=ALU.mult),
         reads=[k.b_mod[l], k.b_ngT], writes=[n["bA"]])
    bss = n["bss"][0]
    for t in range(8):
        P.op("act", lambda e: e.activation(out=n["junk"][:], in_=k.x[:, t, :], func=AF.Square, accum_out=n["ss"][:, t:t + 1]),
             reads=[k.b_x[t]], writes=[n["bj"], bss])
    P.op("dve", lambda e: e.tensor_scalar(out=n["ss"][:], in0=n["ss"][:], scalar1=1.0 / D, scalar2=EPS, op0=ALU.mult, op1=ALU.add), reads=[bss], writes=[bss])
    P.op("act", lambda e: e.activation(out=n["ss"][:], in_=n["ss"][:], func=AF.Sqrt), reads=[bss], writes=[bss])
    P.op("dve", lambda e: e.reciprocal(out=n["ss"][:], in_=n["ss"][:]), reads=[bss], writes=[bss])
    for t in range(8):
        xn, bxn = n["xn"][t % 2], n["bxn"][t % 2]
        P.op("act", lambda e: e.activation(out=xn[:], in_=k.x[:, t, :], func=AF.Copy, scale=n["ss"][:, t:t + 1]), reads=[k.b_x[t], bss], writes=[bxn])
        ps, pb = k.nxt("tr")
        for c in range(8):
            P.op("pe", lambda e: e.transpose(out=ps[:, c * 128:(c + 1) * 128], in_=xn[:, c * 128:(c + 1) * 128], identity=k.idb[:]),
                 reads=[bxn, k.b_idb], writes=[pb], inc=(c == 7))
        for c in range(8):
            if t % 2 == 0:
                P.op("act", lambda e: e.activation(out=k.hT[:, c, t * 128:(t + 1) * 128], in_=ps[:, c * 128:(c + 1) * 128], func=AF.Identity,
                                                   scale=n["A"][:, c:c + 1], bias=k.mod[:, l, shc + c, bi:bi + 1]),
                     reads=[pb, n["bA"], k.b_mod[l]], writes=[k.b_hT[t]])
            else:
                P.op("dve", lambda e: e.tensor_scalar(out=k.hT[:, c, t * 128:(t + 1) * 128], in0=ps[:, c * 128:(c + 1) * 128],
                                                      scalar1=n["A"][:, c:c + 1], scalar2=k.mod[:, l, shc + c, bi:bi + 1],
                                                      op0=ALU.mult, op1=ALU.add),
                     reads=[pb, n["bA"], k.b_mod[l]], writes=[k.b_hT[t]])


def row_bcast(k, vec_ap_fn, reads, dst, dst_buf):
    P = k.P
    for c in range(8):
        P.op("dve", lambda e: e.tensor_scalar(out=k.rb_diag[:, c * 128:(c + 1) * 128], in0=k.idf[:], scalar1=vec_ap_fn(c), scalar2=None, op0=ALU.mult),
             reads=list(reads) + [k.b_idf], writes=[k.b_rbdiag])
    for h in range(2):
        ps, pb = k.nxt("x")
        P.op("pe", lambda e: e.matmul(ps[:], lhsT=k.onesf[:], rhs=k.rb_diag[:, h * 512:(h + 1) * 512], start=True, stop=True),
             reads=[k.b_ones, k.b_rbdiag], writes=[pb])
        P.op("act", lambda e: acopy(e, out=dst[:, h * 512:(h + 1) * 512], in_=ps[:]), reads=[pb], writes=[dst_buf])


def emit_gate_vec(k, l, bi, which):
    P = k.P
    gc = 16 if which == 0 else 40
    gi = l * 4 + (1 if which == 0 else 3)
    P.op("dve", lambda e: e.tensor_tensor(out=k.ggv[:], in0=k.mod[:, l, gc:gc + 8, bi], in1=k.ngT[:, :, gi], op=ALU.mult),
         reads=[k.b_mod[l], k.b_ngT], writes=[k.b_ggv])
    row_bcast(k, lambda c: k.ggv[:, c:c + 1], [k.b_ggv], k.gg, k.b_gg)


def emit_resid_all(k, ys):
    P = k.P
    r = k.rs
    ss, bss = r["ss2"], r["bss"]
    for t in range(8):
        for h in range(2):
            P.op("act", lambda e: e.activation(out=r["junk"][:], in_=ys[t][0][h], func=AF.Square, accum_out=ss[:, t * 2 + h:t * 2 + h + 1]),
                 reads=[ys[t][1][h]], writes=[r["bj"], bss])
    P.op("dve", lambda e: e.tensor_reduce(out=ss[:, 16:24], in_=ss[:, 0:16].rearrange("p (t h) -> p t h", h=2), axis=AX.X, op=ALU.add), reads=[bss], writes=[bss])
    P.op("dve", lambda e: e.tensor_scalar(out=ss[:, 16:24], in0=ss[:, 16:24], scalar1=1.0 / D, scalar2=EPS, op0=ALU.mult, op1=ALU.add), reads=[bss], writes=[bss])
    P.op("act", lambda e: e.activation(out=ss[:, 16:24], in_=ss[:, 16:24], func=AF.Sqrt), reads=[bss], writes=[bss])
    P.op("dve", lambda e: e.reciprocal(out=ss[:, 24:32], in_=ss[:, 16:24]), reads=[bss], writes=[bss])
    for t in range(8):
        for h in range(2):
            P.op("dve", lambda e: e.scalar_tensor_tensor(out=r["tmp"][:, h * 512:(h + 1) * 512], in0=ys[t][0][h], scalar=ss[:, 24 + t:25 + t],
                                                         in1=k.gg[:, h * 512:(h + 1) * 512], op0=ALU.mult, op1=ALU.mult),
                 reads=[ys[t][1][h], bss, k.b_gg], writes=[r["btmp"]])
        P.op("pool", lambda e: e.tensor_tensor(out=k.x[:, t, :], in0=k.x[:, t, :], in1=r["tmp"][:], op=ALU.add),
             reads=[r["btmp"], k.b_x[t]], writes=[k.b_x[t]])


def emit_mlp(k, l, bi):
    P, dr = k.P, k.dr
    emit_gate_vec(k, l, bi, 1)
    k.hid = [(P.sb("hid%d" % i, [128, 4, 1024], BF16), P.buf("hid%d" % i)) for i in range(2)]
    yacc = P.sb("yacc", [128, 8, D])
    k.b_yacc = [P.buf("yacc%d" % i) for i in range(16)]
    for blk in range(8):
        wu, wub = k.wnext()
        wuv = wu[:].rearrange("p (c f) -> p c f", c=8)
        P.dma("pool", wuv, dr["w_up"][l, :, blk * 512:(blk + 1) * 512].rearrange("(c p) f -> p c f", p=128), writes=[wub])
        wd, wdb = k.wnext()
        wdv = wd[:].rearrange("p (c f) -> p c f", c=4)
        P.dma("pool", wdv, dr["w_down"][l, blk * 512:(blk + 1) * 512, :].rearrange("(c p) f -> p c f", p=128), writes=[wdb])
        hid, hb = k.hid[blk % 2]
        for g in range(2):
            for j in range(4):
                ps, pb = k.nxt("mm")
                for kc in range(8):
                    P.op("pe", lambda e: e.matmul(ps[:], lhsT=wuv[:, kc, j * 128:(j + 1) * 128], rhs=k.hT[:, kc, g * 512:(g + 1) * 512],
                                                  start=(kc == 0), stop=(kc == 7)),
                         reads=[wub] + k.b_hT[g * 4:(g + 1) * 4], writes=[pb], inc=(kc == 7))
                rl, rb = k.relu[(g * 4 + j) % 2]
                P.op("act", lambda e: e.activation(out=rl[:], in_=ps[:], func=AF.Relu), reads=[pb], writes=[rb])
                P.op("dve", lambda e: e.tensor_tensor(out=hid[:, j, g * 512:(g + 1) * 512], in0=rl[:], in1=rl[:], op=ALU.mult), reads=[rb], writes=[hb])
        for t in range(8):
            for h in range(2):
                ps, pb = k.nxt("mm")
                for j in range(4):
                    P.op("pe", lambda e: e.matmul(ps[:], lhsT=hid[:, j, t * 128:(t + 1) * 128], rhs=wdv[:, j, h * 512:(h + 1) * 512],
                                                  start=(j == 0), stop=(j == 3)),
                         reads=[hb, wdb], writes=[pb], inc=(j == 3))
                yb = k.b_yacc[t * 2 + h]
                if blk == 0:
                    P.op("act", lambda e: acopy(e, out=yacc[:, t, h * 512:(h + 1) * 512], in_=ps[:]), reads=[pb], writes=[yb])
                else:
                    P.op("dve", lambda e: e.tensor_tensor(out=yacc[:, t, h * 512:(h + 1) * 512], in0=ps[:], in1=yacc[:, t, h * 512:(h + 1) * 512], op=ALU.add),
                         reads=[pb, yb], writes=[yb])
    emit_resid_all(k, [([yacc[:, t, 0:512], yacc[:, t, 512:1024]], [k.b_yacc[t * 2], k.b_yacc[t * 2 + 1]]) for t in range(8)])


_CACHE = {}


def host_consts():
    i = np.arange(128)
    mF = (i[:, None] <= i[None, :]).astype(np.float32)
    mB = (i[:, None] >= i[None, :]).astype(np.float32)
    T, GW = 1024, 64

    def tabs(rot):
        q = rot // 4
        inv = (10000.0 ** (-np.arange(q, dtype=np.float32) / q)).astype(np.float32)
        r = np.repeat(np.arange(T // GW, dtype=np.float32), GW)
        cl = np.tile(np.arange(GW, dtype=np.float32), T // GW)
        ang = np.concatenate([r[:, None] * inv, cl[:, None] * inv], -1).astype(np.float32)
        return np.cos(ang).astype(np.float32), np.sin(ang).astype(np.float32)
    cc, sc = tabs(64)
    cd, sd = tabs(32)
    rt = np.zeros((6, 128, T), np.float32)
    p = np.arange(128)
    rt[0] = cc[:, p % 32].T
    rt[1] = sc[:, p % 32].T
    rt[2, :64] = 1.0
    rt[2, 64:96] = cd[:, np.arange(32) % 16].T
    rt[3, 64:96] = sd[:, np.arange(32) % 16].T
    rt[4, :32] = cd[:, np.arange(32) % 16].T
    rt[5, :32] = sd[:, np.arange(32) % 16].T
    pm = np.zeros((3, 128, 128), np.float32)
    for m in range(128):
        if m % 64 < 32:
            pm[0, m + 32, m] = -1.0
        else:
            pm[0, m - 32, m] = 1.0
    for m in range(64, 80):
        pm[1, m + 16, m] = -1.0
    for m in range(80, 96):
        pm[1, m - 16, m] = 1.0
    for m in range(16):
        pm[2, m + 16, m] = -1.0
    for m in range(16, 32):
        pm[2, m - 16, m] = 1.0
    es = np.zeros((32, 96), np.float32)
    es[np.arange(32), 64 + np.arange(32)] = 1.0
    return {"ident": np.eye(128, dtype=np.float32), "maskF": mF, "maskB": mB,
            "nmF": (mF - 1.0) * 30000.0, "nmB": (mB - 1.0) * 30000.0,
            "rope_tab": rt, "rope_perm": pm, "esel": es}


def make_in_maps(inp):
    f = lambda a: np.ascontiguousarray(np.asarray(a, dtype=np.float32))
    consts = host_consts()
    in_maps = []
    for c in range(NCORES):
        b = c // 4
        m = dict(consts)
        m["xp"] = f(inp["x_prompt"][4 * c:4 * c + 4]).reshape(1024, D)
        m["xs"] = f(inp["x_sample"][b]).reshape(1024, D)
        m["cond2"] = f(np.stack([np.asarray(inp["c_ctx"]), np.asarray(inp["c"])[b]], 0))
        m["w_ada"] = f(inp["w_ada"]); m["b_ada"] = f(inp["b_ada"])
        m["norm_g"] = f(inp["norm_g"]).reshape(DEPTH * 4, D)
        m["w_up"] = f(inp["w_up"]); m["w_down"] = f(inp["w_down"])
        for nm in ("w_in_even", "conv_a_w", "conv_a_b", "conv_b_w", "conv_b_b", "gate_b", "a_norm_w", "d_skip", "b_norm_w", "w_out_even", "w_out_odd"):
            m[nm] = f(inp[nm])
        m["dt_bias"] = f(inp["dt_bias"]).reshape(2, 16); m["a_log"] = f(inp["a_log"]).reshape(2, 16)
        m["st_C"] = f(inp["state_mlstm_C"][b]); m["st_n"] = f(inp["state_mlstm_n"][b]); m["st_m"] = f(inp["state_mlstm_m"][b])
        m["st_S"] = f(inp["state_ssd"][b])
        for nm in ("w_in_odd", "sink", "q_a_norm", "kv_a_norm", "w_q_b", "w_kv_b"):
            m[nm] = f(inp[nm])
        m["c_k"] = f(inp["cache_gqa_k"][b]); m["c_v"] = f(inp["cache_gqa_v"][b])
        m["c_ckv"] = f(inp["cache_mla_ckv"][b]); m["c_kpe"] = f(inp["cache_mla_kpe"][b])
        in_maps.append(m)
    return in_maps


def kernel(**inp):
    opts = inp.pop("_opts", None)
    key = repr(opts)
    if key not in _CACHE:
        _CACHE[key] = build_program(opts)
    nc, k = _CACHE[key]
    in_maps = make_in_maps(inp)
    if opts and "cores" in opts:
        in_maps = in_maps[:opts["cores"]]
        res = run_bass_kernel_spmd(nc, in_maps, core_ids=list(range(opts["cores"])))
        return [res.results[0][n] for n in ("yp", "ys", "o_C", "o_n", "o_m", "o_S", "o_k", "o_v", "o_ckv", "o_kpe")]
    res = run_bass_kernel_spmd(nc, in_maps, core_ids=list(range(NCORES)))
    R = res.results
    yp = np.concatenate([R[c]["yp"].reshape(4, 256, D) for c in range(NCORES)], 0)
    ys = np.stack([R[0]["ys"], R[4]["ys"]], 0)
    cat = lambda nm: np.concatenate([R[c][nm] for c in range(NCORES)], 0)
    return (yp, ys, cat("o_C"), cat("o_n"), cat("o_m"), cat("o_S"), cat("o_k"), cat("o_v"), cat("o_ckv"), cat("o_kpe"))


def acopy(e, out, in_):
    return e.activation(out=out, in_=in_, func=AF.Copy)


def V(t, p0, npart, off, dims):
    row = 1
    for d in t.shape[1:]:
        row *= d
    return bass.AP(t, p0 * row + off, [[row, npart]] + [list(d) for d in dims])


def init_persistent(k):
    P = k.P
    k.nrm = dict(
        junk=P.sb("njunk", [128, D], BF16), bj=P.buf(),
        ss=P.sb("nss", [128, 8]), bss=[P.buf() for _ in range(8)],
        xn=[P.sb("nxn%d" % i, [128, D], BF16) for i in range(2)], bxn=[P.buf() for _ in range(2)],
        A=P.sb("nA", [128, 8]), bA=P.buf(), )
    k.rs = dict(junk=P.sb("rjunk", [128, 512], BF16), bj=P.buf(), ss2=P.sb("rss", [128, 32]), bss=P.buf(),
                tmp=P.sb("rtmp", [128, D]), btmp=P.buf())
    k.gg = P.sb("gg", [128, D]); k.b_gg = P.buf()
    k.ggv = P.sb("ggv", [128, 8]); k.b_ggv = P.buf()
    k.rb_diag = P.sb("rbdiag", [128, D]); k.b_rbdiag = P.buf()
    k.relu = [(P.sb("relu%d" % i, [128, 512], BF16), P.buf()) for i in range(2)]
    k.mod = P.sb("mod", [128, DEPTH, 48, 2]); k.b_mod = [P.buf("mod%d" % l) for l in range(DEPTH)]
    k.ngT = P.sb("ngT", [128, 8, 16]); k.b_ngT = P.buf("ngT")
    k.badaT = P.sb("badaT", [128, 48, 4]); k.b_badaT = P.buf("badaT")
    k.stg = P.sb("stg2", [16, 1024]); k.b_stg = P.buf()
    k.onesb = P.sb("onesb", [128, 8], BF16); k.b_onesb = P.buf()
    P.op("dve", lambda e: e.memset(k.onesb[:], 1.0), writes=[k.b_onesb])
    k.maskF = P.sb("maskF", [128, 128]); k.maskB = P.sb("maskB", [128, 128]); k.b_mask = P.buf()
    k.nmF = P.sb("nmF", [128, 128]); k.nmB = P.sb("nmB", [128, 128])
    P.dma("sp", k.maskF[:], k.dr["maskF"], writes=[k.b_mask])
    P.dma("sp", k.maskB[:], k.dr["maskB"], writes=[k.b_mask])
    P.dma("sp", k.nmF[:], k.dr["nmF"], writes=[k.b_mask])
    P.dma("sp", k.nmB[:], k.dr["nmB"], writes=[k.b_mask])


def load_w(k, name, idx, col0, ncols, kch=8):
    P = k.P
    wt, wb = k.wnext()
    wv = wt[:, 0:kch * ncols].rearrange("p (c f) -> p c f", c=kch)
    P.dma("pool", wv, k.dr[name][idx, :, col0:col0 + ncols].rearrange("(c p) f -> p c f", p=128), writes=[wb])
    return wv, wb


def load_rows_fm(k, row_aps, nchunk, dst, dst_buf):
    P = k.P
    for r, a in enumerate(row_aps):
        P.dma("sp", k.stg[r:r + 1, 0:nchunk * 128], a, writes=[k.b_stg])
    transpose_rows(k, lambda c: dst[:, c, :], k.stg, len(row_aps), nchunk, k.b_stg, dst_buf)


def bcast_rows(k, dst, dram_ap_1d_off, tensor, n, dst_buf):
    src = bass.AP(tensor, dram_ap_1d_off, [[0, 128], [1, n]])
    k.P.dma("sp", dst, src, writes=[dst_buf])


def conv_silu(k, pre, b_pre, acc, b_acc, cw, b_cw, c, seqs, out_ap, out_buf, post_scale=None):
    P = k.P
    T = seqs[0][1]
    ns = len(seqs)
    P.op("dve", lambda e: e.tensor_scalar(out=acc[:], in0=pre[:], scalar1=cw[:, c, 2:3], scalar2=cw[:, c, 5:6], op0=ALU.mult, op1=ALU.add),
         reads=[b_pre, b_cw], writes=[b_acc])
    for jj, eng in ((0, "dve"), (1, "dve"), (3, "dve"), (4, "dve")):
        s = jj - 2
        a, b = max(0, -s), T - max(0, s)
        o = V(acc, 0, 128, a, [[T, ns], [1, b - a]])
        i = V(pre, 0, 128, a + s, [[T, ns], [1, b - a]])
        P.op(eng, lambda e: e.scalar_tensor_tensor(out=o, in0=i, scalar=cw[:, c, jj:jj + 1], in1=o, op0=ALU.mult, op1=ALU.add),
             reads=[b_pre, b_cw, b_acc], writes=[b_acc])
    if post_scale is None:
        P.op("act", lambda e: e.activation(out=out_ap, in_=acc[:], func=AF.Silu), reads=[b_acc], writes=[out_buf])
    else:
        P.op("act", lambda e: e.activation(out=acc[:], in_=acc[:], func=AF.Silu), reads=[b_acc], writes=[b_acc])
        P.op("dve", lambda e: e.tensor_scalar(out=out_ap, in0=acc[:], scalar1=post_scale, scalar2=None, op0=ALU.mult), reads=[b_acc], writes=[out_buf])


def proj_fm_chunk(k, wv, wb, j, evac):
    P = k.P
    for g in range(2):
        ps, pb = k.nxt("mm")
        for kc in range(8):
            P.op("pe", lambda e: e.matmul(ps[:], lhsT=wv[:, kc, j * 128:(j + 1) * 128], rhs=k.hT[:, kc, g * 512:(g + 1) * 512],
                                          start=(kc == 0), stop=(kc == 7)),
                 reads=[wb] + k.b_hT[g * 4:(g + 1) * 4], writes=[pb], inc=(kc == 7))
        evac(g, ps, pb)


def proj_tm(k, wv, wb, c0, ncols, evac):
    P = k.P
    pend = None
    for t in range(8):
        ps, pb = k.nxt("mm")
        for kc in range(8):
            P.op("pe", lambda e: e.matmul(ps[:, 0:ncols], lhsT=k.hT[:, kc, t * 128:(t + 1) * 128], rhs=wv[:, kc, c0:c0 + ncols],
                                          start=(kc == 0), stop=(kc == 7)),
                 reads=[wb, k.b_hT[t]], writes=[pb], inc=(kc == 7))
        if pend is not None:
            pend()
        pend = evac(t, ps, pb)
        if not callable(pend):
            pend = None
    if pend is not None:
        pend()


def log_sigmoid_inplace(k, x, bx, tmp, btmp, tmp2):
    P = k.P
    P.op("act", lambda e: e.activation(out=tmp, in_=x, func=AF.Abs), reads=[bx], writes=[btmp])
    P.op("act", lambda e: e.activation(out=tmp, in_=tmp, func=AF.Exp, scale=-1.0), reads=[btmp], writes=[btmp])
    P.op("act", lambda e: e.activation(out=tmp, in_=tmp, func=AF.Ln, bias=1.0), reads=[btmp], writes=[btmp])
    P.op("dve", lambda e: e.tensor_scalar_min(out=tmp2, in0=x, scalar1=0.0), reads=[bx], writes=[btmp])
    P.op("dve", lambda e: e.tensor_tensor(out=x, in0=tmp2, in1=tmp, op=ALU.subtract), reads=[btmp], writes=[bx])


def softplus_inplace(k, x, bx, tmp, btmp, tmp2):
    P = k.P
    P.op("act", lambda e: e.activation(out=tmp, in_=x, func=AF.Abs), reads=[bx], writes=[btmp])
    P.op("act", lambda e: e.activation(out=tmp, in_=tmp, func=AF.Exp, scale=-1.0), reads=[btmp], writes=[btmp])
    P.op("act", lambda e: e.activation(out=tmp, in_=tmp, func=AF.Ln, bias=1.0), reads=[btmp], writes=[btmp])
    P.op("dve", lambda e: e.tensor_scalar_max(out=tmp2, in0=x, scalar1=0.0), reads=[bx], writes=[btmp])
    P.op("dve", lambda e: e.tensor_tensor(out=x, in0=tmp2, in1=tmp, op=ALU.add), reads=[btmp], writes=[bx])


def group_norm_to_mixT(k, src, b_src, t, ngroups, gsize, nw_fm, b_nw, chunk0, scr):
    P = k.P
    sq, ss, hn, bs = scr["sq"], scr["ss"], scr["hn"], scr["b"]
    width = ngroups * gsize
    P.op("pool", lambda e: e.tensor_tensor(out=sq[:, 0:width], in0=src, in1=src, op=ALU.mult), reads=[b_src], writes=[bs])
    P.op("dve", lambda e: e.tensor_reduce(out=ss[:, 0:ngroups], in_=sq[:, 0:width].rearrange("p (g f) -> p g f", g=ngroups), axis=AX.X, op=ALU.add),
         reads=[bs], writes=[bs])
    P.op("dve", lambda e: e.tensor_scalar(out=ss[:, 0:ngroups], in0=ss[:, 0:ngroups], scalar1=1.0 / gsize, scalar2=EPS, op0=ALU.mult, op1=ALU.add), reads=[bs], writes=[bs])
    P.op("act", lambda e: e.activation(out=ss[:, 0:ngroups], in_=ss[:, 0:ngroups], func=AF.Sqrt), reads=[bs], writes=[bs])
    P.op("dve", lambda e: e.reciprocal(out=ss[:, 0:ngroups], in_=ss[:, 0:ngroups]), reads=[bs], writes=[bs])
    P.op("dve", lambda e: e.tensor_tensor(out=hn[:, 0:width].rearrange("p (g f) -> p g f", g=ngroups), in0=src.rearrange("p (g f) -> p g f", g=ngroups),
                                          in1=V(ss, 0, 128, 0, [[1, ngroups], [0, gsize]]), op=ALU.mult), reads=[b_src, bs], writes=[bs])
    nch = width // 128

    def tail():
        ps, pb = k.nxt("tr")
        for c in range(nch):
            P.op("pe", lambda e: e.transpose(out=ps[:, c * 128:(c + 1) * 128], in_=hn[:, c * 128:(c + 1) * 128], identity=k.idb[:]),
                 reads=[bs, k.b_idb], writes=[pb], inc=(c == nch - 1))
        for c in range(nch):
            P.op("act", lambda e: e.activation(out=k.mixT[:, chunk0 + c, t * 128:(t + 1) * 128], in_=ps[:, c * 128:(c + 1) * 128], func=AF.Copy, scale=nw_fm[:, c:c + 1]),
                 reads=[pb, b_nw], writes=[k.b_mixT[t]])
    return tail


def emit_even(k, l, bi):
    P, dr = k.P, k.dr
    j = l // 2
    seqs = [(s * 256, 256) for s in range(4)] if bi == 0 else [(0, 1024)]
    skip = k.opts.get("skip", ())
    if "mlstm" in skip:
        for t in range(8):
            P.op("pool", lambda e: e.memset(k.mixT[:, 0:4, t * 128:(t + 1) * 128], 0.0), writes=[k.b_mixT[t]])
    else:
        emit_mlstm(k, l, j, bi, seqs)
    if "ssd" in skip:
        for t in range(8):
            P.op("pool", lambda e: e.memset(k.mixT[:, 4:8, t * 128:(t + 1) * 128], 0.0), writes=[k.b_mixT[t]])
    else:
        emit_ssd(k, l, j, bi, seqs)


def emit_mlstm(k, l, j, bi, seqs):
    P, dr = k.P, k.dr
    with P.phase():
        cw = P.sb("cwA", [128, 8, 6]); b_cw = P.buf()
        load_rows_fm(k, [dr["conv_a_w"][j, r:r + 1, :] for r in range(5)] + [dr["conv_a_b"][j:j + 1, :]], 8, cw, b_cw)
        anw = P.sb("anw", [128, 4, 1]); b_anw = P.buf()
        load_rows_fm(k, [dr["a_norm_w"][j:j + 1, :]], 4, anw, b_anw)
        gb = P.sb("gb", [128, 16]); b_gb = P.buf()
        bcast_rows(k, gb[:], j * 16, dr["gate_b"].tensor, 16, b_gb)
        qkT = P.sb("qkT", [128, 8, 1024], BF16); b_qkT = [P.buf() for _ in range(8)]
        ktm = P.sb("ktm", [128, 8, 512], BF16); b_ktm = [P.buf() for _ in range(8)]
        vtm = P.sb("vtm", [128, 8, 512], BF16); b_vtm = [P.buf() for _ in range(8)]
        gt = P.sb("gates", [128, 8, 16]); b_gt = P.buf()
        bb = P.sb("bb", [128, 8, 8]); b_bb = P.buf()
        aa = P.sb("aa", [128, 8, 8]); b_aa = P.buf()
        gtmp = P.sb("gtmp", [128, 2, 64]); b_gtmp = P.buf()
        with P.phase():
            pre = [(P.sb("pre%d" % i, [128, 1024]), P.buf()) for i in range(2)]
            acc = [(P.sb("acc%d" % i, [128, 1024]), P.buf()) for i in range(2)]
            for blk in range(2):
                wv, wb = load_w(k, "w_in_even", j, blk * 512, 512)
                for jj in range(4):
                    c = blk * 4 + jj
                    pr, bpr = pre[c % 2]
                    ac, bac = acc[c % 2]
                    proj_fm_chunk(k, wv, wb, jj, lambda g, ps, pb: P.op("act", lambda e: acopy(e, out=pr[:, g * 512:(g + 1) * 512], in_=ps[:]), reads=[pb], writes=[bpr]))
                    conv_silu(k, pr, bpr, ac, bac, cw, b_cw, c, seqs, qkT[:, c, :], b_qkT[c], post_scale=(128.0 ** -0.5 if c >= 4 else None))
            for t in range(8):
                ps, pb = k.nxt("tr")
                for h in range(4):
                    P.op("pe", lambda e: e.transpose(out=ps[:, h * 128:(h + 1) * 128], in_=qkT[:, 4 + h, t * 128:(t + 1) * 128], identity=k.idb[:]),
                         reads=[b_qkT[4 + h], k.b_idb], writes=[pb], inc=(h == 3))
                P.op("dve", lambda e: e.tensor_copy(out=ktm[:, t, :], in_=ps[:, 0:512]), reads=[pb], writes=[b_ktm[t]])
            wv, wb = load_w(k, "w_in_even", j, 1024, 512)
            proj_tm(k, wv, wb, 0, 512, lambda t, ps, pb: P.op("act", lambda e: acopy(e, out=vtm[:, t, :], in_=ps[:]), reads=[pb], writes=[b_vtm[t]]))
            wv, wb = load_w(k, "w_in_even", j, 2048, 16)
            proj_tm(k, wv, wb, 0, 16, lambda t, ps, pb: P.op("dve", lambda e: e.tensor_tensor(out=gt[:, t, :], in0=ps[:, 0:16], in1=gb[:], op=ALU.add),
                                                              reads=[pb, b_gb], writes=[b_gt]))
            lfv = gt[:, :, 8:16]
            log_sigmoid_inplace(k, lfv, b_gt, gtmp[:, 0, :].rearrange("p (t f) -> p t f", t=8), b_gtmp, gtmp[:, 1, :].rearrange("p (t f) -> p t f", t=8))
            ps, pb = k.nxt("x")
            for t in range(8):
                P.op("pe", lambda e: e.matmul(ps[:, t * 8:t * 8 + 4], lhsT=k.maskF[:], rhs=gt[:, t, 8:12], start=True, stop=True), reads=[k.b_mask, b_gt], writes=[pb], inc=False)
                P.op("pe", lambda e: e.matmul(ps[:, t * 8 + 4:t * 8 + 8], lhsT=k.maskB[:], rhs=gt[:, t, 12:16], start=True, stop=True), reads=[k.b_mask, b_gt], writes=[pb], inc=(t == 7))
            P.op("dve", lambda e: e.tensor_copy(out=bb[:], in_=ps[:, 0:64].rearrange("p (t f) -> p t f", t=8)), reads=[pb], writes=[b_bb])
            P.op("dve", lambda e: e.tensor_tensor(out=aa[:], in0=gt[:, :, 0:8], in1=bb[:], op=ALU.subtract), reads=[b_gt, b_bb], writes=[b_aa])
        with P.phase():
            hacc = P.sb("hacc", [128, 8, 512]); b_hacc = [P.buf() for _ in range(8)]
            seen = set()
            ch = []
            for d in range(2):
                ch.append(dict(C=P.sb("C%d" % d, [128, 4, 128]), Cb=P.sb("Cb%d" % d, [128, 4, 128], BF16), n=P.sb("n%d" % d, [128, 4]),
                               nb=P.sb("nb%d" % d, [128, 4], BF16), m=P.sb("m%d" % d, [128, 4]), bC=P.buf(), bCb=P.buf(), bn=P.buf(), bm=P.buf(),
                               sm=P.sb("sm%d" % d, [128, 8, 4]), bsm=P.buf(),
                               dg=P.sb("dg%d" % d, [128, 512]), bdg=P.buf(), eam=P.sb("eam%d" % d, [128, 512]), beam=P.buf(),
                               PT=P.sb("PT%d" % d, [128, 4, 128], BF16), bPT=P.buf(), kea=P.sb("kea%d" % d, [128, 4, 128], BF16), bkea=P.buf(),
                               tmp=P.sb("htmp%d" % d, [128, 512]), btmp=P.buf()))
            for si, (t0, T) in enumerate(seqs):
                nt = T // 128
                tb = t0 // 128
                for d in range(2):
                    c = ch[d]
                    if bi == 0:
                        P.op("pool", lambda e: e.memset(c["C"][:], 0.0), writes=[c["bC"]])
                        P.op("pool", lambda e: e.memset(c["n"][:], 0.0), writes=[c["bn"]])
                        P.op("pool", lambda e: e.memset(c["m"][:], 0.0), writes=[c["bm"]])
                    else:
                        P.dma("sp", c["C"][:], dr["st_C"][j, d].rearrange("h d v -> d h v"), writes=[c["bC"]])
                        P.dma("sp", c["n"][:], dr["st_n"][j, d].rearrange("h d -> d h"), writes=[c["bn"]], allow_slow_non_contiguous=True)
                        bcast_rows(k, c["m"][:], (j * 2 + d) * 4, dr["st_m"].tensor, 4, c["bm"])
                for i in range(nt):
                    gens = []
                    for d in range(2):
                        t = tb + (i if d == 0 else nt - 1 - i)
                        first = t not in seen
                        seen.add(t)
                        g_ = mlstm_step(k, ch[d], d, first, t, qkT, b_qkT, ktm, b_ktm, vtm, b_vtm, gt, b_gt, bb, b_bb, aa, b_aa, hacc, b_hacc)
                        next(g_)
                        gens.append(g_)
                    for g_ in gens:
                        next(g_, None)
                if bi == 0:
                    for d in range(2):
                        c = ch[d]
                        P.dma("sp", dr["o_C"][si, j, d].rearrange("h d v -> d h v"), c["C"][:], reads=[c["bC"]])
                        P.dma("sp", dr["o_n"][si, j, d].rearrange("h d -> d h"), c["n"][:], reads=[c["bn"]], allow_slow_non_contiguous=True)
                        P.dma("sp", dr["o_m"][si, j, d:d + 1, :], c["m"][0:1, :], reads=[c["bm"]])
            otms = [(P.sb("otm%d" % i, [128, 512], BF16), P.buf()) for i in range(2)]
            scrs = [dict(sq=P.sb("fsq%d" % i, [128, 512]), ss=P.sb("fss%d" % i, [128, 4]), hn=P.sb("fhn%d" % i, [128, 512], BF16), b=P.buf()) for i in range(2)]
            wv, wb = load_w(k, "w_in_even", j, 1536, 512)

            def fin(t, ps, pb):
                otm, b_otm = otms[t % 2]
                P.op("act", lambda e: e.activation(out=otm[:], in_=ps[:], func=AF.Sigmoid), reads=[pb], writes=[b_otm])
                P.op("dve", lambda e: e.tensor_tensor(out=hacc[:, t, :], in0=hacc[:, t, :], in1=otm[:], op=ALU.mult), reads=[b_otm, b_hacc[t]], writes=[b_hacc[t]])
                return group_norm_to_mixT(k, hacc[:, t, :], b_hacc[t], t, 4, 128, anw[:, :, 0], b_anw, 0, scrs[t % 2])
            proj_tm(k, wv, wb, 0, 512, fin)


def mlstm_step(k, c, d, first, t, qkT, b_qkT, ktm, b_ktm, vtm, b_vtm, gt, b_gt, bb, b_bb, aa, b_aa, hacc, b_hacc):
    P = k.P
    mask = k.maskF if d == 0 else k.maskB
    a_ap = aa[:, t, d * 4:d * 4 + 4]
    b_ap = bb[:, t, d * 4:d * 4 + 4]
    lf_ap = gt[:, t, 8 + d * 4:12 + d * 4]
    sm = c["sm"]
    bsm = c["bsm"]
    tok = slice(t * 128, (t + 1) * 128)
    P.op("dve", lambda e: e.tensor_tensor(out=c["dg"][:].rearrange("p (h s) -> p h s", h=4), in0=V(k.idf, 0, 128, 0, [[0, 4], [1, 128]]),
                                          in1=V(aa, 0, 128, t * 8 + d * 4, [[1, 4], [0, 128]]), op=ALU.mult),
         reads=[k.b_idf, b_aa], writes=[c["bdg"]])
    ps_a, pb_a = k.nxt("mm")
    P.op("pe", lambda e: e.matmul(ps_a[:], lhsT=k.onesf[:], rhs=c["dg"][:], start=True, stop=True), reads=[k.b_ones, c["bdg"]], writes=[pb_a])
    ps_b, pb_b = k.nxt("x")
    P.op("pe", lambda e: e.matmul(ps_b[:, 0:4], lhsT=k.onesf[:], rhs=lf_ap, start=True, stop=True), reads=[k.b_ones, b_gt], writes=[pb_b])
    P.op("dve", lambda e: e.tensor_reduce(out=sm[:, 0, :], in_=ps_a[:].rearrange("p (h s) -> p h s", h=4), axis=AX.X, op=ALU.max), reads=[pb_a], writes=[bsm])
    P.op("dve", lambda e: e.tensor_tensor(out=sm[:, 1, :], in0=sm[:, 0, :], in1=c["m"][:], op=ALU.max), reads=[bsm, c["bm"]], writes=[bsm])
    P.op("dve", lambda e: e.tensor_tensor(out=sm[:, 2, :], in0=c["m"][:], in1=sm[:, 1, :], op=ALU.subtract), reads=[bsm, c["bm"]], writes=[bsm])
    P.op("act", lambda e: e.activation(out=sm[:, 2, :], in_=sm[:, 2, :], func=AF.Exp), reads=[bsm], writes=[bsm])
    P.op("dve", lambda e: e.tensor_tensor(out=c["m"][:], in0=ps_b[:, 0:4], in1=sm[:, 1, :], op=ALU.add), reads=[pb_b, bsm], writes=[c["bm"]])
    P.op("dve", lambda e: e.tensor_tensor(out=sm[:, 3, :], in0=a_ap, in1=sm[:, 1, :], op=ALU.subtract), reads=[b_aa, bsm], writes=[bsm])
    P.op("act", lambda e: e.activation(out=sm[:, 3, :], in_=sm[:, 3, :], func=AF.Exp), reads=[bsm], writes=[bsm])
    P.op("dve", lambda e: e.tensor_tensor(out=sm[:, 4, :], in0=b_ap, in1=sm[:, 1, :], op=ALU.add), reads=[b_bb, bsm], writes=[bsm])
    P.op("act", lambda e: e.activation(out=sm[:, 4, :], in_=sm[:, 4, :], func=AF.Exp, scale=-1.0), reads=[bsm], writes=[bsm])
    P.op("dve", lambda e: e.tensor_tensor(out=c["C"][:], in0=c["C"][:], in1=V(sm, 0, 128, 8, [[1, 4], [0, 128]]), op=ALU.mult), reads=[bsm, c["bC"]], writes=[c["bC"]])
    P.op("dve", lambda e: e.tensor_tensor(out=c["n"][:], in0=c["n"][:], in1=sm[:, 2, :], op=ALU.mult), reads=[bsm, c["bn"]], writes=[c["bn"]])
    P.op("act", lambda e: acopy(e, out=c["Cb"][:], in_=c["C"][:]), reads=[c["bC"]], writes=[c["bCb"]])
    P.op("act", lambda e: acopy(e, out=c["nb"][:], in_=c["n"][:]), reads=[c["bn"]], writes=[c["bCb"]])
    ps_s, pb_s = k.nxt("mm")
    for h in range(4):
        P.op("pe", lambda e: e.matmul(ps_s[:, h * 128:(h + 1) * 128], lhsT=qkT[:, 4 + h, tok], rhs=qkT[:, h, tok], start=True, stop=True),
             reads=[b_qkT[4 + h], b_qkT[h]], writes=[pb_s], inc=(h == 3))
    P.op("pool", lambda e: e.tensor_tensor(out=c["eam"][:].rearrange("p (h s) -> p h s", h=4), in0=V(mask, 0, 128, 0, [[0, 4], [1, 128]]),
                                           in1=V(sm, 0, 128, 12, [[1, 4], [0, 128]]), op=ALU.mult), reads=[k.b_mask, bsm], writes=[c["beam"]])
    P.op("dve", lambda e: e.tensor_tensor(out=c["PT"][:].rearrange("p h s -> p (h s)"), in0=ps_s[:], in1=c["eam"][:], op=ALU.mult), reads=[pb_s, c["beam"]], writes=[c["bPT"]])
    ps_n, pb_n = k.nxt("mm")
    for h in range(4):
        P.op("pe", lambda e: e.matmul(ps_n[:, h * 128:(h + 1) * 128], lhsT=c["PT"][:, h, :], rhs=vtm[:, t, h * 128:(h + 1) * 128], start=True, stop=False),
             reads=[c["bPT"], b_vtm[t]], writes=[pb_n], inc=False)
        P.op("pe", lambda e: e.matmul(ps_n[:, h * 128:(h + 1) * 128], lhsT=qkT[:, h, tok], rhs=c["Cb"][:, h, :], start=False, stop=True),
             reads=[b_qkT[h], c["bCb"]], writes=[pb_n], inc=False)
    for h in range(4):
        P.op("pe", lambda e: e.matmul(ps_b[:, 8 + h:9 + h], lhsT=c["PT"][:, h, :], rhs=k.onesb[:, 0:1], start=True, stop=False),
             reads=[c["bPT"], k.b_onesb], writes=[pb_b], inc=False)
        P.op("pe", lambda e: e.matmul(ps_b[:, 8 + h:9 + h], lhsT=qkT[:, h, tok], rhs=c["nb"][:, h:h + 1], start=False, stop=True),
             reads=[b_qkT[h], c["bCb"]], writes=[pb_b], inc=(h == 3))
    yield
    P.op("act", lambda e: e.activation(out=sm[:, 5, :], in_=ps_b[:, 8:12], func=AF.Abs), reads=[pb_b], writes=[bsm])
    P.op("dve", lambda e: e.tensor_tensor(out=sm[:, 5, :], in0=sm[:, 5, :], in1=sm[:, 4, :], op=ALU.max), reads=[bsm], writes=[bsm])
    P.op("dve", lambda e: e.reciprocal(out=sm[:, 6, :], in_=sm[:, 5, :]), reads=[bsm], writes=[bsm])
    rd_bc = V(sm, 0, 128, 24, [[1, 4], [0, 128]])
    if first:
        P.op("dve", lambda e: e.tensor_tensor(out=hacc[:, t, :].rearrange("p (h s) -> p h s", h=4), in0=ps_n[:].rearrange("p (h s) -> p h s", h=4), in1=rd_bc, op=ALU.mult),
             reads=[pb_n, bsm], writes=[b_hacc[t]])
    else:
        P.op("dve", lambda e: e.tensor_tensor(out=c["tmp"][:].rearrange("p (h s) -> p h s", h=4), in0=ps_n[:].rearrange("p (h s) -> p h s", h=4), in1=rd_bc, op=ALU.mult),
             reads=[pb_n, bsm], writes=[c["btmp"]])
        P.op("pool", lambda e: e.tensor_tensor(out=hacc[:, t, :], in0=hacc[:, t, :], in1=c["tmp"][:], op=ALU.add), reads=[c["btmp"], b_hacc[t]], writes=[b_hacc[t]])
    P.op("pool", lambda e: e.tensor_tensor(out=c["kea"][:], in0=ktm[:, t, :].rearrange("p (h s) -> p h s", h=4), in1=V(sm, 0, 128, 12, [[1, 4], [0, 128]]), op=ALU.mult),
         reads=[b_ktm[t], bsm], writes=[c["bkea"]])
    ps_c, pb_c = k.nxt("mm")
    for h in range(4):
        P.op("pe", lambda e: e.matmul(ps_c[:, h * 128:(h + 1) * 128], lhsT=c["kea"][:, h, :], rhs=vtm[:, t, h * 128:(h + 1) * 128], start=True, stop=True),
             reads=[c["bkea"], b_vtm[t]], writes=[pb_c], inc=False)
    for h in range(4):
        P.op("pe", lambda e: e.matmul(ps_b[:, 16 + h:17 + h], lhsT=c["kea"][:, h, :], rhs=k.onesb[:, 0:1], start=True, stop=True),
             reads=[c["bkea"], k.b_onesb], writes=[pb_b, pb_c], inc=(h == 3))
    P.op("dve", lambda e: e.tensor_tensor(out=c["C"][:].rearrange("p h s -> p (h s)"), in0=ps_c[:], in1=c["C"][:].rearrange("p h s -> p (h s)"), op=ALU.add),
         reads=[pb_c, c["bC"]], writes=[c["bC"]])
    P.op("dve", lambda e: e.tensor_tensor(out=c["n"][:], in0=ps_b[:, 16:20], in1=c["n"][:], op=ALU.add), reads=[pb_b, c["bn"]], writes=[c["bn"]])


def emit_ssd(k, l, j, bi, seqs):
    P, dr = k.P, k.dr
    base = 2576
    with P.phase():
        cw = P.sb("cwB", [128, 8, 6]); b_cw = P.buf()
        load_rows_fm(k, [dr["conv_b_w"][j, r:r + 1, :] for r in range(5)] + [dr["conv_b_b"][j:j + 1, :]], 8, cw, b_cw)
        bnw = P.sb("bnw", [128, 4, 1]); b_bnw = P.buf()
        load_rows_fm(k, [dr["b_norm_w"][j:j + 1, :]], 4, bnw, b_bnw)
        dtb = P.sb("dtb", [128, 16]); b_dtb = P.buf()
        bcast_rows(k, dtb[:], j * 16, dr["dt_bias"].tensor, 16, b_dtb)
        Aneg = P.sb("Aneg", [128, 16]); b_A = P.buf()
        bcast_rows(k, Aneg[:], j * 16, dr["a_log"].tensor, 16, b_A)
        P.op("act", lambda e: e.activation(out=Aneg[:], in_=Aneg[:], func=AF.Exp), reads=[b_A], writes=[b_A])
        dsk = P.sb("dsk", [128, 8]); b_dsk = P.buf()
        bcast_rows(k, dsk[:], j * 8, dr["d_skip"].tensor, 8, b_dsk)
        xbcT = P.sb("xbcT", [128, 8, 1024], BF16); b_xbcT = [P.buf() for _ in range(8)]
        xtm = P.sb("xtm", [128, 8, 512], BF16); b_xtm = [P.buf() for _ in range(8)]
        Btm = P.sb("Btm", [128, 8, 256], BF16); b_Btm = [P.buf() for _ in range(8)]
        dt = P.sb("dt", [128, 8, 16]); b_dt = P.buf()
        dtA = P.sb("dtA", [128, 8, 16]); b_dtA = P.buf()
        acum = P.sb("acum", [128, 8, 16]); b_acum = P.buf()
        gtmp = P.sb("gtmp2", [128, 2, 128]); b_gtmp = P.buf()
        stage = k.opts.get("ssd_stage", 99)
        if stage <= 1:
            return
        with P.phase():
            pre = [(P.sb("pre%d" % i, [128, 1024]), P.buf()) for i in range(2)]
            acc = [(P.sb("acc%d" % i, [128, 1024]), P.buf()) for i in range(2)]
            for blk in range(2):
                wv, wb = load_w(k, "w_in_even", j, base + blk * 512, 512)
                for jj in range(4):
                    c = blk * 4 + jj
                    pr, bpr = pre[c % 2]
                    ac, bac = acc[c % 2]
                    proj_fm_chunk(k, wv, wb, jj, lambda g, ps, pb: P.op("act", lambda e: acopy(e, out=pr[:, g * 512:(g + 1) * 512], in_=ps[:]), reads=[pb], writes=[bpr]))
                    conv_silu(k, pr, bpr, ac, bac, cw, b_cw, c, seqs, xbcT[:, c, :], b_xbcT[c])
            ntr = k.opts.get("ssd_tr", 6)
            for t in range(8 if stage > 2 else 0):
                ps, pb = k.nxt("tr")
                for c in range(ntr):
                    P.op("pe", lambda e: e.transpose(out=ps[:, c * 128:(c + 1) * 128], in_=xbcT[:, c, t * 128:(t + 1) * 128], identity=k.idb[:]),
                         reads=[b_xbcT[c], k.b_idb], writes=[pb], inc=(c == ntr - 1))
                P.op("dve", lambda e: e.tensor_copy(out=xtm[:, t, :], in_=ps[:, 0:512]), reads=[pb], writes=[b_xtm[t]])
                bt = k.opts.get("ssd_btm", 2)
                if bt == 1:
                    P.op("act", lambda e: acopy(e, out=Btm[:, t, :], in_=ps[:, 512:768]), reads=[pb], writes=[b_Btm[t]])
                elif bt == 2:
                    P.op("dve", lambda e: e.tensor_copy(out=Btm[:, t, :], in_=ps[:, 512:768]), reads=[pb], writes=[b_Btm[t]])
                elif bt == 4:
                    P.op("act", lambda e: e.activation(out=Btm[:, t, :], in_=ps[:, 512:768], func=AF.Identity, scale=k.onesf[:, 0:1]), reads=[pb, k.b_ones], writes=[b_Btm[t]])
                elif bt == 3:
                    for q in range(2):
                        P.op("act", lambda e: acopy(e, out=Btm[:, t, q * 128:(q + 1) * 128], in_=ps[:, 512 + q * 128:640 + q * 128]), reads=[pb], writes=[b_Btm[t]])
            if stage <= 3:
                return
            wv, wb = load_w(k, "w_in_even", j, 3600, 16)
            proj_tm(k, wv, wb, 0, 16, lambda t, ps, pb: P.op("dve", lambda e: e.tensor_tensor(out=dt[:, t, :], in0=ps[:, 0:16], in1=dtb[:], op=ALU.add),
                                                              reads=[pb, b_dtb], writes=[b_dt]))
            softplus_inplace(k, dt[:], b_dt, gtmp[:, 0, :].rearrange("p (t f) -> p t f", t=8), b_gtmp, gtmp[:, 1, :].rearrange("p (t f) -> p t f", t=8))
            P.op("dve", lambda e: e.scalar_tensor_tensor(out=dtA[:], in0=dt[:], scalar=-1.0, in1=V(Aneg, 0, 128, 0, [[0, 8], [1, 16]]), op0=ALU.mult, op1=ALU.mult),
                 reads=[b_dt, b_A], writes=[b_dtA])
            ps, pb = k.nxt("x")
            for t in range(8):
                P.op("pe", lambda e: e.matmul(ps[:, t * 16:t * 16 + 8], lhsT=k.maskF[:], rhs=dtA[:, t, 0:8], start=True, stop=True), reads=[k.b_mask, b_dtA], writes=[pb], inc=False)
                P.op("pe", lambda e: e.matmul(ps[:, t * 16 + 8:t * 16 + 16], lhsT=k.maskB[:], rhs=dtA[:, t, 8:16], start=True, stop=True), reads=[k.b_mask, b_dtA], writes=[pb], inc=(t == 7))
            P.op("dve", lambda e: e.tensor_copy(out=acum[:], in_=ps[:, 0:128].rearrange("p (t f) -> p t f", t=8)), reads=[pb], writes=[b_acum])
        if stage <= 4:
            return
        with P.phase():
            yss = P.sb("yss", [128, 8, 512]); b_yss = [P.buf() for _ in range(8)]
            seen = set()
            ch = []
            for d in range(2):
                ch.append(dict(S=P.sb("S%d" % d, [128, 8, 64]), Sb=P.sb("Sb%d" % d, [128, 8, 64], BF16), bS=P.buf(), bSb=P.buf(),
                               sm=P.sb("ssm%d" % d, [128, 4, 8]), bsm=P.buf()))
            tp = dict(dgA=P.sb("dgA", [128, 1024]), bdgA=P.buf(), Dm=P.sb("Dm", [128, 1024]), bDm=P.buf(),
                      GT=P.sb("GT", [128, 8, 128], BF16), bGT=P.buf(), xdt=P.sb("xdt", [128, 8, 64], BF16), bxdt=P.buf(),
                      xw=P.sb("xw", [128, 8, 64], BF16), bxw=P.buf(), tmp=P.sb("stmp", [128, 512]), btmp=P.buf())
            for si, (t0, T) in enumerate(seqs):
                nt = T // 128
                tb = t0 // 128
                for d in range(2):
                    c = ch[d]
                    if bi == 0:
                        P.op("pool", lambda e: e.memset(c["S"][:], 0.0), writes=[c["bS"]])
                    else:
                        for h2 in range(4):
                            P.dma("sp", tp["Dm"][:, h2 * 128:(h2 + 1) * 128], dr["st_S"][j, d, 2 * h2:2 * h2 + 2].rearrange("h p n -> (h p) n"), writes=[tp["bDm"]])
                        ps, pb = k.nxt("x")
                        for h2 in range(4):
                            P.op("pe", lambda e: e.transpose(out=ps[:, h2 * 128:(h2 + 1) * 128], in_=tp["Dm"][:, h2 * 128:(h2 + 1) * 128], identity=k.idf[:]),
                                 reads=[tp["bDm"], k.b_idf], writes=[pb], inc=(h2 == 3))
                        P.op("dve", lambda e: e.tensor_copy(out=c["S"][:].rearrange("p h s -> p (h s)"), in_=ps[:]), reads=[pb], writes=[c["bS"]])
                    P.op("act", lambda e: acopy(e, out=c["Sb"][:], in_=c["S"][:]), reads=[c["bS"]], writes=[c["bSb"]])
                for i in range(nt):
                    gens = []
                    for d in range(2):
                        t = tb + (i if d == 0 else nt - 1 - i)
                        first = t not in seen
                        seen.add(t)
                        g_ = ssd_step(k, ch[d], tp, d, first, t, xbcT, b_xbcT, xtm, b_xtm, Btm, b_Btm, dt, b_dt, dtA, b_dtA, acum, b_acum, yss, b_yss)
                        for _ in g_:
                            pass
                if bi == 0:
                    for d in range(2):
                        c = ch[d]
                        ps, pb = k.nxt("x")
                        for h2 in range(4):
                            P.op("pe", lambda e: e.transpose(out=ps[:, h2 * 128:(h2 + 1) * 128], in_=c["S"][:, 2 * h2:2 * h2 + 2, :].rearrange("p h s -> p (h s)"), identity=k.idf[:]),
                                 reads=[c["bS"], k.b_idf], writes=[pb], inc=(h2 == 3))
                        P.op("dve", lambda e: e.tensor_copy(out=tp["Dm"][:, 0:512], in_=ps[:]), reads=[pb], writes=[tp["bDm"]])
                        for h2 in range(4):
                            P.dma("sp", dr["o_S"][si, j, d, 2 * h2:2 * h2 + 2].rearrange("h p n -> (h p) n"), tp["Dm"][:, h2 * 128:(h2 + 1) * 128], reads=[tp["bDm"]])
            if stage <= 5:
                return
            ztms = [(P.sb("ztm%d" % i, [128, 512], BF16), P.buf()) for i in range(2)]
            scrs = [dict(sq=P.sb("gsq%d" % i, [128, 512]), ss=P.sb("gss%d" % i, [128, 4]), hn=P.sb("ghn%d" % i, [128, 512], BF16), b=P.buf()) for i in range(2)]
            wv, wb = load_w(k, "w_in_even", j, 2064, 512)

            def fin(t, ps, pb):
                ztm, b_ztm = ztms[t % 2]
                scr = scrs[t % 2]
                P.op("act", lambda e: e.activation(out=ztm[:], in_=ps[:], func=AF.Silu), reads=[pb], writes=[b_ztm])
                P.op("dve", lambda e: e.tensor_tensor(out=scr["sq"][:].rearrange("p (h s) -> p h s", h=8), in0=xtm[:, t, :].rearrange("p (h s) -> p h s", h=8),
                                                      in1=V(dsk, 0, 128, 0, [[1, 8], [0, 64]]), op=ALU.mult), reads=[b_xtm[t], b_dsk], writes=[scr["b"]])
                P.op("dve", lambda e: e.tensor_tensor(out=yss[:, t, :], in0=yss[:, t, :], in1=scr["sq"][:], op=ALU.add), reads=[scr["b"], b_yss[t]], writes=[b_yss[t]])
                P.op("dve", lambda e: e.tensor_tensor(out=yss[:, t, :], in0=yss[:, t, :], in1=ztm[:], op=ALU.mult), reads=[b_ztm, b_yss[t]], writes=[b_yss[t]])
                return group_norm_to_mixT(k, yss[:, t, :], b_yss[t], t, 2, 256, bnw[:, :, 0], b_bnw, 4, scr)
            proj_tm(k, wv, wb, 0, 512, fin)


def ssd_step(k, c, tp, d, first, t, xbcT, b_xbcT, xtm, b_xtm, Btm, b_Btm, dt, b_dt, dtA, b_dtA, acum, b_acum, yss, b_yss):
    P = k.P
    nm = k.nmF if d == 0 else k.nmB
    tok = slice(t * 128, (t + 1) * 128)
    sm, bsm = c["sm"], c["bsm"]
    ac_off = t * 16 + d * 8
    P.op("dve", lambda e: e.tensor_tensor(out=tp["dgA"][:].rearrange("p (h s) -> p h s", h=8), in0=V(k.idf, 0, 128, 0, [[0, 8], [1, 128]]),
                                          in1=V(acum, 0, 128, ac_off, [[1, 8], [0, 128]]), op=ALU.mult), reads=[k.b_idf, b_acum], writes=[tp["bdgA"]])
    psD = []
    for hh in range(2):
        ps, pb = k.nxt("mm")
        P.op("pe", lambda e: e.matmul(ps[:], lhsT=k.onesf[:], rhs=tp["dgA"][:, hh * 512:(hh + 1) * 512], start=True, stop=False), reads=[k.b_ones, tp["bdgA"]], writes=[pb], inc=False)
        P.op("pe", lambda e: e.matmul(ps[:].rearrange("p (h s) -> p h s", h=4), lhsT=k.idf[:], rhs=V(nm, 0, 128, 0, [[0, 4], [1, 128]]), start=False, stop=True),
             reads=[k.b_idf, k.b_mask], writes=[pb])
        psD.append((ps, pb))
    for hh in range(2):
        ps, pb = psD[hh]
        P.op("dve", lambda e: e.tensor_tensor(out=tp["Dm"][:, hh * 512:(hh + 1) * 512].rearrange("p (h s) -> p h s", h=4), in0=ps[:].rearrange("p (h s) -> p h s", h=4),
                                              in1=V(acum, 0, 128, ac_off + hh * 4, [[1, 4], [0, 128]]), op=ALU.subtract), reads=[pb, b_acum], writes=[tp["bDm"]])
    P.op("act", lambda e: e.activation(out=tp["Dm"][:], in_=tp["Dm"][:], func=AF.Exp), reads=[tp["bDm"]], writes=[tp["bDm"]])
    sub = k.opts.get("ssd_sub", 99)
    if sub <= 1:
        return
    ps_cb, pb_cb = k.nxt("x")
    for g in range(2):
        P.op("pe", lambda e: e.matmul(ps_cb[:, g * 128:(g + 1) * 128], lhsT=xbcT[:, 4 + g, tok], rhs=xbcT[:, 6 + g, tok], start=True, stop=True),
             reads=[b_xbcT[4 + g], b_xbcT[6 + g]], writes=[pb_cb], inc=False)
    P.op("pe", lambda e: e.matmul(ps_cb[:, 256:264], lhsT=k.onesf[:], rhs=dtA[:, t, d * 8:d * 8 + 8], start=True, stop=True), reads=[k.b_ones, b_dtA], writes=[pb_cb])
    for g in range(2):
        P.op("dve", lambda e: e.tensor_tensor(out=tp["GT"][:, g * 4:(g + 1) * 4, :], in0=tp["Dm"][:, g * 512:(g + 1) * 512].rearrange("p (h s) -> p h s", h=4),
                                              in1=V(ps_cb, 0, 128, g * 128, [[0, 4], [1, 128]]), op=ALU.mult), reads=[tp["bDm"], pb_cb], writes=[tp["bGT"]])
    if sub <= 2:
        return
    P.op("act", lambda e: e.activation(out=sm[:, 0, :], in_=acum[:, t, d * 8:d * 8 + 8], func=AF.Exp), reads=[b_acum], writes=[bsm])
    P.op("dve", lambda e: e.tensor_tensor(out=sm[:, 1, :], in0=ps_cb[:, 256:264], in1=acum[:, t, d * 8:d * 8 + 8], op=ALU.subtract), reads=[pb_cb, b_acum], writes=[bsm])
    P.op("act", lambda e: e.activation(out=sm[:, 1, :], in_=sm[:, 1, :], func=AF.Exp), reads=[bsm], writes=[bsm])
    P.op("dve", lambda e: e.tensor_tensor(out=sm[:, 1, :], in0=sm[:, 1, :], in1=dt[:, t, d * 8:d * 8 + 8], op=ALU.mult), reads=[bsm, b_dt], writes=[bsm])
    P.op("dve", lambda e: e.tensor_copy(out=sm[:, 3, :], in_=ps_cb[:, 256:264]), reads=[pb_cb], writes=[bsm])
    P.op("act", lambda e: e.activation(out=sm[:, 2, :], in_=sm[:, 3, :], func=AF.Exp), reads=[bsm], writes=[bsm])
    xv = xtm[:, t, :].rearrange("p (h s) -> p h s", h=8)
    P.op("pool", lambda e: e.tensor_tensor(out=tp["xdt"][:], in0=xv, in1=V(dt, 0, 128, t * 16 + d * 8, [[1, 8], [0, 64]]), op=ALU.mult), reads=[b_xtm[t], b_dt], writes=[tp["bxdt"]])
    P.op("pool", lambda e: e.tensor_tensor(out=tp["xw"][:], in0=xv, in1=V(sm, 0, 128, 8, [[1, 8], [0, 64]]), op=ALU.mult), reads=[b_xtm[t], bsm], writes=[tp["bxw"]])
    if sub <= 3:
        return
    ps_y, pb_y = k.nxt("mm")
    ps_z, pb_z = k.nxt("mm")
    for hd in range(8):
        P.op("pe", lambda e: e.matmul(ps_y[:, hd * 64:(hd + 1) * 64], lhsT=tp["GT"][:, hd, :], rhs=tp["xdt"][:, hd, :], start=True, stop=True),
             reads=[tp["bGT"], tp["bxdt"]], writes=[pb_y], inc=(hd == 7))
    for hd in range(8):
        P.op("pe", lambda e: e.matmul(ps_z[:, hd * 64:(hd + 1) * 64], lhsT=xbcT[:, 6 + hd // 4, tok], rhs=c["Sb"][:, hd, :], start=True, stop=True),
             reads=[b_xbcT[6 + hd // 4], c["bSb"]], writes=[pb_z], inc=(hd == 7))
    yield
    P.op("dve", lambda e: e.tensor_tensor(out=tp["tmp"][:].rearrange("p (h s) -> p h s", h=8), in0=ps_z[:].rearrange("p (h s) -> p h s", h=8),
                                          in1=V(sm, 0, 128, 0, [[1, 8], [0, 64]]), op=ALU.mult), reads=[pb_z, bsm], writes=[tp["btmp"]])
    if first:
        P.op("dve", lambda e: e.tensor_tensor(out=yss[:, t, :], in0=ps_y[:], in1=tp["tmp"][:], op=ALU.add), reads=[pb_y, tp["btmp"]], writes=[b_yss[t]])
    else:
        P.op("dve", lambda e: e.tensor_tensor(out=tp["tmp"][:], in0=ps_y[:], in1=tp["tmp"][:], op=ALU.add), reads=[pb_y, tp["btmp"]], writes=[tp["btmp"]])
        P.op("pool", lambda e: e.tensor_tensor(out=yss[:, t, :], in0=yss[:, t, :], in1=tp["tmp"][:], op=ALU.add), reads=[tp["btmp"], b_yss[t]], writes=[b_yss[t]])
    if sub <= 4:
        return
    ps_s, pb_s = k.nxt("mm")
    for hd in range(8):
        g = hd // 4
        P.op("pe", lambda e: e.matmul(ps_s[:, hd * 64:(hd + 1) * 64], lhsT=Btm[:, t, g * 128:(g + 1) * 128], rhs=tp["xw"][:, hd, :], start=True, stop=True),
             reads=[b_Btm[t], tp["bxw"]], writes=[pb_s], inc=(hd == 7))
    P.op("dve", lambda e: e.tensor_tensor(out=c["S"][:], in0=c["S"][:], in1=V(sm, 0, 128, 16, [[1, 8], [0, 64]]), op=ALU.mult), reads=[bsm, c["bS"]], writes=[c["bS"]])
    P.op("dve", lambda e: e.tensor_tensor(out=c["S"][:].rearrange("p h s -> p (h s)"), in0=ps_s[:], in1=c["S"][:].rearrange("p h s -> p (h s)"), op=ALU.add),
         reads=[pb_s, c["bS"]], writes=[c["bS"]])
    P.op("act", lambda e: acopy(e, out=c["Sb"][:], in_=c["S"][:]), reads=[c["bS"]], writes=[c["bSb"]])


def emit_outproj_resid(k, l, bi):
    P = k.P
    j = l // 2
    name = "w_out_even" if l % 2 == 0 else "w_out_odd"
    emit_gate_vec(k, l, bi, 0)
    with P.phase():
        ybuf = P.sb("ybuf", [128, 8, D]); b_y = [P.buf() for _ in range(16)]
        for h in range(2):
            wv, wb = load_w(k, name, j, h * 512, 512)
            for t in range(8):
                ps, pb = k.nxt("mm")
                for kc in range(8):
                    P.op("pe", lambda e: e.matmul(ps[:], lhsT=k.mixT[:, kc, t * 128:(t + 1) * 128], rhs=wv[:, kc, :], start=(kc == 0), stop=(kc == 7)),
                         reads=[wb, k.b_mixT[t]], writes=[pb], inc=(kc == 7))
                P.op("act", lambda e: acopy(e, out=ybuf[:, t, h * 512:(h + 1) * 512], in_=ps[:]), reads=[pb], writes=[b_y[t * 2 + h]])
        emit_resid_all(k, [([ybuf[:, t, 0:512], ybuf[:, t, 512:1024]], [b_y[t * 2], b_y[t * 2 + 1]]) for t in range(8)])


def emit_mixer(k, l, bi):
    if l % 2 == 0:
        emit_even(k, l, bi)
    else:
        emit_odd(k, l, bi)


def emit_odd(k, l, bi):
    raise NotImplementedError


def attn_unit(k, A, qT, ksegs, vlist, scale, sinkcol, out_ap, out_buf, reads):
    P = k.P
    u = A["rr"]
    A["rr"] = u + 1
    Ssb, bS = A["Ssb"][u % len(A["Ssb"])]
    Pb, bP = A["Pb"][u % len(A["Pb"])]
    PT, bPT = A["PT"][u % len(A["PT"])]
    sm, bsm = A["sm"][u % 2]
    col = 0
    for si, (kT, n, masks) in enumerate(ksegs):
        ps, pb = k.nxt("mm")
        P.op("pe", lambda e: e.matmul(ps[:, 0:n], lhsT=qT, rhs=kT, start=True, stop=(len(masks) == 0)), reads=reads, writes=[pb], inc=(len(masks) == 0))
        for mi, (c0, nm) in enumerate(masks):
            P.op("pe", lambda e: e.matmul(ps[:, c0:c0 + 128], lhsT=k.idb[:], rhs=nm, start=False, stop=(mi == len(masks) - 1)),
                 reads=[k.b_idb, A["b_nm"]], writes=[pb], inc=(mi == len(masks) - 1))
        eng = "act" if si % 2 == 0 else "dve"
        if eng == "act":
            P.op("act", lambda e: acopy(e, out=Ssb[:, col:col + n], in_=ps[:, 0:n]), reads=[pb], writes=[bS])
        else:
            P.op("dve", lambda e: e.tensor_copy(out=Ssb[:, col:col + n], in_=ps[:, 0:n]), reads=[pb], writes=[bS])
        col += n
    N = col
    nblk = N // 128
    assert nblk == len(vlist)
    P.op("dve", lambda e: e.tensor_reduce(out=sm[:, 0:1], in_=Ssb[:, 0:N], axis=AX.X, op=ALU.max), reads=[bS], writes=[bsm])
    if sinkcol is not None:
        P.op("dve", lambda e: e.tensor_scalar(out=sm[:, 1:2], in0=sm[:, 0:1], scalar1=-scale, scalar2=sinkcol, op0=ALU.mult, op1=ALU.min), reads=[bsm, A["b_sink"]], writes=[bsm])
    else:
        P.op("dve", lambda e: e.tensor_scalar(out=sm[:, 1:2], in0=sm[:, 0:1], scalar1=-scale, scalar2=None, op0=ALU.mult), reads=[bsm], writes=[bsm])
    P.op("act", lambda e: e.activation(out=Pb[:, 0:N], in_=Ssb[:, 0:N], func=AF.Exp, scale=scale, bias=sm[:, 1:2], accum_out=sm[:, 2:3]), reads=[bS, bsm], writes=[bP, bsm])
    has_sink = sinkcol is not None
    if has_sink:
        P.op("act", lambda e: e.activation(out=sm[:, 3:4], in_=sinkcol, func=AF.Exp, scale=-1.0, bias=sm[:, 1:2]), reads=[bsm, A["b_sink"]], writes=[bsm])

    def part2():
        _attn_part2(k, A, u, Pb, bP, PT, bPT, sm, bsm, nblk, vlist, reads, out_ap, out_buf, has_sink)
    prev = A.get("pend")
    A["pend"] = part2
    if prev is not None:
        prev()


def attn_flush(A):
    if A.get("pend") is not None:
        A["pend"]()
        A["pend"] = None


def _attn_part2(k, A, u, Pb, bP, PT, bPT, sm, bsm, nblk, vlist, reads, out_ap, out_buf, has_sink):
    P = k.P
    for b0 in range(0, nblk, 8):
        nb_ = min(8, nblk - b0)
        ps, pb = k.nxt("tr")
        for b in range(nb_):
            P.op("pe", lambda e: e.transpose(out=ps[:, b * 128:(b + 1) * 128], in_=Pb[:, (b0 + b) * 128:(b0 + b + 1) * 128], identity=k.idb[:]),
                 reads=[bP, k.b_idb], writes=[pb], inc=(b == nb_ - 1))
        if False:
            P.op("act", lambda e: acopy(e, out=PT[:, b0 * 128:(b0 + nb_) * 128], in_=ps[:, 0:nb_ * 128]), reads=[pb], writes=[bPT])
        else:
            P.op("dve", lambda e: e.tensor_copy(out=PT[:, b0 * 128:(b0 + nb_) * 128], in_=ps[:, 0:nb_ * 128]), reads=[pb], writes=[bPT])
    pso, pbo = k.nxt("x")
    for b in range(nblk):
        P.op("pe", lambda e: e.matmul(pso[:, 0:64], lhsT=PT[:, b * 128:(b + 1) * 128], rhs=vlist[b], start=(b == 0), stop=(b == nblk - 1)),
             reads=[bPT] + reads, writes=[pbo], inc=(b == nblk - 1))
    if has_sink:
        P.op("dve", lambda e: e.tensor_tensor(out=sm[:, 2:3], in0=sm[:, 2:3], in1=sm[:, 3:4], op=ALU.add), reads=[bsm], writes=[bsm])
    P.op("dve", lambda e: e.reciprocal(out=sm[:, 4:5], in_=sm[:, 2:3]), reads=[bsm], writes=[bsm])
    P.op("dve", lambda e: e.tensor_scalar(out=out_ap, in0=pso[:, 0:64], scalar1=sm[:, 4:5], scalar2=None, op0=ALU.mult), reads=[pbo, bsm], writes=[out_buf])


def rope_fm(k, A, ps, pb, nrows, perm, cosT, sinT, g, dst, dst_buf):
    P = k.P
    u = A["rrr"]
    A["rrr"] = u + 1
    raw, braw = A["raw"][u % 2]
    t1, bt1 = A["t1"][u % len(A["t1"])]
    cs = slice(g * 512, (g + 1) * 512)
    P.op("act", lambda e: acopy(e, out=raw[0:nrows, :], in_=ps[0:nrows, :]), reads=[pb], writes=[braw])
    ps2, pb2 = k.nxt("mm")
    P.op("pe", lambda e: e.matmul(ps2[0:nrows, :], lhsT=perm[0:nrows, 0:nrows], rhs=raw[0:nrows, :], start=True, stop=True), reads=[braw, A["b_rope"]], writes=[pb2])
    P.op("pool", lambda e: e.tensor_tensor(out=t1[0:nrows, :], in0=raw[0:nrows, :], in1=cosT[0:nrows, cs], op=ALU.mult), reads=[braw, A["b_rope"]], writes=[bt1])
    P.op("dve", lambda e: e.tensor_tensor(out=raw[0:nrows, :], in0=ps2[0:nrows, :], in1=sinT[0:nrows, cs], op=ALU.mult), reads=[pb2, A["b_rope"]], writes=[braw])
    P.op("dve", lambda e: e.tensor_tensor(out=dst, in0=t1[0:nrows, :], in1=raw[0:nrows, :], op=ALU.add), reads=[bt1, braw], writes=[dst_buf])


def emit_odd(k, l, bi):
    P, dr = k.P, k.dr
    j = l // 2
    sample = (bi == 1)
    NK = 1280 if sample else 1024
    nkt = NK // 128
    with P.phase():
        A = dict(rr=0, rrr=0)
        sinkneg = P.sb("sinkneg", [128, 8]); A["b_sink"] = P.buf()
        bcast_rows(k, sinkneg[:], j * 8, dr["sink"].tensor, 8, A["b_sink"])
        P.op("dve", lambda e: e.tensor_scalar(out=sinkneg[:], in0=sinkneg[:], scalar1=-1.0, scalar2=None, op0=ALU.mult), reads=[A["b_sink"]], writes=[A["b_sink"]])
        qanw = P.sb("qanw", [128, 2, 1]); b_qanw = P.buf()
        load_rows_fm(k, [dr["q_a_norm"][j:j + 1, :]], 2, qanw, b_qanw)
        kvnw = P.sb("kvnw", [128, 128]); b_kvnw = P.buf()
        bcast_rows(k, kvnw[:], j * 128, dr["kv_a_norm"].tensor, 128, b_kvnw)
        wqb = P.sb("wqb", [128, 2, 768], BF16); b_wqb = P.buf()
        P.dma("pool", wqb[:], dr["w_q_b"][j].rearrange("(c p) f -> p c f", p=128), writes=[b_wqb])
        wkvK = P.sb("wkvK", [128, 8, 96], BF16); b_wkv = P.buf()
        wkvV = P.sb("wkvV", [128, 8, 64], BF16)
        P.op("pool", lambda e: e.memset(wkvK[:], 0.0), writes=[b_wkv])
        wkv3 = dr["w_kv_b"][j].rearrange("p (h f) -> p h f", h=8)
        P.dma("pool", wkvK[:, :, 0:64], wkv3[:, :, 0:64], writes=[b_wkv])
        P.dma("pool", wkvV[:], wkv3[:, :, 64:128], writes=[b_wkv])
        esel = P.sb("esel", [32, 96], BF16)
        nmb = P.sb("nmb", [128, 2, 128], BF16); A["b_nm"] = P.buf()
        P.dma("pool", esel[:], dr["esel"], writes=[b_wkv])
        P.dma("pool", nmb[:, 0, :], dr["nmF"], writes=[A["b_nm"]])
        P.dma("pool", nmb[:, 1, :], dr["nmB"], writes=[A["b_nm"]])
        if sample:
            ropeD = P.sb("ropeD", [96, 2, 1024], BF16); A["b_rope"] = P.buf()
            for i in range(2):
                P.dma("pool", ropeD[:, i, :], dr["rope_tab"][2 + i, 0:96, :], writes=[A["b_rope"]])
            perms = P.sb("perms", [128, 3, 128], BF16)
            for i in range(3):
                P.dma("pool", perms[:, i, :], dr["rope_perm"][i], writes=[A["b_rope"]])
            A["raw"] = [(P.sb("rraw%d" % i, [128, 512], BF16), P.buf()) for i in range(2)]
            A["t1"] = [(P.sb("rt1%d" % i, [128, 512]), P.buf()) for i in range(1)]
        qcT = P.sb("qcT", [128, 4, 1024], BF16); b_qcT = [P.buf() for _ in range(4)]
        kdup = [P.sb("kdup%d" % g, [128, NK], BF16) for g in range(2)]; b_kdup = [P.buf() for _ in range(2)]
        vc = P.sb("vc", [128, nkt, 128], BF16); b_vc = P.buf()
        qanT = P.sb("qanT", [128, 2, 1024], BF16); b_qanT = P.buf()
        ckvT = P.sb("ckvT", [128, NK], BF16); b_ckvT = P.buf()
        kpeT = P.sb("kpeT", [32, NK], BF16); b_kpeT = P.buf()
        oall = P.sb("oall", [128, 8, 1024], BF16); b_oall = [P.buf() for _ in range(8)]
        ssm = P.sb("ossm", [128, 8, 4]); b_ssm = [P.buf() for _ in range(8)]
        from contextlib import ExitStack as _ES
        inner = _ES()
        inner.enter_context(P.phase())
        st = [(P.sb("ost%d" % i, [128, 416]), P.buf()) for i in range(2)]
        stb = [(P.sb("ostb%d" % i, [128, 416], BF16), P.buf()) for i in range(2)]
        if sample:
            rope = P.sb("ropeCK", [128, 6, 1024], BF16)
            for i in (0, 1, 4, 5):
                P.dma("pool", rope[:, i, :], dr["rope_tab"][i], writes=[A["b_rope"]])
        ost = k.opts.get("odd_stage", 99)
        if ost <= 1:
            inner.close(); return
        wv, wb = load_w(k, "w_in_odd", j, 0, 512)
        for c in range(4):
            if sample:
                proj_fm_chunk(k, wv, wb, c, lambda g, ps, pb: rope_fm(k, A, ps, pb, 128, perms[:, 0, :], rope[:, 0, :], rope[:, 1, :], g, qcT[:, c, g * 512:(g + 1) * 512], b_qcT[c]))
            else:
                proj_fm_chunk(k, wv, wb, c, lambda g, ps, pb: P.op("act", lambda e: acopy(e, out=qcT[:, c, g * 512:(g + 1) * 512], in_=ps[:]), reads=[pb], writes=[b_qcT[c]]))
        if ost <= 2:
            inner.close(); return
        wt, wb = k.wnext()
        wdup = wt[:, 0:8 * 256].rearrange("p (c f) -> p c f", c=8)
        for g in range(2):
            for dup in range(2):
                P.dma("pool", wdup[:, :, (g * 2 + dup) * 64:(g * 2 + dup + 1) * 64],
                      dr["w_in_odd"][j, :, 512 + g * 64:512 + (g + 1) * 64].rearrange("(c p) f -> p c f", p=128), writes=[wb])
        for g in range(2):
            if sample:
                proj_fm_chunk(k, wdup, wb, g, lambda gg, ps, pb: rope_fm(k, A, ps, pb, 128, perms[:, 0, :], rope[:, 0, :], rope[:, 1, :], gg, kdup[g][:, gg * 512:(gg + 1) * 512], b_kdup[g]))
            else:
                proj_fm_chunk(k, wdup, wb, g, lambda gg, ps, pb: P.op("act", lambda e: acopy(e, out=kdup[g][:, gg * 512:(gg + 1) * 512], in_=ps[:]), reads=[pb], writes=[b_kdup[g]]))
        if ost <= 3:
            inner.close(); return
        wv, wb = load_w(k, "w_in_odd", j, 512, 256)

        def evac_kv(t, ps, pb):
            s_, bs_ = st[t % 2]
            P.op("dve", lambda e: e.tensor_copy(out=s_[:, 0:256], in_=ps[:, 0:256]), reads=[pb], writes=[bs_])
            P.op("pool", lambda e: e.tensor_copy(out=vc[:, t, :], in_=s_[:, 128:256]), reads=[bs_], writes=[b_vc])
            if not sample:
                rows = slice((t % 2) * 128, (t % 2) * 128 + 128)
                P.dma("sp", dr["o_k"][t // 2, j, rows].rearrange("t g d -> t (g d)"), s_[:, 0:128], reads=[bs_])
                P.dma("sp", dr["o_v"][t // 2, j, rows].rearrange("t g d -> t (g d)"), s_[:, 128:256], reads=[bs_])
        proj_tm(k, wv, wb, 0, 256, evac_kv)
        if ost <= 4:
            inner.close(); return
        wv, wb = load_w(k, "w_in_odd", j, 768, 416)

        def evac_lat(t, ps, pb):
            s_, bs_ = st[t % 2]
            sb_, bsb_ = stb[t % 2]
            sm = ssm[:, t, :]
            tok = slice(t * 128, (t + 1) * 128)
            rows = slice((t % 2) * 128, (t % 2) * 128 + 128)
            P.op("act", lambda e: acopy(e, out=s_[:], in_=ps[:, 0:416]), reads=[pb], writes=[bs_])
            P.op("act", lambda e: e.activation(out=sb_[:, 0:256], in_=s_[:, 0:256], func=AF.Square, accum_out=sm[:, 0:1]), reads=[bs_], writes=[bsb_, b_ssm[t]])
            P.op("act", lambda e: e.activation(out=sb_[:, 256:384], in_=s_[:, 256:384], func=AF.Square, accum_out=sm[:, 1:2]), reads=[bs_], writes=[bsb_, b_ssm[t]])
            P.op("dve", lambda e: e.tensor_scalar(out=sm[:, 0:1], in0=sm[:, 0:1], scalar1=1.0 / 256, scalar2=EPS, op0=ALU.mult, op1=ALU.add), reads=[b_ssm[t]], writes=[b_ssm[t]])
            P.op("dve", lambda e: e.tensor_scalar(out=sm[:, 1:2], in0=sm[:, 1:2], scalar1=1.0 / 128, scalar2=EPS, op0=ALU.mult, op1=ALU.add), reads=[b_ssm[t]], writes=[b_ssm[t]])
            P.op("act", lambda e: e.activation(out=sm[:, 0:2], in_=sm[:, 0:2], func=AF.Sqrt), reads=[b_ssm[t]], writes=[b_ssm[t]])
            P.op("dve", lambda e: e.reciprocal(out=sm[:, 2:4], in_=sm[:, 0:2]), reads=[b_ssm[t]], writes=[b_ssm[t]])
            P.op("dve", lambda e: e.tensor_scalar(out=sb_[:, 0:256], in0=s_[:, 0:256], scalar1=sm[:, 2:3], scalar2=None, op0=ALU.mult), reads=[bs_, b_ssm[t]], writes=[bsb_])
            P.op("dve", lambda e: e.scalar_tensor_tensor(out=s_[:, 256:384], in0=s_[:, 256:384], scalar=sm[:, 3:4], in1=kvnw[:], op0=ALU.mult, op1=ALU.mult),
                 reads=[bs_, b_ssm[t], b_kvnw], writes=[bs_])
            P.op("pool", lambda e: e.tensor_copy(out=sb_[:, 256:416], in_=s_[:, 256:416]), reads=[bs_], writes=[bsb_])
            if not sample:
                P.dma("sp", dr["o_ckv"][t // 2, j, rows], s_[:, 256:384], reads=[bs_])
                P.dma("sp", dr["o_kpe"][t // 2, j, rows], s_[:, 384:416], reads=[bs_])
            def tail():
                ps2, pb2 = k.nxt("tr")
                for c in range(3):
                    P.op("pe", lambda e: e.transpose(out=ps2[:, c * 128:(c + 1) * 128], in_=sb_[:, c * 128:(c + 1) * 128], identity=k.idb[:]), reads=[bsb_, k.b_idb], writes=[pb2], inc=False)
                P.op("pe", lambda e: e.transpose(out=ps2[0:32, 384:512], in_=sb_[:, 384:416], identity=k.idb[:]), reads=[bsb_, k.b_idb], writes=[pb2])
                for c in range(2):
                    P.op("dve", lambda e: e.tensor_scalar(out=qanT[:, c, tok], in0=ps2[:, c * 128:(c + 1) * 128], scalar1=qanw[:, c, 0:1], scalar2=None, op0=ALU.mult), reads=[pb2, b_qanw], writes=[b_qanT])
                P.op("dve", lambda e: e.tensor_copy(out=ckvT[:, tok], in_=ps2[:, 256:384]), reads=[pb2], writes=[b_ckvT])
                P.op("dve", lambda e: e.tensor_copy(out=kpeT[:, tok], in_=ps2[0:32, 384:512]), reads=[pb2], writes=[b_kpeT])
            return tail
        proj_tm(k, wv, wb, 0, 416, evac_lat)
        if sample:
            for g in range(2):
                raw, braw = A["raw"][g]
                t1, bt1 = A["t1"][0]
                cs = slice(g * 512, (g + 1) * 512)
                ps2, pb2 = k.nxt("mm")
                P.op("pe", lambda e: e.matmul(ps2[0:32, :], lhsT=perms[0:32, 2, 0:32], rhs=kpeT[:, cs], start=True, stop=True), reads=[b_kpeT, A["b_rope"]], writes=[pb2])
                P.op("pool", lambda e: e.tensor_tensor(out=t1[0:32, :], in0=kpeT[:, cs], in1=rope[0:32, 4, cs], op=ALU.mult), reads=[b_kpeT, A["b_rope"]], writes=[bt1])
                P.op("dve", lambda e: e.tensor_tensor(out=raw[0:32, :], in0=ps2[0:32, :], in1=rope[0:32, 5, cs], op=ALU.mult), reads=[pb2, A["b_rope"]], writes=[braw])
                P.op("dve", lambda e: e.tensor_tensor(out=kpeT[:, cs], in0=t1[0:32, :], in1=raw[0:32, :], op=ALU.add), reads=[bt1, braw], writes=[b_kpeT])
            s_, bs_ = st[0]
            sb_, bsb_ = stb[0]
            for tt in range(2):
                rows = slice(tt * 128, (tt + 1) * 128)
                for g in range(2):
                    for dup in range(2):
                        P.dma("sp", s_[:, (g * 2 + dup) * 64:(g * 2 + dup + 1) * 64], dr["c_k"][j, rows, g, :], writes=[bs_])
                P.dma("sp", s_[:, 256:384], dr["c_v"][j, rows].rearrange("t g d -> t (g d)"), writes=[bs_])
                P.op("dve", lambda e: e.tensor_copy(out=sb_[:, 0:256], in_=s_[:, 0:256]), reads=[bs_], writes=[bsb_])
                P.op("act", lambda e: acopy(e, out=vc[:, 8 + tt, :], in_=s_[:, 256:384]), reads=[bs_], writes=[b_vc])
                ps2, pb2 = k.nxt("tr")
                for g in range(2):
                    P.op("pe", lambda e: e.transpose(out=ps2[:, g * 128:(g + 1) * 128], in_=sb_[:, g * 128:(g + 1) * 128], identity=k.idb[:]), reads=[bsb_, k.b_idb], writes=[pb2], inc=(g == 1))
                for g in range(2):
                    P.op("dve", lambda e: e.tensor_copy(out=kdup[g][:, 1024 + tt * 128:1024 + (tt + 1) * 128], in_=ps2[:, g * 128:(g + 1) * 128]), reads=[pb2], writes=[b_kdup[g]])
                s2, bs2 = st[1]
                sb2, bsb2 = stb[1]
                P.dma("sp", s2[:, 0:128], dr["c_ckv"][j, rows], writes=[bs2])
                P.dma("sp", s2[:, 128:160], dr["c_kpe"][j, rows], writes=[bs2])
                P.op("dve", lambda e: e.tensor_copy(out=sb2[:, 0:160], in_=s2[:, 0:160]), reads=[bs2], writes=[bsb2])
                ps3, pb3 = k.nxt("tr")
                P.op("pe", lambda e: e.transpose(out=ps3[:, 0:128], in_=sb2[:, 0:128], identity=k.idb[:]), reads=[bsb2, k.b_idb], writes=[pb3], inc=False)
                P.op("pe", lambda e: e.transpose(out=ps3[0:32, 128:256], in_=sb2[:, 128:160], identity=k.idb[:]), reads=[bsb2, k.b_idb], writes=[pb3])
                P.op("dve", lambda e: e.tensor_copy(out=ckvT[:, 1024 + tt * 128:1024 + (tt + 1) * 128], in_=ps3[:, 0:128]), reads=[pb3], writes=[b_ckvT])
                P.op("dve", lambda e: e.tensor_copy(out=kpeT[:, 1024 + tt * 128:1024 + (tt + 1) * 128], in_=ps3[0:32, 128:256]), reads=[pb3], writes=[b_kpeT])
        inner.close()
        if ost <= 6:
            return
        A["Ssb"] = [(P.sb("Ssb%d" % i, [128, NK]), P.buf()) for i in range(2)]
        A["Pb"] = [(P.sb("Pb%d" % i, [128, NK], BF16), P.buf()) for i in range(2)]
        A["PT"] = [(P.sb("PTa%d" % i, [128, NK], BF16), P.buf()) for i in range(2)]
        A["sm"] = [(P.sb("asm%d" % i, [128, 8]), P.buf()) for i in range(2)]
        for qt in range(8):
            tok = slice(qt * 128, (qt + 1) * 128)
            for h in range(8):
                g, c, half = h // 4, h // 2, h % 2
                rows = slice(half * 64, half * 64 + 64)
                if sample:
                    k0, k1 = max(0, qt - 1), min(7, qt + 1)
                    masks = []
                    if qt - 1 >= 0:
                        masks.append((0, nmb[:, 0, :]))
                    if qt + 1 <= 7:
                        masks.append(((k1 - k0) * 128, nmb[:, 1, :]))
                    ksegs = [(kdup[g][rows, k0 * 128:(k1 + 1) * 128], (k1 - k0 + 1) * 128, masks), (kdup[g][rows, 1024:1280], 256, [])]
                    kts = list(range(k0, k1 + 1)) + [8, 9]
                else:
                    s0 = (qt // 2) * 2
                    ksegs = [(kdup[g][rows, s0 * 128:(s0 + 2) * 128], 256, [])]
                    kts = [s0, s0 + 1]
                vlist = [vc[:, kt, g * 64:(g + 1) * 64] for kt in kts]
                attn_unit(k, A, qcT[rows, c, tok], ksegs, vlist, 64.0 ** -0.5, sinkneg[:, h:h + 1], oall[:, qt, h * 64:(h + 1) * 64], b_oall[qt],
                          [b_qcT[c], b_kdup[g], b_vc])
        attn_flush(A)
        if ost <= 7:
            return
        QdT = [(P.sb("QdT%d" % i, [96, 1024], BF16), P.buf()) for i in range(2)]
        KTh = [(P.sb("KTh%d" % i, [96, NK], BF16), P.buf()) for i in range(2)]
        Vh = [(P.sb("Vh%d" % i, [128, nkt, 64], BF16), P.buf()) for i in range(2)]
        for h in range(8):
            qd, bqd = QdT[h % 2]
            kt_, bkt = KTh[h % 2]
            for g in range(2):
                ps, pb = k.nxt("mm")
                for kc in range(2):
                    P.op("pe", lambda e: e.matmul(ps[0:96, :], lhsT=wqb[:, kc, h * 96:(h + 1) * 96], rhs=qanT[:, kc, g * 512:(g + 1) * 512], start=(kc == 0), stop=(kc == 1)),
                         reads=[b_wqb, b_qanT], writes=[pb], inc=(kc == 1))
                if sample:
                    rope_fm(k, A, ps, pb, 96, perms[:, 1, :], ropeD[:, 0, :], ropeD[:, 1, :], g, qd[:, g * 512:(g + 1) * 512], bqd)
                else:
                    P.op("act", lambda e: acopy(e, out=qd[:, g * 512:(g + 1) * 512], in_=ps[0:96, :]), reads=[pb], writes=[bqd])
            for c0 in range(0, NK, 512):
                n = min(512, NK - c0)
                ps, pb = k.nxt("mm")
                P.op("pe", lambda e: e.matmul(ps[0:96, 0:n], lhsT=wkvK[:, h, :], rhs=ckvT[:, c0:c0 + n], start=True, stop=False), reads=[b_wkv, b_ckvT], writes=[pb], inc=False)
                P.op("pe", lambda e: e.matmul(ps[0:96, 0:n], lhsT=esel[:], rhs=kpeT[:, c0:c0 + n], start=False, stop=True), reads=[b_wkv, b_kpeT], writes=[pb])
                P.op("dve", lambda e: e.tensor_copy(out=kt_[:, c0:c0 + n], in_=ps[0:96, 0:n]), reads=[pb], writes=[bkt])
            vh, bvh = Vh[h % 2]
            for kt0 in range(0, nkt, 8):
                nk_ = min(8, nkt - kt0)
                ps, pb = k.nxt("mm")
                for kk in range(nk_):
                    P.op("pe", lambda e: e.matmul(ps[:, kk * 64:(kk + 1) * 64], lhsT=ckvT[:, (kt0 + kk) * 128:(kt0 + kk + 1) * 128], rhs=wkvV[:, h, :], start=True, stop=True),
                         reads=[b_ckvT, b_wkv], writes=[pb], inc=(kk == nk_ - 1))
                P.op("act", lambda e: acopy(e, out=vh[:, kt0:kt0 + nk_, :], in_=ps[:, 0:nk_ * 64].rearrange("p (a f) -> p a f", a=nk_)), reads=[pb], writes=[bvh])
            for qt in range(8):
                tok = slice(qt * 128, (qt + 1) * 128)
                if sample:
                    ksegs = [(kt_[:, 0:512], 512, []), (kt_[:, 512:1024], 512, []), (kt_[:, 1024:1280], 256, [])]
                    kts = list(range(10))
                else:
                    s0 = (qt // 2) * 2
                    ksegs = [(kt_[:, s0 * 128:(s0 + 2) * 128], 256, [])]
                    kts = [s0, s0 + 1]
                vlist = [vh[:, kt, :] for kt in kts]
                attn_unit(k, A, qd[:, tok], ksegs, vlist, 96.0 ** -0.5, None, oall[:, qt, 512 + h * 64:512 + (h + 1) * 64], b_oall[qt], [bqd, bkt, bvh])
        attn_flush(A)
        if ost <= 8:
            return
        for qt in range(8):
            ps, pb = k.nxt("tr")
            for c in range(8):
                P.op("pe", lambda e: e.transpose(out=ps[:, c * 128:(c + 1) * 128], in_=oall[:, qt, c * 128:(c + 1) * 128], identity=k.idb[:]), reads=[b_oall[qt], k.b_idb], writes=[pb], inc=(c == 7))
            P.op("dve", lambda e: e.tensor_copy(out=k.mixT[:, 0:4, qt * 128:(qt + 1) * 128], in_=ps[:, 0:512].rearrange("p (c t) -> p c t", c=4)), reads=[pb], writes=[k.b_mixT[qt]])
            P.op("dve", lambda e: e.tensor_copy(out=k.mixT[:, 4:8, qt * 128:(qt + 1) * 128], in_=ps[:, 512:1024].rearrange("p (c t) -> p c t", c=4)), reads=[pb], writes=[k.b_mixT[qt]])
```

```python
import numpy as np
import concourse.bass as bass
import concourse.mybir as mybir
from concourse.bass_utils import run_bass_kernel_spmd

F32 = mybir.dt.float32
BF16 = mybir.dt.bfloat16
AF = mybir.ActivationFunctionType
ALU = mybir.AluOpType
AX = mybir.AxisListType

D = 1024
DFF = 4096
DEPTH = 4
EPS = 1e-6
NCORES = 8


class Buf:
    __slots__ = ("name", "w", "r")

    def __init__(self, name):
        self.name = name
        self.w = None
        self.r = []


class Prog:
    NDMA = 6

    def __init__(self, nc, stack):
        self.nc = nc
        self.stack = stack
        self.eng = {"pe": nc.tensor, "act": nc.scalar, "dve": nc.vector,
                    "pool": nc.gpsimd, "sp": nc.sync}
        self.sem = {}
        self.cnt = {}
        for e in self.eng:
            self.sem[e] = stack.enter_context(nc.semaphore("s_" + e))
            self.cnt[e] = 0
        self.dq = {}
        for q in ("sp", "pool", "act"):
            sems = []
            for i in range(self.NDMA):
                k = "d_%s%d" % (q, i)
                self.sem[k] = stack.enter_context(nc.semaphore(k))
                self.cnt[k] = 0
                sems.append(k)
            self.dq[q] = [sems, 0]
        self.waited = {}
        self.pe_pending = []
        self.nbuf = 0
        self.ninstr = 0
        self.npe = 0
        self.marks = []

    def sb(self, name, shape, dt=F32):
        self.nbuf += 1
        name = "%s_s%d" % (name, self.nbuf)
        t = self.stack.enter_context(self.nc.sbuf_tensor(name, list(shape), dt))
        return t

    def ps(self, name, shape, dt=F32):
        t = self.stack.enter_context(self.nc.psum_tensor(name, list(shape), dt))
        return t

    def buf(self, name=None):
        self.nbuf += 1
        return Buf(name or ("b%d" % self.nbuf))

    def _wait(self, e, key, val):
        if val <= 0:
            return
        if key == e and e == "pe":
            return
        k = (e, key)
        if self.waited.get(k, 0) >= val:
            return
        self.waited[k] = val
        self.eng[e].wait_ge(self.sem[key], val)

    def _deps(self, e, reads, writes):
        for b in reads:
            if b.w is not None:
                self._wait(e, b.w[0], b.w[1])
        for b in writes:
            if b.w is not None:
                self._wait(e, b.w[0], b.w[1])
            for (k, v) in b.r:
                self._wait(e, k, v)

    def op(self, e, fn, reads=(), writes=(), inc=True):
        self._deps(e, reads, writes)
        ins = fn(self.eng[e])
        self.ninstr += 1
        if e == "pe":
            self.npe += 1
        if e == "pe" and not inc:
            for b in reads:
                self.pe_pending.append(("r", b))
            for b in writes:
                self.pe_pending.append(("w", b))
            return ins
        self.cnt[e] += 1
        ins.then_inc(self.sem[e], 1)
        me = (e, self.cnt[e])
        if e == "pe" and self.pe_pending:
            for kind, b in self.pe_pending:
                if kind == "r":
                    b.r.append(me)
                else:
                    b.w = me
                    b.r = []
            self.pe_pending = []
        for b in reads:
            b.r.append(me)
            if len(b.r) > 24:
                b.r = b.r[-24:] if False else self._compact(b.r)
        for b in writes:
            b.w = me
            b.r = []
        return ins

    @staticmethod
    def _compact(rl):
        best = {}
        for k, v in rl:
            if best.get(k, 0) < v:
                best[k] = v
        return list(best.items())

    def dma(self, q, out_ap, in_ap, reads=(), writes=(), **kw):
        sems, idx = self.dq[q]
        key = sems[idx % self.NDMA]
        self.dq[q][1] = idx + 1
        self._wait(q, key, self.cnt[key])
        self._deps(q, reads, writes)
        ins = self.eng[q].dma_start(out=out_ap, in_=in_ap, **kw)
        self.ninstr += 1
        self.cnt[key] += 16
        ins.then_inc(self.sem[key], 16)
        me = (key, self.cnt[key])
        for b in reads:
            b.r.append(me)
            if len(b.r) > 24:
                b.r = self._compact(b.r)
        for b in writes:
            b.w = me
            b.r = []
        return ins

    def mark(self, label):
        self.marks.append((label, self.npe))

    def barrier(self):
        for e in ("pe", "act", "dve", "pool", "sp"):
            for key in self.cnt:
                if key == e and e == "pe":
                    continue
                self._wait(e, key, self.cnt[key])

    def phase(self):
        from contextlib import contextmanager, ExitStack

        @contextmanager
        def cm():
            old = self.stack
            with ExitStack() as sub:
                self.stack = sub
                try:
                    yield
                finally:
                    assert not self.pe_pending
                    self.barrier()
                    self.stack = old
        return cm()

    def finish(self):
        for q in ("sp",):
            for key in self.cnt:
                self._wait(q, key, self.cnt[key])


class K:
    pass


def ap3(t, off, dims):
    return bass.AP(t, off, [list(d) for d in dims])


def build_program(opts=None):
    from contextlib import ExitStack
    opts = opts or {}
    nlayers = opts.get("nlayers", DEPTH)
    do_mixer = opts.get("mixer", True)
    nc = bass.Bass("TRN2", target_bir_lowering=False)
    dr = {}

    def din(name, shape, dt=F32):
        dr[name] = nc.dram_tensor(name, list(shape), dt, kind="ExternalInput").ap()
        return dr[name]

    def dout(name, shape, dt=F32):
        dr[name] = nc.dram_tensor(name, list(shape), dt, kind="ExternalOutput").ap()
        return dr[name]

    din("xp", [1024, D]); din("xs", [1024, D]); din("cond2", [2, D])
    din("w_ada", [DEPTH, D, 6 * D]); din("b_ada", [DEPTH, 6 * D]); din("norm_g", [DEPTH * 4, D])
    din("w_up", [DEPTH, D, DFF]); din("w_down", [DEPTH, DFF, D])
    din("ident", [128, 128]); din("maskF", [128, 128]); din("maskB", [128, 128]); din("nmF", [128, 128]); din("nmB", [128, 128])
    din("w_in_even", [2, D, 3616]); din("conv_a_w", [2, 5, 1024]); din("conv_a_b", [2, 1024]); din("conv_b_w", [2, 5, 1024]); din("conv_b_b", [2, 1024])
    din("gate_b", [2, 16]); din("a_norm_w", [2, 512]); din("dt_bias", [2, 16]); din("a_log", [2, 16]); din("d_skip", [2, 8]); din("b_norm_w", [2, 512])
    din("w_out_even", [2, D, D]); din("w_out_odd", [2, D, D])
    din("w_in_odd", [2, D, 1184]); din("sink", [2, 8]); din("q_a_norm", [2, 256]); din("kv_a_norm", [2, 128])
    din("w_q_b", [2, 256, 768]); din("w_kv_b", [2, 128, 1024]); din("esel", [32, 96]); din("rope_tab", [6, 128, 1024]); din("rope_perm", [3, 128, 128])
    din("c_k", [2, 256, 2, 64]); din("c_v", [2, 256, 2, 64]); din("c_ckv", [2, 256, 128]); din("c_kpe", [2, 256, 32])
    dout("o_k", [4, 2, 256, 2, 64]); dout("o_v", [4, 2, 256, 2, 64]); dout("o_ckv", [4, 2, 256, 128]); dout("o_kpe", [4, 2, 256, 32])
    din("st_C", [2, 2, 4, 128, 128]); din("st_n", [2, 2, 4, 128]); din("st_m", [2, 2, 4]); din("st_S", [2, 2, 8, 64, 128])
    dout("o_C", [4, 2, 2, 4, 128, 128]); dout("o_n", [4, 2, 2, 4, 128]); dout("o_m", [4, 2, 2, 4]); dout("o_S", [4, 2, 2, 8, 64, 128])
    dout("yp", [1024, D]); dout("ys", [1024, D])

    with ExitStack() as st:
        P = Prog(nc, st)
        k = K()
        k.P, k.nc, k.dr, k.opts = P, nc, dr, opts
        k.idf = P.sb("idf", [128, 128]); k.b_idf = P.buf("idf")
        k.idb = P.sb("idb", [128, 128], BF16); k.b_idb = P.buf("idb")
        k.onesf = P.sb("onesf", [128, 128]); k.b_ones = P.buf("ones")
        P.dma("sp", k.idf[:], dr["ident"], writes=[k.b_idf])
        P.op("dve", lambda e: e.tensor_copy(out=k.idb[:], in_=k.idf[:]), reads=[k.b_idf], writes=[k.b_idb])
        P.op("dve", lambda e: e.memset(k.onesf[:], 1.0), writes=[k.b_ones])
        k.ps_mm = [(P.ps("psmm%d" % i, [128, 512]), P.buf("psmm%d" % i)) for i in range(4)]
        k.ps_tr = [(P.ps("pstr%d" % i, [128, 1024], BF16), P.buf("pstr%d" % i)) for i in range(2)]
        k.ps_x = [(P.ps("psx%d" % i, [128, 512]), P.buf("psx%d" % i)) for i in range(2)]
        k.rr = {"mm": 0, "tr": 0, "x": 0, "w": 0}

        def nxt(pool):
            lst = {"mm": k.ps_mm, "tr": k.ps_tr, "x": k.ps_x}[pool]
            i = k.rr[pool]
            k.rr[pool] = i + 1
            return lst[i % len(lst)]
        k.nxt = nxt
        k.wring = [(P.sb("wr%d" % i, [128, 4096], BF16), P.buf("wr%d" % i)) for i in range(3)]

        def wnext():
            i = k.rr["w"]
            k.rr["w"] = i + 1
            return k.wring[i % len(k.wring)]
        k.wnext = wnext

        k.x = P.sb("x", [128, 8, D]); k.b_x = [P.buf("x%d" % i) for i in range(8)]
        k.hT = P.sb("hT", [128, 8, 1024], BF16); k.b_hT = [P.buf("hT%d" % i) for i in range(8)]
        k.mixT = P.sb("mixT", [128, 8, 1024], BF16); k.b_mixT = [P.buf("mixT%d" % i) for i in range(8)]

        init_persistent(k)
        P.mark("prologue")
        with P.phase():
            emit_prologue(k)
        for bi, (xin, xout) in enumerate((("xp", "yp"), ("xs", "ys"))):
            if bi in opts.get("skipbatch", ()):
                continue
            for t in range(8):
                P.dma("sp", k.x[:, t, :], dr[xin][t * 128:(t + 1) * 128, :], writes=[k.b_x[t]])
            for l in range(nlayers):
                P.mark("b%d l%d norm1" % (bi, l))
                emit_norm(k, l, bi, 0)
                if do_mixer:
                    P.mark("b%d l%d mixer" % (bi, l))
                    emit_mixer(k, l, bi)
                    P.mark("b%d l%d outproj" % (bi, l))
                    emit_outproj_resid(k, l, bi)
                P.mark("b%d l%d norm2" % (bi, l))
                emit_norm(k, l, bi, 1)
                P.mark("b%d l%d mlp" % (bi, l))
                with P.phase():
                    emit_mlp(k, l, bi)
            P.mark("b%d end" % bi)
            for t in range(8):
                P.dma("sp", dr[xout][t * 128:(t + 1) * 128, :], k.x[:, t, :], reads=[k.b_x[t]])
        P.finish()
        k.ninstr = P.ninstr
    return nc, k


def transpose_rows(k, dst_ap_fn, src_sb, rows, nchunk, src_buf, dst_buf):
    P = k.P
    for c0 in range(0, nchunk, 4):
        ps, pb = k.nxt("x")
        n = min(4, nchunk - c0)
        for c in range(n):
            P.op("pe", lambda e: e.transpose(out=ps[:, c * rows:(c + 1) * rows], in_=src_sb[0:rows, (c0 + c) * 128:(c0 + c + 1) * 128],
                                             identity=k.idf[0:rows, 0:rows]),
                 reads=[src_buf, k.b_idf], writes=[pb], inc=(c == n - 1))
        for c in range(n):
            P.op("dve", lambda e: e.tensor_copy(out=dst_ap_fn(c0 + c), in_=ps[:, c * rows:(c + 1) * rows]), reads=[pb], writes=[dst_buf])


def emit_prologue(k):
    P, dr = k.P, k.dr
    stg = P.sb("stg", [16, 8192])
    cond = stg[0:2, 0:1024]; b_cond = P.buf()
    scT = P.sb("scT", [128, 8, 2]); b_scT = P.buf()
    scTb = P.sb("scTb", [128, 8, 2], BF16); b_scTb = P.buf()
    ng = stg[0:16, 1024:2048]; b_ng = P.buf()
    bada = stg[0:4, 2048:8192]; b_bada = P.buf()
    P.dma("sp", cond, dr["cond2"], writes=[b_cond])
    P.dma("sp", ng, dr["norm_g"], writes=[b_ng])
    P.dma("sp", bada, dr["b_ada"], writes=[b_bada])
    P.op("act", lambda e: e.activation(out=cond, in_=cond, func=AF.Silu), reads=[b_cond], writes=[b_cond])
    transpose_rows(k, lambda c: scT[:, c, :], cond, 2, 8, b_cond, b_scT)
    P.op("dve", lambda e: e.tensor_copy(out=scTb[:], in_=scT[:]), reads=[b_scT], writes=[b_scTb])
    transpose_rows(k, lambda c: k.ngT[:, c, :], ng, 16, 8, b_ng, k.b_ngT)
    transpose_rows(k, lambda c: k.badaT[:, c, :], bada, 4, 48, b_bada, k.b_badaT)
    for l in range(DEPTH):
        ps, pb = k.nxt("x")
        for blk in range(12):
            wt, wb = k.wnext()
            wv = wt[:].rearrange("p (c f) -> p c f", c=8)
            P.dma("pool", wv, dr["w_ada"][l, :, blk * 512:(blk + 1) * 512].rearrange("(c p) f -> p c f", p=128), writes=[wb])
            for j in range(4):
                ch = blk * 4 + j
                for kc in range(8):
                    P.op("pe", lambda e: e.matmul(ps[:, ch * 2:ch * 2 + 2], lhsT=wv[:, kc, j * 128:(j + 1) * 128], rhs=scTb[:, kc, :],
                                                  start=(kc == 0), stop=(kc == 7)),
                         reads=[wb, b_scTb], writes=[pb], inc=(kc == 7))
        P.op("dve", lambda e: e.tensor_tensor(out=k.mod[:, l, :, :], in0=ps[:, 0:96].rearrange("p (c t) -> p c t", t=2),
                                              in1=ap3(k.badaT, l, [[48 * 4, 128], [4, 48], [0, 2]]), op=ALU.add),
             reads=[pb, k.b_badaT], writes=[k.b_mod[l]])


def emit_norm(k, l, bi, which):
    P = k.P
    n = k.nrm
    shc, scc = (0, 8) if which == 0 else (24, 32)
    gi = l * 4 + (0 if which == 0 else 2)
    P.op("dve", lambda e: e.scalar_tensor_tensor(out=n["A"][:], in0=k.mod[:, l, scc:scc + 8, bi], scalar=1.0, in1=k.ngT[:, :, gi],
                                                 op0=ALU.add, op1=ALU.mult),
         reads=[k.b_mod[l], k.b_ngT], writes=[n["bA"]])
    bss = n["bss"][0]
    for t in range(8):
        P.op("act", lambda e: e.activation(out=n["junk"][:], in_=k.x[:, t, :], func=AF.Square, accum_out=n["ss"][:, t:t + 1]),
             reads=[k.b_x[t]], writes=[n["bj"], bss])
    P.op("dve", lambda e: e.tensor_scalar(out=n["ss"][:], in0=n["ss"][:], scalar1=1.0 / D, scalar2=EPS, op0=ALU.mult, op1=ALU.add), reads=[bss], writes=[bss])
    P.op("act", lambda e: e.activation(out=n["ss"][:], in_=n["ss"][:], func=AF.Sqrt), reads=[bss], writes=[bss])
    P.op("dve", lambda e: e.reciprocal(out=n["ss"][:], in_=n["ss"][:]), reads=[bss], writes=[bss])
    for t in range(8):
        xn, bxn = n["xn"][t % 2], n["bxn"][t % 2]
        P.op("act", lambda e: e.activation(out=xn[:], in_=k.x[:, t, :], func=AF.Copy, scale=n["ss"][:, t:t + 1]), reads=[k.b_x[t], bss], writes=[bxn])
        ps, pb = k.nxt("tr")
        for c in range(8):
            P.op("pe", lambda e: e.transpose(out=ps[:, c * 128:(c + 1) * 128], in_=xn[:, c * 128:(c + 1) * 128], identity=k.idb[:]),
                 reads=[bxn, k.b_idb], writes=[pb], inc=(c == 7))
        for c in range(8):
            if t % 2 == 0:
                P.op("act", lambda e: e.activation(out=k.hT[:, c, t * 128:(t + 1) * 128], in_=ps[:, c * 128:(c + 1) * 128], func=AF.Identity,
                                                   scale=n["A"][:, c:c + 1], bias=k.mod[:, l, shc + c, bi:bi + 1]),
                     reads=[pb, n["bA"], k.b_mod[l]], writes=[k.b_hT[t]])
            else:
                P.op("dve", lambda e: e.tensor_scalar(out=k.hT[:, c, t * 128:(t + 1) * 128], in0=ps[:, c * 128:(c + 1) * 128],
                                                      scalar1=n["A"][:, c:c + 1], scalar2=k.mod[:, l, shc + c, bi:bi + 1],
                                                      op0=ALU.mult, op1=ALU.add),
                     reads=[pb, n["bA"], k.b_mod[l]], writes=[k.b_hT[t]])


def row_bcast(k, vec_ap_fn, reads, dst, dst_buf):
    P = k.P
    for c in range(8):
        P.op("dve", lambda e: e.tensor_scalar(out=k.rb_diag[:, c * 128:(c + 1) * 128], in0=k.idf[:], scalar1=vec_ap_fn(c), scalar2=None, op0=ALU.mult),
             reads=list(reads) + [k.b_idf], writes=[k.b_rbdiag])
    for h in range(2):
        ps, pb = k.nxt("x")
        P.op("pe", lambda e: e.matmul(ps[:], lhsT=k.onesf[:], rhs=k.rb_diag[:, h * 512:(h + 1) * 512], start=True, stop=True),
             reads=[k.b_ones, k.b_rbdiag], writes=[pb])
        P.op("act", lambda e: acopy(e, out=dst[:, h * 512:(h + 1) * 512], in_=ps[:]), reads=[pb], writes=[dst_buf])


def emit_gate_vec(k, l, bi, which):
    P = k.P
    gc = 16 if which == 0 else 40
    gi = l * 4 + (1 if which == 0 else 3)
    P.op("dve", lambda e: e.tensor_tensor(out=k.ggv[:], in0=k.mod[:, l, gc:gc + 8, bi], in1=k.ngT[:, :, gi], op=ALU.mult),
         reads=[k.b_mod[l], k.b_ngT], writes=[k.b_ggv])
    row_bcast(k, lambda c: k.ggv[:, c:c + 1], [k.b_ggv], k.gg, k.b_gg)


def emit_resid_all(k, ys):
    P = k.P
    r = k.rs
    ss, bss = r["ss2"], r["bss"]
    for t in range(8):
        for h in range(2):
            P.op("act", lambda e: e.activation(out=r["junk"][:], in_=ys[t][0][h], func=AF.Square, accum_out=ss[:, t * 2 + h:t * 2 + h + 1]),
                 reads=[ys[t][1][h]], writes=[r["bj"], bss])
    P.op("dve", lambda e: e.tensor_reduce(out=ss[:, 16:24], in_=ss[:, 0:16].rearrange("p (t h) -> p t h", h=2), axis=AX.X, op=ALU.add), reads=[bss], writes=[bss])
    P.op("dve", lambda e: e.tensor_scalar(out=ss[:, 16:24], in0=ss[:, 16:24], scalar1=1.0 / D, scalar2=EPS, op0=ALU.mult, op1=ALU.add), reads=[bss], writes=[bss])
    P.op("act", lambda e: e.activation(out=ss[:, 16:24], in_=ss[:, 16:24], func=AF.Sqrt), reads=[bss], writes=[bss])
    P.op("dve", lambda e: e.reciprocal(out=ss[:, 24:32], in_=ss[:, 16:24]), reads=[bss], writes=[bss])
    for t in range(8):
        for h in range(2):
            P.op("dve", lambda e: e.scalar_tensor_tensor(out=r["tmp"][:, h * 512:(h + 1) * 512], in0=ys[t][0][h], scalar=ss[:, 24 + t:25 + t],
                                                         in1=k.gg[:, h * 512:(h + 1) * 512], op0=ALU.mult, op1=ALU.mult),
                 reads=[ys[t][1][h], bss, k.b_gg], writes=[r["btmp"]])
        P.op("pool", lambda e: e.tensor_tensor(out=k.x[:, t, :], in0=k.x[:, t, :], in1=r["tmp"][:], op=ALU.add),
             reads=[r["btmp"], k.b_x[t]], writes=[k.b_x[t]])


def emit_mlp(k, l, bi):
    P, dr = k.P, k.dr
    emit_gate_vec(k, l, bi, 1)
    k.hid = [(P.sb("hid%d" % i, [128, 4, 1024], BF16), P.buf("hid%d" % i)) for i in range(2)]
    yacc = P.sb("yacc", [128, 8, D])
    k.b_yacc = [P.buf("yacc%d" % i) for i in range(16)]
    for blk in range(8):
        wu, wub = k.wnext()
        wuv = wu[:].rearrange("p (c f) -> p c f", c=8)
        P.dma("pool", wuv, dr["w_up"][l, :, blk * 512:(blk + 1) * 512].rearrange("(c p) f -> p c f", p=128), writes=[wub])
        wd, wdb = k.wnext()
        wdv = wd[:].rearrange("p (c f) -> p c f", c=4)
        P.dma("pool", wdv, dr["w_down"][l, blk * 512:(blk + 1) * 512, :].rearrange("(c p) f -> p c f", p=128), writes=[wdb])
        hid, hb = k.hid[blk % 2]
        for g in range(2):
            for j in range(4):
                ps, pb = k.nxt("mm")
                for kc in range(8):
                    P.op("pe", lambda e: e.matmul(ps[:], lhsT=wuv[:, kc, j * 128:(j + 1) * 128], rhs=k.hT[:, kc, g * 512:(g + 1) * 512],
                                                  start=(kc == 0), stop=(kc == 7)),
                         reads=[wub] + k.b_hT[g * 4:(g + 1) * 4], writes=[pb], inc=(kc == 7))
                rl, rb = k.relu[(g * 4 + j) % 2]
                P.op("act", lambda e: e.activation(out=rl[:], in_=ps[:], func=AF.Relu), reads=[pb], writes=[rb])
                P.op("dve", lambda e: e.tensor_tensor(out=hid[:, j, g * 512:(g + 1) * 512], in0=rl[:], in1=rl[:], op=ALU.mult), reads=[rb], writes=[hb])
        for t in range(8):
            for h in range(2):
                ps, pb = k.nxt("mm")
                for j in range(4):
                    P.op("pe", lambda e: e.matmul(ps[:], lhsT=hid[:, j, t * 128:(t + 1) * 128], rhs=wdv[:, j, h * 512:(h + 1) * 512],
                                                  start=(j == 0), stop=(j == 3)),
                         reads=[hb, wdb], writes=[pb], inc=(j == 3))
                yb = k.b_yacc[t * 2 + h]
                if blk == 0:
                    P.op("act", lambda e: acopy(e, out=yacc[:, t, h * 512:(h + 1) * 512], in_=ps[:]), reads=[pb], writes=[yb])
                else:
                    P.op("dve", lambda e: e.tensor_tensor(out=yacc[:, t, h * 512:(h + 1) * 512], in0=ps[:], in1=yacc[:, t, h * 512:(h + 1) * 512], op=ALU.add),
                         reads=[pb, yb], writes=[yb])
    emit_resid_all(k, [([yacc[:, t, 0:512], yacc[:, t, 512:1024]], [k.b_yacc[t * 2], k.b_yacc[t * 2 + 1]]) for t in range(8)])


_CACHE = {}


def host_consts():
    i = np.arange(128)
    mF = (i[:, None] <= i[None, :]).astype(np.float32)
    mB = (i[:, None] >= i[None, :]).astype(np.float32)
    T, GW = 1024, 64

    def tabs(rot):
        q = rot // 4
        inv = (10000.0 ** (-np.arange(q, dtype=np.float32) / q)).astype(np.float32)
        r = np.repeat(np.arange(T // GW, dtype=np.float32), GW)
        cl = np.tile(np.arange(GW, dtype=np.float32), T // GW)
        ang = np.concatenate([r[:, None] * inv, cl[:, None] * inv], -1).astype(np.float32)
        return np.cos(ang).astype(np.float32), np.sin(ang).astype(np.float32)
    cc, sc = tabs(64)
    cd, sd = tabs(32)
    rt = np.zeros((6, 128, T), np.float32)
    p = np.arange(128)
    rt[0] = cc[:, p % 32].T
    rt[1] = sc[:, p % 32].T
    rt[2, :64] = 1.0
    rt[2, 64:96] = cd[:, np.arange(32) % 16].T
    rt[3, 64:96] = sd[:, np.arange(32) % 16].T
    rt[4, :32] = cd[:, np.arange(32) % 16].T
    rt[5, :32] = sd[:, np.arange(32) % 16].T
    pm = np.zeros((3, 128, 128), np.float32)
    for m in range(128):
        if m % 64 < 32:
            pm[0, m + 32, m] = -1.0
        else:
            pm[0, m - 32, m] = 1.0
    for m in range(64, 80):
        pm[1, m + 16, m] = -1.0
    for m in range(80, 96):
        pm[1, m - 16, m] = 1.0
    for m in range(16):
        pm[2, m + 16, m] = -1.0
    for m in range(16, 32):
        pm[2, m - 16, m] = 1.0
    es = np.zeros((32, 96), np.float32)
    es[np.arange(32), 64 + np.arange(32)] = 1.0
    return {"ident": np.eye(128, dtype=np.float32), "maskF": mF, "maskB": mB,
            "nmF": (mF - 1.0) * 30000.0, "nmB": (mB - 1.0) * 30000.0,
            "rope_tab": rt, "rope_perm": pm, "esel": es}


def make_in_maps(inp):
    f = lambda a: np.ascontiguousarray(np.asarray(a, dtype=np.float32))
    consts = host_consts()
    in_maps = []
    for c in range(NCORES):
        b = c // 4
        m = dict(consts)
        m["xp"] = f(inp["x_prompt"][4 * c:4 * c + 4]).reshape(1024, D)
        m["xs"] = f(inp["x_sample"][b]).reshape(1024, D)
        m["cond2"] = f(np.stack([np.asarray(inp["c_ctx"]), np.asarray(inp["c"])[b]], 0))
        m["w_ada"] = f(inp["w_ada"]); m["b_ada"] = f(inp["b_ada"])
        m["norm_g"] = f(inp["norm_g"]).reshape(DEPTH * 4, D)
        m["w_up"] = f(inp["w_up"]); m["w_down"] = f(inp["w_down"])
        for nm in ("w_in_even", "conv_a_w", "conv_a_b", "conv_b_w", "conv_b_b", "gate_b", "a_norm_w", "d_skip", "b_norm_w", "w_out_even", "w_out_odd"):
            m[nm] = f(inp[nm])
        m["dt_bias"] = f(inp["dt_bias"]).reshape(2, 16); m["a_log"] = f(inp["a_log"]).reshape(2, 16)
        m["st_C"] = f(inp["state_mlstm_C"][b]); m["st_n"] = f(inp["state_mlstm_n"][b]); m["st_m"] = f(inp["state_mlstm_m"][b])
        m["st_S"] = f(inp["state_ssd"][b])
        for nm in ("w_in_odd", "sink", "q_a_norm", "kv_a_norm", "w_q_b", "w_kv_b"):
            m[nm] = f(inp[nm])
        m["c_k"] = f(inp["cache_gqa_k"][b]); m["c_v"] = f(inp["cache_gqa_v"][b])
        m["c_ckv"] = f(inp["cache_mla_ckv"][b]); m["c_kpe"] = f(inp["cache_mla_kpe"][b])
        in_maps.append(m)
    return in_maps


def kernel(**inp):
    opts = inp.pop("_opts", None)
    key = repr(opts)
    if key not in _CACHE:
        _CACHE[key] = build_program(opts)
    nc, k = _CACHE[key]
    in_maps = make_in_maps(inp)
    if opts and "cores" in opts:
        in_maps = in_maps[:opts["cores"]]
        res = run_bass_kernel_spmd(nc, in_maps, core_ids=list(range(opts["cores"])))
        return [res.results[0][n] for n in ("yp", "ys", "o_C", "o_n", "o_m", "o_S", "o_k", "o_v", "o_ckv", "o_kpe")]
    res = run_bass_kernel_spmd(nc, in_maps, core_ids=list(range(NCORES)))
    R = res.results
    yp = np.concatenate([R[c]["yp"].reshape(4, 256, D) for c in range(NCORES)], 0)
    ys = np.stack([R[0]["ys"], R[4]["ys"]], 0)
    cat = lambda nm: np.concatenate([R[c][nm] for c in range(NCORES)], 0)
    return (yp, ys, cat("o_C"), cat("o_n"), cat("o_m"), cat("o_S"), cat("o_k"), cat("o_v"), cat("o_ckv"), cat("o_kpe"))


def acopy(e, out, in_):
    return e.activation(out=out, in_=in_, func=AF.Copy)


def V(t, p0, npart, off, dims):
    row = 1
    for d in t.shape[1:]:
        row *= d
    return bass.AP(t, p0 * row + off, [[row, npart]] + [list(d) for d in dims])


def init_persistent(k):
    P = k.P
    k.nrm = dict(
        junk=P.sb("njunk", [128, D], BF16), bj=P.buf(),
        ss=P.sb("nss", [128, 8]), bss=[P.buf() for _ in range(8)],
        xn=[P.sb("nxn%d" % i, [128, D], BF16) for i in range(2)], bxn=[P.buf() for _ in range(2)],
        A=P.sb("nA", [128, 8]), bA=P.buf(), )
    k.rs = dict(junk=P.sb("rjunk", [128, 512], BF16), bj=P.buf(), ss2=P.sb("rss", [128, 32]), bss=P.buf(),
                tmp=P.sb("rtmp", [128, D]), btmp=P.buf())
    k.gg = P.sb("gg", [128, D]); k.b_gg = P.buf()
    k.ggv = P.sb("ggv", [128, 8]); k.b_ggv = P.buf()
    k.rb_diag = P.sb("rbdiag", [128, D]); k.b_rbdiag = P.buf()
    k.relu = [(P.sb("relu%d" % i, [128, 512], BF16), P.buf()) for i in range(2)]
    k.mod = P.sb("mod", [128, DEPTH, 48, 2]); k.b_mod = [P.buf("mod%d" % l) for l in range(DEPTH)]
    k.ngT = P.sb("ngT", [128, 8, 16]); k.b_ngT = P.buf("ngT")
    k.badaT = P.sb("badaT", [128, 48, 4]); k.b_badaT = P.buf("badaT")
    k.stg = P.sb("stg2", [16, 1024]); k.b_stg = P.buf()
    k.onesb = P.sb("onesb", [128, 8], BF16); k.b_onesb = P.buf()
    P.op("dve", lambda e: e.memset(k.onesb[:], 1.0), writes=[k.b_onesb])
    k.maskF = P.sb("maskF", [128, 128]); k.maskB = P.sb("maskB", [128, 128]); k.b_mask = P.buf()
    k.nmF = P.sb("nmF", [128, 128]); k.nmB = P.sb("nmB", [128, 128])
    P.dma("sp", k.maskF[:], k.dr["maskF"], writes=[k.b_mask])
    P.dma("sp", k.maskB[:], k.dr["maskB"], writes=[k.b_mask])
    P.dma("sp", k.nmF[:], k.dr["nmF"], writes=[k.b_mask])
    P.dma("sp", k.nmB[:], k.dr["nmB"], writes=[k.b_mask])
    k.nmbf = P.sb("nmbf", [128, 2, 128], BF16); k.b_nmbf = P.buf()
    P.op("dve", lambda e: e.tensor_copy(out=k.nmbf[:, 0, :], in_=k.nmF[:]), reads=[k.b_mask], writes=[k.b_nmbf])
    P.op("dve", lambda e: e.tensor_copy(out=k.nmbf[:, 1, :], in_=k.nmB[:]), reads=[k.b_mask], writes=[k.b_nmbf])


def load_w(k, name, idx, col0, ncols, kch=8):
    P = k.P
    wt, wb = k.wnext()
    wv = wt[:, 0:kch * ncols].rearrange("p (c f) -> p c f", c=kch)
    P.dma("pool", wv, k.dr[name][idx, :, col0:col0 + ncols].rearrange("(c p) f -> p c f", p=128), writes=[wb])
    return wv, wb


def load_rows_fm(k, row_aps, nchunk, dst, dst_buf):
    P = k.P
    for r, a in enumerate(row_aps):
        P.dma("sp", k.stg[r:r + 1, 0:nchunk * 128], a, writes=[k.b_stg])
    transpose_rows(k, lambda c: dst[:, c, :], k.stg, len(row_aps), nchunk, k.b_stg, dst_buf)


def bcast_rows(k, dst, dram_ap_1d_off, tensor, n, dst_buf):
    src = bass.AP(tensor, dram_ap_1d_off, [[0, 128], [1, n]])
    k.P.dma("sp", dst, src, writes=[dst_buf])


def conv_silu(k, pre, b_pre, acc, b_acc, cw, b_cw, c, seqs, out_ap, out_buf, post_scale=None):
    P = k.P
    T = seqs[0][1]
    ns = len(seqs)
    P.op("dve", lambda e: e.tensor_scalar(out=acc[:], in0=pre[:], scalar1=cw[:, c, 2:3], scalar2=cw[:, c, 5:6], op0=ALU.mult, op1=ALU.add),
         reads=[b_pre, b_cw], writes=[b_acc])
    for jj, eng in ((0, "dve"), (1, "dve"), (3, "dve"), (4, "dve")):
        s = jj - 2
        a, b = max(0, -s), T - max(0, s)
        o = V(acc, 0, 128, a, [[T, ns], [1, b - a]])
        i = V(pre, 0, 128, a + s, [[T, ns], [1, b - a]])
        P.op(eng, lambda e: e.scalar_tensor_tensor(out=o, in0=i, scalar=cw[:, c, jj:jj + 1], in1=o, op0=ALU.mult, op1=ALU.add),
             reads=[b_pre, b_cw, b_acc], writes=[b_acc])
    if post_scale is None:
        P.op("act", lambda e: e.activation(out=out_ap, in_=acc[:], func=AF.Silu), reads=[b_acc], writes=[out_buf])
    else:
        P.op("act", lambda e: e.activation(out=acc[:], in_=acc[:], func=AF.Silu), reads=[b_acc], writes=[b_acc])
        P.op("dve", lambda e: e.tensor_scalar(out=out_ap, in0=acc[:], scalar1=post_scale, scalar2=None, op0=ALU.mult), reads=[b_acc], writes=[out_buf])


def proj_fm_chunk(k, wv, wb, j, evac):
    P = k.P
    for g in range(2):
        ps, pb = k.nxt("mm")
        for kc in range(8):
            P.op("pe", lambda e: e.matmul(ps[:], lhsT=wv[:, kc, j * 128:(j + 1) * 128], rhs=k.hT[:, kc, g * 512:(g + 1) * 512],
                                          start=(kc == 0), stop=(kc == 7)),
                 reads=[wb] + k.b_hT[g * 4:(g + 1) * 4], writes=[pb], inc=(kc == 7))
        evac(g, ps, pb)


def proj_tm(k, wv, wb, c0, ncols, evac):
    P = k.P
    pend = None
    for t in range(8):
        ps, pb = k.nxt("mm")
        for kc in range(8):
            P.op("pe", lambda e: e.matmul(ps[:, 0:ncols], lhsT=k.hT[:, kc, t * 128:(t + 1) * 128], rhs=wv[:, kc, c0:c0 + ncols],
                                          start=(kc == 0), stop=(kc == 7)),
                 reads=[wb, k.b_hT[t]], writes=[pb], inc=(kc == 7))
        if pend is not None:
            pend()
        pend = evac(t, ps, pb)
        if not callable(pend):
            pend = None
    if pend is not None:
        pend()


def log_sigmoid_inplace(k, x, bx, tmp, btmp, tmp2):
    P = k.P
    P.op("act", lambda e: e.activation(out=tmp, in_=x, func=AF.Abs), reads=[bx], writes=[btmp])
    P.op("act", lambda e: e.activation(out=tmp, in_=tmp, func=AF.Exp, scale=-1.0), reads=[btmp], writes=[btmp])
    P.op("act", lambda e: e.activation(out=tmp, in_=tmp, func=AF.Ln, bias=1.0), reads=[btmp], writes=[btmp])
    P.op("dve", lambda e: e.tensor_scalar_min(out=tmp2, in0=x, scalar1=0.0), reads=[bx], writes=[btmp])
    P.op("dve", lambda e: e.tensor_tensor(out=x, in0=tmp2, in1=tmp, op=ALU.subtract), reads=[btmp], writes=[bx])


def softplus_inplace(k, x, bx, tmp, btmp, tmp2):
    P = k.P
    P.op("act", lambda e: e.activation(out=tmp, in_=x, func=AF.Abs), reads=[bx], writes=[btmp])
    P.op("act", lambda e: e.activation(out=tmp, in_=tmp, func=AF.Exp, scale=-1.0), reads=[btmp], writes=[btmp])
    P.op("act", lambda e: e.activation(out=tmp, in_=tmp, func=AF.Ln, bias=1.0), reads=[btmp], writes=[btmp])
    P.op("dve", lambda e: e.tensor_scalar_max(out=tmp2, in0=x, scalar1=0.0), reads=[bx], writes=[btmp])
    P.op("dve", lambda e: e.tensor_tensor(out=x, in0=tmp2, in1=tmp, op=ALU.add), reads=[btmp], writes=[bx])


def group_norm_to_mixT(k, src, b_src, t, ngroups, gsize, nw_fm, b_nw, chunk0, scr):
    P = k.P
    sq, ss, hn, bs = scr["sq"], scr["ss"], scr["hn"], scr["b"]
    width = ngroups * gsize
    P.op("pool", lambda e: e.tensor_tensor(out=sq[:, 0:width], in0=src, in1=src, op=ALU.mult), reads=[b_src], writes=[bs])
    P.op("dve", lambda e: e.tensor_reduce(out=ss[:, 0:ngroups], in_=sq[:, 0:width].rearrange("p (g f) -> p g f", g=ngroups), axis=AX.X, op=ALU.add),
         reads=[bs], writes=[bs])
    P.op("dve", lambda e: e.tensor_scalar(out=ss[:, 0:ngroups], in0=ss[:, 0:ngroups], scalar1=1.0 / gsize, scalar2=EPS, op0=ALU.mult, op1=ALU.add), reads=[bs], writes=[bs])
    P.op("act", lambda e: e.activation(out=ss[:, 0:ngroups], in_=ss[:, 0:ngroups], func=AF.Sqrt), reads=[bs], writes=[bs])
    P.op("dve", lambda e: e.reciprocal(out=ss[:, 0:ngroups], in_=ss[:, 0:ngroups]), reads=[bs], writes=[bs])
    P.op("dve", lambda e: e.tensor_tensor(out=hn[:, 0:width].rearrange("p (g f) -> p g f", g=ngroups), in0=src.rearrange("p (g f) -> p g f", g=ngroups),
                                          in1=V(ss, 0, 128, 0, [[1, ngroups], [0, gsize]]), op=ALU.mult), reads=[b_src, bs], writes=[bs])
    nch = width // 128

    def tail():
        ps, pb = k.nxt("tr")
        for c in range(nch):
            P.op("pe", lambda e: e.transpose(out=ps[:, c * 128:(c + 1) * 128], in_=hn[:, c * 128:(c + 1) * 128], identity=k.idb[:]),
                 reads=[bs, k.b_idb], writes=[pb], inc=(c == nch - 1))
        for c in range(nch):
            P.op("act", lambda e: e.activation(out=k.mixT[:, chunk0 + c, t * 128:(t + 1) * 128], in_=ps[:, c * 128:(c + 1) * 128], func=AF.Copy, scale=nw_fm[:, c:c + 1]),
                 reads=[pb, b_nw], writes=[k.b_mixT[t]])
    return tail


def emit_even(k, l, bi):
    P, dr = k.P, k.dr
    j = l // 2
    seqs = [(s * 256, 256) for s in range(4)] if bi == 0 else [(0, 1024)]
    skip = k.opts.get("skip", ())
    if "mlstm" in skip:
        for t in range(8):
            P.op("pool", lambda e: e.memset(k.mixT[:, 0:4, t * 128:(t + 1) * 128], 0.0), writes=[k.b_mixT[t]])
    else:
        emit_mlstm(k, l, j, bi, seqs)
    if "ssd" in skip:
        for t in range(8):
            P.op("pool", lambda e: e.memset(k.mixT[:, 4:8, t * 128:(t + 1) * 128], 0.0), writes=[k.b_mixT[t]])
    else:
        emit_ssd(k, l, j, bi, seqs)


def emit_mlstm(k, l, j, bi, seqs):
    P, dr = k.P, k.dr
    with P.phase():
        cw = P.sb("cwA", [128, 8, 6]); b_cw = P.buf()
        load_rows_fm(k, [dr["conv_a_w"][j, r:r + 1, :] for r in range(5)] + [dr["conv_a_b"][j:j + 1, :]], 8, cw, b_cw)
        anw = P.sb("anw", [128, 4, 1]); b_anw = P.buf()
        load_rows_fm(k, [dr["a_norm_w"][j:j + 1, :]], 4, anw, b_anw)
        gb = P.sb("gb", [128, 16]); b_gb = P.buf()
        bcast_rows(k, gb[:], j * 16, dr["gate_b"].tensor, 16, b_gb)
        qkT = P.sb("qkT", [128, 8, 1024], BF16); b_qkT = [P.buf() for _ in range(8)]
        ktm = P.sb("ktm", [128, 8, 512], BF16); b_ktm = [P.buf() for _ in range(8)]
        vtm = P.sb("vtm", [128, 8, 512], BF16); b_vtm = [P.buf() for _ in range(8)]
        gt = P.sb("gates", [128, 8, 16]); b_gt = P.buf()
        bb = P.sb("bb", [128, 8, 8]); b_bb = P.buf()
        aa = P.sb("aa", [128, 8, 8]); b_aa = P.buf()
        gtmp = P.sb("gtmp", [128, 2, 64]); b_gtmp = P.buf()
        with P.phase():
            pre = [(P.sb("pre%d" % i, [128, 1024]), P.buf()) for i in range(2)]
            acc = [(P.sb("acc%d" % i, [128, 1024]), P.buf()) for i in range(2)]
            for blk in range(2):
                wv, wb = load_w(k, "w_in_even", j, blk * 512, 512)
                for jj in range(4):
                    c = blk * 4 + jj
                    pr, bpr = pre[c % 2]
                    ac, bac = acc[c % 2]
                    proj_fm_chunk(k, wv, wb, jj, lambda g, ps, pb: P.op("act", lambda e: acopy(e, out=pr[:, g * 512:(g + 1) * 512], in_=ps[:]), reads=[pb], writes=[bpr]))
                    conv_silu(k, pr, bpr, ac, bac, cw, b_cw, c, seqs, qkT[:, c, :], b_qkT[c], post_scale=(128.0 ** -0.5 if c >= 4 else None))
            for t in range(8):
                ps, pb = k.nxt("tr")
                for h in range(4):
                    P.op("pe", lambda e: e.transpose(out=ps[:, h * 128:(h + 1) * 128], in_=qkT[:, 4 + h, t * 128:(t + 1) * 128], identity=k.idb[:]),
                         reads=[b_qkT[4 + h], k.b_idb], writes=[pb], inc=(h == 3))
                P.op("dve", lambda e: e.tensor_copy(out=ktm[:, t, :], in_=ps[:, 0:512]), reads=[pb], writes=[b_ktm[t]])
            wv, wb = load_w(k, "w_in_even", j, 1024, 512)
            proj_tm(k, wv, wb, 0, 512, lambda t, ps, pb: P.op("act", lambda e: acopy(e, out=vtm[:, t, :], in_=ps[:]), reads=[pb], writes=[b_vtm[t]]))
            wv, wb = load_w(k, "w_in_even", j, 2048, 16)
            proj_tm(k, wv, wb, 0, 16, lambda t, ps, pb: P.op("dve", lambda e: e.tensor_tensor(out=gt[:, t, :], in0=ps[:, 0:16], in1=gb[:], op=ALU.add),
                                                              reads=[pb, b_gb], writes=[b_gt]))
            lfv = gt[:, :, 8:16]
            log_sigmoid_inplace(k, lfv, b_gt, gtmp[:, 0, :].rearrange("p (t f) -> p t f", t=8), b_gtmp, gtmp[:, 1, :].rearrange("p (t f) -> p t f", t=8))
            ps, pb = k.nxt("x")
            for t in range(8):
                P.op("pe", lambda e: e.matmul(ps[:, t * 8:t * 8 + 4], lhsT=k.maskF[:], rhs=gt[:, t, 8:12], start=True, stop=True), reads=[k.b_mask, b_gt], writes=[pb], inc=False)
                P.op("pe", lambda e: e.matmul(ps[:, t * 8 + 4:t * 8 + 8], lhsT=k.maskB[:], rhs=gt[:, t, 12:16], start=True, stop=True), reads=[k.b_mask, b_gt], writes=[pb], inc=(t == 7))
            P.op("dve", lambda e: e.tensor_copy(out=bb[:], in_=ps[:, 0:64].rearrange("p (t f) -> p t f", t=8)), reads=[pb], writes=[b_bb])
            P.op("dve", lambda e: e.tensor_tensor(out=aa[:], in0=gt[:, :, 0:8], in1=bb[:], op=ALU.subtract), reads=[b_gt, b_bb], writes=[b_aa])
        with P.phase():
            hacc = P.sb("hacc", [128, 8, 512]); b_hacc = [P.buf() for _ in range(8)]
            seen = set()
            ch = []
            for d in range(2):
                ch.append(dict(C=P.sb("C%d" % d, [128, 4, 128]), Cb=P.sb("Cb%d" % d, [128, 4, 128], BF16), n=P.sb("n%d" % d, [128, 4]),
                               nb=P.sb("nb%d" % d, [128, 4], BF16), m=P.sb("m%d" % d, [128, 4]), bC=P.buf(), bCb=P.buf(), bn=P.buf(), bm=P.buf(),
                               sm=P.sb("sm%d" % d, [128, 8, 4]), bsm=P.buf(),
                               dg=P.sb("dg%d" % d, [128, 512]), bdg=P.buf(), eam=P.sb("eam%d" % d, [128, 512]), beam=P.buf(),
                               PT=P.sb("PT%d" % d, [128, 4, 128], BF16), bPT=P.buf(), kea=P.sb("kea%d" % d, [128, 4, 128], BF16), bkea=P.buf(),
                               tmp=P.sb("htmp%d" % d, [128, 512]), btmp=P.buf()))
            for si, (t0, T) in enumerate(seqs):
                nt = T // 128
                tb = t0 // 128
                for d in range(2):
                    c = ch[d]
                    if bi == 0:
                        P.op("pool", lambda e: e.memset(c["C"][:], 0.0), writes=[c["bC"]])
                        P.op("pool", lambda e: e.memset(c["n"][:], 0.0), writes=[c["bn"]])
                        P.op("pool", lambda e: e.memset(c["m"][:], 0.0), writes=[c["bm"]])
                    else:
                        P.dma("sp", c["C"][:], dr["st_C"][j, d].rearrange("h d v -> d h v"), writes=[c["bC"]])
                        P.dma("sp", c["n"][:], dr["st_n"][j, d].rearrange("h d -> d h"), writes=[c["bn"]], allow_slow_non_contiguous=True)
                        bcast_rows(k, c["m"][:], (j * 2 + d) * 4, dr["st_m"].tensor, 4, c["bm"])
                for i in range(nt):
                    gens = []
                    for d in range(2):
                        t = tb + (i if d == 0 else nt - 1 - i)
                        first = t not in seen
                        seen.add(t)
                        g_ = mlstm_step(k, ch[d], d, first, t, qkT, b_qkT, ktm, b_ktm, vtm, b_vtm, gt, b_gt, bb, b_bb, aa, b_aa, hacc, b_hacc)
                        next(g_)
                        gens.append(g_)
                    for g_ in gens:
                        next(g_, None)
                if bi == 0:
                    for d in range(2):
                        c = ch[d]
                        P.dma("sp", dr["o_C"][si, j, d].rearrange("h d v -> d h v"), c["C"][:], reads=[c["bC"]])
                        P.dma("sp", dr["o_n"][si, j, d].rearrange("h d -> d h"), c["n"][:], reads=[c["bn"]], allow_slow_non_contiguous=True)
                        P.dma("sp", dr["o_m"][si, j, d:d + 1, :], c["m"][0:1, :], reads=[c["bm"]])
            otms = [(P.sb("otm%d" % i, [128, 512], BF16), P.buf()) for i in range(2)]
            scrs = [dict(sq=P.sb("fsq%d" % i, [128, 512]), ss=P.sb("fss%d" % i, [128, 4]), hn=P.sb("fhn%d" % i, [128, 512], BF16), b=P.buf()) for i in range(2)]
            wv, wb = load_w(k, "w_in_even", j, 1536, 512)

            def fin(t, ps, pb):
                otm, b_otm = otms[t % 2]
                P.op("act", lambda e: e.activation(out=otm[:], in_=ps[:], func=AF.Sigmoid), reads=[pb], writes=[b_otm])
                P.op("dve", lambda e: e.tensor_tensor(out=hacc[:, t, :], in0=hacc[:, t, :], in1=otm[:], op=ALU.mult), reads=[b_otm, b_hacc[t]], writes=[b_hacc[t]])
                return group_norm_to_mixT(k, hacc[:, t, :], b_hacc[t], t, 4, 128, anw[:, :, 0], b_anw, 0, scrs[t % 2])
            proj_tm(k, wv, wb, 0, 512, fin)


def mlstm_step(k, c, d, first, t, qkT, b_qkT, ktm, b_ktm, vtm, b_vtm, gt, b_gt, bb, b_bb, aa, b_aa, hacc, b_hacc):
    P = k.P
    mask = k.maskF if d == 0 else k.maskB
    a_ap = aa[:, t, d * 4:d * 4 + 4]
    b_ap = bb[:, t, d * 4:d * 4 + 4]
    lf_ap = gt[:, t, 8 + d * 4:12 + d * 4]
    sm = c["sm"]
    bsm = c["bsm"]
    tok = slice(t * 128, (t + 1) * 128)
    P.op("dve", lambda e: e.tensor_tensor(out=c["dg"][:].rearrange("p (h s) -> p h s", h=4), in0=V(k.idf, 0, 128, 0, [[0, 4], [1, 128]]),
                                          in1=V(aa, 0, 128, t * 8 + d * 4, [[1, 4], [0, 128]]), op=ALU.mult),
         reads=[k.b_idf, b_aa], writes=[c["bdg"]])
    ps_a, pb_a = k.nxt("mm")
    P.op("pe", lambda e: e.matmul(ps_a[:], lhsT=k.onesf[:], rhs=c["dg"][:], start=True, stop=True), reads=[k.b_ones, c["bdg"]], writes=[pb_a])
    ps_b, pb_b = k.nxt("x")
    P.op("pe", lambda e: e.matmul(ps_b[:, 0:4], lhsT=k.onesf[:], rhs=lf_ap, start=True, stop=True), reads=[k.b_ones, b_gt], writes=[pb_b])
    P.op("dve", lambda e: e.tensor_reduce(out=sm[:, 0, :], in_=ps_a[:].rearrange("p (h s) -> p h s", h=4), axis=AX.X, op=ALU.max), reads=[pb_a], writes=[bsm])
    P.op("dve", lambda e: e.tensor_tensor(out=sm[:, 1, :], in0=sm[:, 0, :], in1=c["m"][:], op=ALU.max), reads=[bsm, c["bm"]], writes=[bsm])
    P.op("dve", lambda e: e.tensor_tensor(out=sm[:, 2, :], in0=c["m"][:], in1=sm[:, 1, :], op=ALU.subtract), reads=[bsm, c["bm"]], writes=[bsm])
    P.op("act", lambda e: e.activation(out=sm[:, 2, :], in_=sm[:, 2, :], func=AF.Exp), reads=[bsm], writes=[bsm])
    P.op("dve", lambda e: e.tensor_tensor(out=c["m"][:], in0=ps_b[:, 0:4], in1=sm[:, 1, :], op=ALU.add), reads=[pb_b, bsm], writes=[c["bm"]])
    P.op("dve", lambda e: e.tensor_tensor(out=sm[:, 3, :], in0=a_ap, in1=sm[:, 1, :], op=ALU.subtract), reads=[b_aa, bsm], writes=[bsm])
    P.op("act", lambda e: e.activation(out=sm[:, 3, :], in_=sm[:, 3, :], func=AF.Exp), reads=[bsm], writes=[bsm])
    P.op("dve", lambda e: e.tensor_tensor(out=sm[:, 4, :], in0=b_ap, in1=sm[:, 1, :], op=ALU.add), reads=[b_bb, bsm], writes=[bsm])
    P.op("act", lambda e: e.activation(out=sm[:, 4, :], in_=sm[:, 4, :], func=AF.Exp, scale=-1.0), reads=[bsm], writes=[bsm])
    P.op("dve", lambda e: e.tensor_tensor(out=c["C"][:], in0=c["C"][:], in1=V(sm, 0, 128, 8, [[1, 4], [0, 128]]), op=ALU.mult), reads=[bsm, c["bC"]], writes=[c["bC"]])
    P.op("dve", lambda e: e.tensor_tensor(out=c["n"][:], in0=c["n"][:], in1=sm[:, 2, :], op=ALU.mult), reads=[bsm, c["bn"]], writes=[c["bn"]])
    P.op("act", lambda e: acopy(e, out=c["Cb"][:], in_=c["C"][:]), reads=[c["bC"]], writes=[c["bCb"]])
    P.op("act", lambda e: acopy(e, out=c["nb"][:], in_=c["n"][:]), reads=[c["bn"]], writes=[c["bCb"]])
    ps_s, pb_s = k.nxt("mm")
    for h in range(4):
        P.op("pe", lambda e: e.matmul(ps_s[:, h * 128:(h + 1) * 128], lhsT=qkT[:, 4 + h, tok], rhs=qkT[:, h, tok], start=True, stop=True),
             reads=[b_qkT[4 + h], b_qkT[h]], writes=[pb_s], inc=(h == 3))
    P.op("pool", lambda e: e.tensor_tensor(out=c["eam"][:].rearrange("p (h s) -> p h s", h=4), in0=V(mask, 0, 128, 0, [[0, 4], [1, 128]]),
                                           in1=V(sm, 0, 128, 12, [[1, 4], [0, 128]]), op=ALU.mult), reads=[k.b_mask, bsm], writes=[c["beam"]])
    P.op("dve", lambda e: e.tensor_tensor(out=c["PT"][:].rearrange("p h s -> p (h s)"), in0=ps_s[:], in1=c["eam"][:], op=ALU.mult), reads=[pb_s, c["beam"]], writes=[c["bPT"]])
    ps_n, pb_n = k.nxt("mm")
    for h in range(4):
        P.op("pe", lambda e: e.matmul(ps_n[:, h * 128:(h + 1) * 128], lhsT=c["PT"][:, h, :], rhs=vtm[:, t, h * 128:(h + 1) * 128], start=True, stop=False),
             reads=[c["bPT"], b_vtm[t]], writes=[pb_n], inc=False)
        P.op("pe", lambda e: e.matmul(ps_n[:, h * 128:(h + 1) * 128], lhsT=qkT[:, h, tok], rhs=c["Cb"][:, h, :], start=False, stop=True),
             reads=[b_qkT[h], c["bCb"]], writes=[pb_n], inc=False)
    for h in range(4):
        P.op("pe", lambda e: e.matmul(ps_b[:, 8 + h:9 + h], lhsT=c["PT"][:, h, :], rhs=k.onesb[:, 0:1], start=True, stop=False),
             reads=[c["bPT"], k.b_onesb], writes=[pb_b], inc=False)
        P.op("pe", lambda e: e.matmul(ps_b[:, 8 + h:9 + h], lhsT=qkT[:, h, tok], rhs=c["nb"][:, h:h + 1], start=False, stop=True),
             reads=[b_qkT[h], c["bCb"]], writes=[pb_b], inc=(h == 3))
    yield
    P.op("act", lambda e: e.activation(out=sm[:, 5, :], in_=ps_b[:, 8:12], func=AF.Abs), reads=[pb_b], writes=[bsm])
    P.op("dve", lambda e: e.tensor_tensor(out=sm[:, 5, :], in0=sm[:, 5, :], in1=sm[:, 4, :], op=ALU.max), reads=[bsm], writes=[bsm])
    P.op("dve", lambda e: e.reciprocal(out=sm[:, 6, :], in_=sm[:, 5, :]), reads=[bsm], writes=[bsm])
    rd_bc = V(sm, 0, 128, 24, [[1, 4], [0, 128]])
    if first:
        P.op("dve", lambda e: e.tensor_tensor(out=hacc[:, t, :].rearrange("p (h s) -> p h s", h=4), in0=ps_n[:].rearrange("p (h s) -> p h s", h=4), in1=rd_bc, op=ALU.mult),
             reads=[pb_n, bsm], writes=[b_hacc[t]])
    else:
        P.op("dve", lambda e: e.tensor_tensor(out=c["tmp"][:].rearrange("p (h s) -> p h s", h=4), in0=ps_n[:].rearrange("p (h s) -> p h s", h=4), in1=rd_bc, op=ALU.mult),
             reads=[pb_n, bsm], writes=[c["btmp"]])
        P.op("pool", lambda e: e.tensor_tensor(out=hacc[:, t, :], in0=hacc[:, t, :], in1=c["tmp"][:], op=ALU.add), reads=[c["btmp"], b_hacc[t]], writes=[b_hacc[t]])
    P.op("pool", lambda e: e.tensor_tensor(out=c["kea"][:], in0=ktm[:, t, :].rearrange("p (h s) -> p h s", h=4), in1=V(sm, 0, 128, 12, [[1, 4], [0, 128]]), op=ALU.mult),
         reads=[b_ktm[t], bsm], writes=[c["bkea"]])
    ps_c, pb_c = k.nxt("mm")
    for h in range(4):
        P.op("pe", lambda e: e.matmul(ps_c[:, h * 128:(h + 1) * 128], lhsT=c["kea"][:, h, :], rhs=vtm[:, t, h * 128:(h + 1) * 128], start=True, stop=True),
             reads=[c["bkea"], b_vtm[t]], writes=[pb_c], inc=False)
    for h in range(4):
        P.op("pe", lambda e: e.matmul(ps_b[:, 16 + h:17 + h], lhsT=c["kea"][:, h, :], rhs=k.onesb[:, 0:1], start=True, stop=True),
             reads=[c["bkea"], k.b_onesb], writes=[pb_b, pb_c], inc=(h == 3))
    P.op("dve", lambda e: e.tensor_tensor(out=c["C"][:].rearrange("p h s -> p (h s)"), in0=ps_c[:], in1=c["C"][:].rearrange("p h s -> p (h s)"), op=ALU.add),
         reads=[pb_c, c["bC"]], writes=[c["bC"]])
    P.op("dve", lambda e: e.tensor_tensor(out=c["n"][:], in0=ps_b[:, 16:20], in1=c["n"][:], op=ALU.add), reads=[pb_b, c["bn"]], writes=[c["bn"]])


def emit_ssd(k, l, j, bi, seqs):
    P, dr = k.P, k.dr
    base = 2576
    with P.phase():
        cw = P.sb("cwB", [128, 8, 6]); b_cw = P.buf()
        load_rows_fm(k, [dr["conv_b_w"][j, r:r + 1, :] for r in range(5)] + [dr["conv_b_b"][j:j + 1, :]], 8, cw, b_cw)
        bnw = P.sb("bnw", [128, 4, 1]); b_bnw = P.buf()
        load_rows_fm(k, [dr["b_norm_w"][j:j + 1, :]], 4, bnw, b_bnw)
        dtb = P.sb("dtb", [128, 16]); b_dtb = P.buf()
        bcast_rows(k, dtb[:], j * 16, dr["dt_bias"].tensor, 16, b_dtb)
        Aneg = P.sb("Aneg", [128, 16]); b_A = P.buf()
        bcast_rows(k, Aneg[:], j * 16, dr["a_log"].tensor, 16, b_A)
        P.op("act", lambda e: e.activation(out=Aneg[:], in_=Aneg[:], func=AF.Exp), reads=[b_A], writes=[b_A])
        dsk = P.sb("dsk", [128, 8]); b_dsk = P.buf()
        bcast_rows(k, dsk[:], j * 8, dr["d_skip"].tensor, 8, b_dsk)
        xbcT = P.sb("xbcT", [128, 8, 1024], BF16); b_xbcT = [P.buf() for _ in range(8)]
        xtm = P.sb("xtm", [128, 8, 512], BF16); b_xtm = [P.buf() for _ in range(8)]
        Btm = P.sb("Btm", [128, 8, 256], BF16); b_Btm = [P.buf() for _ in range(8)]
        dt = P.sb("dt", [128, 8, 16]); b_dt = P.buf()
        dtA = P.sb("dtA", [128, 8, 16]); b_dtA = P.buf()
        acum = P.sb("acum", [128, 8, 16]); b_acum = P.buf()
        gtmp = P.sb("gtmp2", [128, 2, 128]); b_gtmp = P.buf()
        stage = k.opts.get("ssd_stage", 99)
        if stage <= 1:
            return
        with P.phase():
            pre = [(P.sb("pre%d" % i, [128, 1024]), P.buf()) for i in range(2)]
            acc = [(P.sb("acc%d" % i, [128, 1024]), P.buf()) for i in range(2)]
            for blk in range(2):
                wv, wb = load_w(k, "w_in_even", j, base + blk * 512, 512)
                for jj in range(4):
                    c = blk * 4 + jj
                    pr, bpr = pre[c % 2]
                    ac, bac = acc[c % 2]
                    proj_fm_chunk(k, wv, wb, jj, lambda g, ps, pb: P.op("act", lambda e: acopy(e, out=pr[:, g * 512:(g + 1) * 512], in_=ps[:]), reads=[pb], writes=[bpr]))
                    conv_silu(k, pr, bpr, ac, bac, cw, b_cw, c, seqs, xbcT[:, c, :], b_xbcT[c])
            ntr = k.opts.get("ssd_tr", 6)
            for t in range(8 if stage > 2 else 0):
                ps, pb = k.nxt("tr")
                for c in range(ntr):
                    P.op("pe", lambda e: e.transpose(out=ps[:, c * 128:(c + 1) * 128], in_=xbcT[:, c, t * 128:(t + 1) * 128], identity=k.idb[:]),
                         reads=[b_xbcT[c], k.b_idb], writes=[pb], inc=(c == ntr - 1))
                P.op("dve", lambda e: e.tensor_copy(out=xtm[:, t, :], in_=ps[:, 0:512]), reads=[pb], writes=[b_xtm[t]])
                bt = k.opts.get("ssd_btm", 2)
                if bt == 1:
                    P.op("act", lambda e: acopy(e, out=Btm[:, t, :], in_=ps[:, 512:768]), reads=[pb], writes=[b_Btm[t]])
                elif bt == 2:
                    P.op("dve", lambda e: e.tensor_copy(out=Btm[:, t, :], in_=ps[:, 512:768]), reads=[pb], writes=[b_Btm[t]])
                elif bt == 4:
                    P.op("act", lambda e: e.activation(out=Btm[:, t, :], in_=ps[:, 512:768], func=AF.Identity, scale=k.onesf[:, 0:1]), reads=[pb, k.b_ones], writes=[b_Btm[t]])
                elif bt == 3:
                    for q in range(2):
                        P.op("act", lambda e: acopy(e, out=Btm[:, t, q * 128:(q + 1) * 128], in_=ps[:, 512 + q * 128:640 + q * 128]), reads=[pb], writes=[b_Btm[t]])
            if stage <= 3:
                return
            wv, wb = load_w(k, "w_in_even", j, 3600, 16)
            proj_tm(k, wv, wb, 0, 16, lambda t, ps, pb: P.op("dve", lambda e: e.tensor_tensor(out=dt[:, t, :], in0=ps[:, 0:16], in1=dtb[:], op=ALU.add),
                                                              reads=[pb, b_dtb], writes=[b_dt]))
            softplus_inplace(k, dt[:], b_dt, gtmp[:, 0, :].rearrange("p (t f) -> p t f", t=8), b_gtmp, gtmp[:, 1, :].rearrange("p (t f) -> p t f", t=8))
            P.op("dve", lambda e: e.scalar_tensor_tensor(out=dtA[:], in0=dt[:], scalar=-1.0, in1=V(Aneg, 0, 128, 0, [[0, 8], [1, 16]]), op0=ALU.mult, op1=ALU.mult),
                 reads=[b_dt, b_A], writes=[b_dtA])
            ps, pb = k.nxt("x")
            for t in range(8):
                P.op("pe", lambda e: e.matmul(ps[:, t * 16:t * 16 + 8], lhsT=k.maskF[:], rhs=dtA[:, t, 0:8], start=True, stop=True), reads=[k.b_mask, b_dtA], writes=[pb], inc=False)
                P.op("pe", lambda e: e.matmul(ps[:, t * 16 + 8:t * 16 + 16], lhsT=k.maskB[:], rhs=dtA[:, t, 8:16], start=True, stop=True), reads=[k.b_mask, b_dtA], writes=[pb], inc=(t == 7))
            P.op("dve", lambda e: e.tensor_copy(out=acum[:], in_=ps[:, 0:128].rearrange("p (t f) -> p t f", t=8)), reads=[pb], writes=[b_acum])
        if stage <= 4:
            return
        with P.phase():
            yss = P.sb("yss", [128, 8, 512]); b_yss = [P.buf() for _ in range(8)]
            seen = set()
            ch = []
            for d in range(2):
                ch.append(dict(S=P.sb("S%d" % d, [128, 8, 64]), Sb=P.sb("Sb%d" % d, [128, 8, 64], BF16), bS=P.buf(), bSb=P.buf(),
                               sm=P.sb("ssm%d" % d, [128, 4, 8]), bsm=P.buf()))
            tp = dict(dgA=P.sb("dgA", [128, 1024]), bdgA=P.buf(), Dm=P.sb("Dm", [128, 1024]), bDm=P.buf(),
                      GT=P.sb("GT", [128, 8, 128], BF16), bGT=P.buf(), xdt=P.sb("xdt", [128, 8, 64], BF16), bxdt=P.buf(),
                      xw=P.sb("xw", [128, 8, 64], BF16), bxw=P.buf(), tmp=P.sb("stmp", [128, 512]), btmp=P.buf())
            for si, (t0, T) in enumerate(seqs):
                nt = T // 128
                tb = t0 // 128
                for d in range(2):
                    c = ch[d]
                    if bi == 0:
                        P.op("pool", lambda e: e.memset(c["S"][:], 0.0), writes=[c["bS"]])
                    else:
                        for h2 in range(4):
                            P.dma("sp", tp["Dm"][:, h2 * 128:(h2 + 1) * 128], dr["st_S"][j, d, 2 * h2:2 * h2 + 2].rearrange("h p n -> (h p) n"), writes=[tp["bDm"]])
                        ps, pb = k.nxt("x")
                        for h2 in range(4):
                            P.op("pe", lambda e: e.transpose(out=ps[:, h2 * 128:(h2 + 1) * 128], in_=tp["Dm"][:, h2 * 128:(h2 + 1) * 128], identity=k.idf[:]),
                                 reads=[tp["bDm"], k.b_idf], writes=[pb], inc=(h2 == 3))
                        P.op("dve", lambda e: e.tensor_copy(out=c["S"][:].rearrange("p h s -> p (h s)"), in_=ps[:]), reads=[pb], writes=[c["bS"]])
                    P.op("act", lambda e: acopy(e, out=c["Sb"][:], in_=c["S"][:]), reads=[c["bS"]], writes=[c["bSb"]])
                for i in range(nt):
                    gens = []
                    for d in range(2):
                        t = tb + (i if d == 0 else nt - 1 - i)
                        first = t not in seen
                        seen.add(t)
                        g_ = ssd_step(k, ch[d], tp, d, first, t, xbcT, b_xbcT, xtm, b_xtm, Btm, b_Btm, dt, b_dt, dtA, b_dtA, acum, b_acum, yss, b_yss)
                        for _ in g_:
                            pass
                if bi == 0:
                    for d in range(2):
                        c = ch[d]
                        ps, pb = k.nxt("x")
                        for h2 in range(4):
                            P.op("pe", lambda e: e.transpose(out=ps[:, h2 * 128:(h2 + 1) * 128], in_=c["S"][:, 2 * h2:2 * h2 + 2, :].rearrange("p h s -> p (h s)"), identity=k.idf[:]),
                                 reads=[c["bS"], k.b_idf], writes=[pb], inc=(h2 == 3))
                        P.op("dve", lambda e: e.tensor_copy(out=tp["Dm"][:, 0:512], in_=ps[:]), reads=[pb], writes=[tp["bDm"]])
                        for h2 in range(4):
                            P.dma("sp", dr["o_S"][si, j, d, 2 * h2:2 * h2 + 2].rearrange("h p n -> (h p) n"), tp["Dm"][:, h2 * 128:(h2 + 1) * 128], reads=[tp["bDm"]])
            if stage <= 5:
                return
            ztms = [(P.sb("ztm%d" % i, [128, 512], BF16), P.buf()) for i in range(2)]
            scrs = [dict(sq=P.sb("gsq%d" % i, [128, 512]), ss=P.sb("gss%d" % i, [128, 4]), hn=P.sb("ghn%d" % i, [128, 512], BF16), b=P.buf()) for i in range(2)]
            wv, wb = load_w(k, "w_in_even", j, 2064, 512)

            def fin(t, ps, pb):
                ztm, b_ztm = ztms[t % 2]
                scr = scrs[t % 2]
                P.op("act", lambda e: e.activation(out=ztm[:], in_=ps[:], func=AF.Silu), reads=[pb], writes=[b_ztm])
                P.op("dve", lambda e: e.tensor_tensor(out=scr["sq"][:].rearrange("p (h s) -> p h s", h=8), in0=xtm[:, t, :].rearrange("p (h s) -> p h s", h=8),
                                                      in1=V(dsk, 0, 128, 0, [[1, 8], [0, 64]]), op=ALU.mult), reads=[b_xtm[t], b_dsk], writes=[scr["b"]])
                P.op("dve", lambda e: e.tensor_tensor(out=yss[:, t, :], in0=yss[:, t, :], in1=scr["sq"][:], op=ALU.add), reads=[scr["b"], b_yss[t]], writes=[b_yss[t]])
                P.op("dve", lambda e: e.tensor_tensor(out=yss[:, t, :], in0=yss[:, t, :], in1=ztm[:], op=ALU.mult), reads=[b_ztm, b_yss[t]], writes=[b_yss[t]])
                return group_norm_to_mixT(k, yss[:, t, :], b_yss[t], t, 2, 256, bnw[:, :, 0], b_bnw, 4, scr)
            proj_tm(k, wv, wb, 0, 512, fin)


def ssd_step(k, c, tp, d, first, t, xbcT, b_xbcT, xtm, b_xtm, Btm, b_Btm, dt, b_dt, dtA, b_dtA, acum, b_acum, yss, b_yss):
    P = k.P
    nm = k.nmF if d == 0 else k.nmB
    tok = slice(t * 128, (t + 1) * 128)
    sm, bsm = c["sm"], c["bsm"]
    ac_off = t * 16 + d * 8
    P.op("dve", lambda e: e.tensor_tensor(out=tp["dgA"][:].rearrange("p (h s) -> p h s", h=8), in0=V(k.idf, 0, 128, 0, [[0, 8], [1, 128]]),
                                          in1=V(acum, 0, 128, ac_off, [[1, 8], [0, 128]]), op=ALU.mult), reads=[k.b_idf, b_acum], writes=[tp["bdgA"]])
    psD = []
    for hh in range(2):
        ps, pb = k.nxt("mm")
        P.op("pe", lambda e: e.matmul(ps[:], lhsT=k.onesf[:], rhs=tp["dgA"][:, hh * 512:(hh + 1) * 512], start=True, stop=False), reads=[k.b_ones, tp["bdgA"]], writes=[pb], inc=False)
        P.op("pe", lambda e: e.matmul(ps[:].rearrange("p (h s) -> p h s", h=4), lhsT=k.idb[:], rhs=V(k.nmbf, 0, 128, d * 128, [[0, 4], [1, 128]]), start=False, stop=True),
             reads=[k.b_idb, k.b_nmbf], writes=[pb])
        psD.append((ps, pb))
    for hh in range(2):
        ps, pb = psD[hh]
        P.op("dve", lambda e: e.tensor_tensor(out=tp["Dm"][:, hh * 512:(hh + 1) * 512].rearrange("p (h s) -> p h s", h=4), in0=ps[:].rearrange("p (h s) -> p h s", h=4),
                                              in1=V(acum, 0, 128, ac_off + hh * 4, [[1, 4], [0, 128]]), op=ALU.subtract), reads=[pb, b_acum], writes=[tp["bDm"]])
    P.op("act", lambda e: e.activation(out=tp["Dm"][:], in_=tp["Dm"][:], func=AF.Exp), reads=[tp["bDm"]], writes=[tp["bDm"]])
    sub = k.opts.get("ssd_sub", 99)
    if sub <= 1:
        return
    ps_cb, pb_cb = k.nxt("x")
    for g in range(2):
        P.op("pe", lambda e: e.matmul(ps_cb[:, g * 128:(g + 1) * 128], lhsT=xbcT[:, 4 + g, tok], rhs=xbcT[:, 6 + g, tok], start=True, stop=True),
             reads=[b_xbcT[4 + g], b_xbcT[6 + g]], writes=[pb_cb], inc=False)
    P.op("pe", lambda e: e.matmul(ps_cb[:, 256:264], lhsT=k.onesf[:], rhs=dtA[:, t, d * 8:d * 8 + 8], start=True, stop=True), reads=[k.b_ones, b_dtA], writes=[pb_cb])
    for g in range(2):
        P.op("dve", lambda e: e.tensor_tensor(out=tp["GT"][:, g * 4:(g + 1) * 4, :], in0=tp["Dm"][:, g * 512:(g + 1) * 512].rearrange("p (h s) -> p h s", h=4),
                                              in1=V(ps_cb, 0, 128, g * 128, [[0, 4], [1, 128]]), op=ALU.mult), reads=[tp["bDm"], pb_cb], writes=[tp["bGT"]])
    if sub <= 2:
        return
    P.op("act", lambda e: e.activation(out=sm[:, 0, :], in_=acum[:, t, d * 8:d * 8 + 8], func=AF.Exp), reads=[b_acum], writes=[bsm])
    P.op("dve", lambda e: e.tensor_tensor(out=sm[:, 1, :], in0=ps_cb[:, 256:264], in1=acum[:, t, d * 8:d * 8 + 8], op=ALU.subtract), reads=[pb_cb, b_acum], writes=[bsm])
    P.op("act", lambda e: e.activation(out=sm[:, 1, :], in_=sm[:, 1, :], func=AF.Exp), reads=[bsm], writes=[bsm])
    P.op("dve", lambda e: e.tensor_tensor(out=sm[:, 1, :], in0=sm[:, 1, :], in1=dt[:, t, d * 8:d * 8 + 8], op=ALU.mult), reads=[bsm, b_dt], writes=[bsm])
    P.op("dve", lambda e: e.tensor_copy(out=sm[:, 3, :], in_=ps_cb[:, 256:264]), reads=[pb_cb], writes=[bsm])
    P.op("act", lambda e: e.activation(out=sm[:, 2, :], in_=sm[:, 3, :], func=AF.Exp), reads=[bsm], writes=[bsm])
    xv = xtm[:, t, :].rearrange("p (h s) -> p h s", h=8)
    P.op("pool", lambda e: e.tensor_tensor(out=tp["xdt"][:], in0=xv, in1=V(dt, 0, 128, t * 16 + d * 8, [[1, 8], [0, 64]]), op=ALU.mult), reads=[b_xtm[t], b_dt], writes=[tp["bxdt"]])
    P.op("pool", lambda e: e.tensor_tensor(out=tp["xw"][:], in0=xv, in1=V(sm, 0, 128, 8, [[1, 8], [0, 64]]), op=ALU.mult), reads=[b_xtm[t], bsm], writes=[tp["bxw"]])
    if sub <= 3:
        return
    ps_y, pb_y = k.nxt("mm")
    ps_z, pb_z = k.nxt("mm")
    for hd in range(8):
        P.op("pe", lambda e: e.matmul(ps_y[:, hd * 64:(hd + 1) * 64], lhsT=tp["GT"][:, hd, :], rhs=tp["xdt"][:, hd, :], start=True, stop=True),
             reads=[tp["bGT"], tp["bxdt"]], writes=[pb_y], inc=(hd == 7))
    for hd in range(8):
        P.op("pe", lambda e: e.matmul(ps_z[:, hd * 64:(hd + 1) * 64], lhsT=xbcT[:, 6 + hd // 4, tok], rhs=c["Sb"][:, hd, :], start=True, stop=True),
             reads=[b_xbcT[6 + hd // 4], c["bSb"]], writes=[pb_z], inc=(hd == 7))
    yield
    P.op("dve", lambda e: e.tensor_tensor(out=tp["tmp"][:].rearrange("p (h s) -> p h s", h=8), in0=ps_z[:].rearrange("p (h s) -> p h s", h=8),
                                          in1=V(sm, 0, 128, 0, [[1, 8], [0, 64]]), op=ALU.mult), reads=[pb_z, bsm], writes=[tp["btmp"]])
    if first:
        P.op("dve", lambda e: e.tensor_tensor(out=yss[:, t, :], in0=ps_y[:], in1=tp["tmp"][:], op=ALU.add), reads=[pb_y, tp["btmp"]], writes=[b_yss[t]])
    else:
        P.op("dve", lambda e: e.tensor_tensor(out=tp["tmp"][:], in0=ps_y[:], in1=tp["tmp"][:], op=ALU.add), reads=[pb_y, tp["btmp"]], writes=[tp["btmp"]])
        P.op("pool", lambda e: e.tensor_tensor(out=yss[:, t, :], in0=yss[:, t, :], in1=tp["tmp"][:], op=ALU.add), reads=[tp["btmp"], b_yss[t]], writes=[b_yss[t]])
    if sub <= 4:
        return
    ps_s, pb_s = k.nxt("mm")
    for hd in range(8):
        g = hd // 4
        P.op("pe", lambda e: e.matmul(ps_s[:, hd * 64:(hd + 1) * 64], lhsT=Btm[:, t, g * 128:(g + 1) * 128], rhs=tp["xw"][:, hd, :], start=True, stop=True),
             reads=[b_Btm[t], tp["bxw"]], writes=[pb_s], inc=(hd == 7))
    P.op("dve", lambda e: e.tensor_tensor(out=c["S"][:], in0=c["S"][:], in1=V(sm, 0, 128, 16, [[1, 8], [0, 64]]), op=ALU.mult), reads=[bsm, c["bS"]], writes=[c["bS"]])
    P.op("dve", lambda e: e.tensor_tensor(out=c["S"][:].rearrange("p h s -> p (h s)"), in0=ps_s[:], in1=c["S"][:].rearrange("p h s -> p (h s)"), op=ALU.add),
         reads=[pb_s, c["bS"]], writes=[c["bS"]])
    P.op("act", lambda e: acopy(e, out=c["Sb"][:], in_=c["S"][:]), reads=[c["bS"]], writes=[c["bSb"]])


def emit_outproj_resid(k, l, bi):
    P = k.P
    j = l // 2
    name = "w_out_even" if l % 2 == 0 else "w_out_odd"
    emit_gate_vec(k, l, bi, 0)
    with P.phase():
        ybuf = P.sb("ybuf", [128, 8, D]); b_y = [P.buf() for _ in range(16)]
        for h in range(2):
            wv, wb = load_w(k, name, j, h * 512, 512)
            for t in range(8):
                ps, pb = k.nxt("mm")
                for kc in range(8):
                    P.op("pe", lambda e: e.matmul(ps[:], lhsT=k.mixT[:, kc, t * 128:(t + 1) * 128], rhs=wv[:, kc, :], start=(kc == 0), stop=(kc == 7)),
                         reads=[wb, k.b_mixT[t]], writes=[pb], inc=(kc == 7))
                P.op("act", lambda e: acopy(e, out=ybuf[:, t, h * 512:(h + 1) * 512], in_=ps[:]), reads=[pb], writes=[b_y[t * 2 + h]])
        emit_resid_all(k, [([ybuf[:, t, 0:512], ybuf[:, t, 512:1024]], [b_y[t * 2], b_y[t * 2 + 1]]) for t in range(8)])


def emit_mixer(k, l, bi):
    if l % 2 == 0:
        emit_even(k, l, bi)
    else:
        emit_odd(k, l, bi)


def emit_odd(k, l, bi):
    raise NotImplementedError


def attn_unit(k, A, qT, ksegs, vlist, scale, sinkcol, out_ap, out_buf, reads):
    P = k.P
    u = A["rr"]
    A["rr"] = u + 1
    Ssb, bS = A["Ssb"][u % len(A["Ssb"])]
    Pb, bP = A["Pb"][u % len(A["Pb"])]
    PT, bPT = A["PT"][u % len(A["PT"])]
    sm, bsm = A["sm"][u % 2]
    col = 0
    for si, (kT, n, masks) in enumerate(ksegs):
        ps, pb = k.nxt("mm")
        P.op("pe", lambda e: e.matmul(ps[:, 0:n], lhsT=qT, rhs=kT, start=True, stop=(len(masks) == 0)), reads=reads, writes=[pb], inc=(len(masks) == 0))
        for mi, (c0, nm) in enumerate(masks):
            P.op("pe", lambda e: e.matmul(ps[:, c0:c0 + 128], lhsT=k.idb[:], rhs=nm, start=False, stop=(mi == len(masks) - 1)),
                 reads=[k.b_idb, A["b_nm"]], writes=[pb], inc=(mi == len(masks) - 1))
        eng = "act" if si % 2 == 0 else "dve"
        if eng == "act":
            P.op("act", lambda e: acopy(e, out=Ssb[:, col:col + n], in_=ps[:, 0:n]), reads=[pb], writes=[bS])
        else:
            P.op("dve", lambda e: e.tensor_copy(out=Ssb[:, col:col + n], in_=ps[:, 0:n]), reads=[pb], writes=[bS])
        col += n
    N = col
    nblk = N // 128
    assert nblk == len(vlist)
    P.op("dve", lambda e: e.tensor_reduce(out=sm[:, 0:1], in_=Ssb[:, 0:N], axis=AX.X, op=ALU.max), reads=[bS], writes=[bsm])
    if sinkcol is not None:
        P.op("dve", lambda e: e.tensor_scalar(out=sm[:, 1:2], in0=sm[:, 0:1], scalar1=-scale, scalar2=sinkcol, op0=ALU.mult, op1=ALU.min), reads=[bsm, A["b_sink"]], writes=[bsm])
    else:
        P.op("dve", lambda e: e.tensor_scalar(out=sm[:, 1:2], in0=sm[:, 0:1], scalar1=-scale, scalar2=None, op0=ALU.mult), reads=[bsm], writes=[bsm])
    P.op("act", lambda e: e.activation(out=Pb[:, 0:N], in_=Ssb[:, 0:N], func=AF.Exp, scale=scale, bias=sm[:, 1:2], accum_out=sm[:, 2:3]), reads=[bS, bsm], writes=[bP, bsm])
    has_sink = sinkcol is not None
    if has_sink:
        P.op("act", lambda e: e.activation(out=sm[:, 3:4], in_=sinkcol, func=AF.Exp, scale=-1.0, bias=sm[:, 1:2]), reads=[bsm, A["b_sink"]], writes=[bsm])

    def part2():
        _attn_part2(k, A, u, Pb, bP, PT, bPT, sm, bsm, nblk, vlist, reads, out_ap, out_buf, has_sink)
    prev = A.get("pend")
    A["pend"] = part2
    if prev is not None:
        prev()


def attn_flush(A):
    if A.get("pend") is not None:
        A["pend"]()
        A["pend"] = None


def _attn_part2(k, A, u, Pb, bP, PT, bPT, sm, bsm, nblk, vlist, reads, out_ap, out_buf, has_sink):
    P = k.P
    for b0 in range(0, nblk, 8):
        nb_ = min(8, nblk - b0)
        ps, pb = k.nxt("tr")
        for b in range(nb_):
            P.op("pe", lambda e: e.transpose(out=ps[:, b * 128:(b + 1) * 128], in_=Pb[:, (b0 + b) * 128:(b0 + b + 1) * 128], identity=k.idb[:]),
                 reads=[bP, k.b_idb], writes=[pb], inc=(b == nb_ - 1))
        if False:
            P.op("act", lambda e: acopy(e, out=PT[:, b0 * 128:(b0 + nb_) * 128], in_=ps[:, 0:nb_ * 128]), reads=[pb], writes=[bPT])
        else:
            P.op("dve", lambda e: e.tensor_copy(out=PT[:, b0 * 128:(b0 + nb_) * 128], in_=ps[:, 0:nb_ * 128]), reads=[pb], writes=[bPT])
    pso, pbo = k.nxt("x")
    for b in range(nblk):
        P.op("pe", lambda e: e.matmul(pso[:, 0:64], lhsT=PT[:, b * 128:(b + 1) * 128], rhs=vlist[b], start=(b == 0), stop=(b == nblk - 1)),
             reads=[bPT] + reads, writes=[pbo], inc=(b == nblk - 1))
    if has_sink:
        P.op("dve", lambda e: e.tensor_tensor(out=sm[:, 2:3], in0=sm[:, 2:3], in1=sm[:, 3:4], op=ALU.add), reads=[bsm], writes=[bsm])
    P.op("dve", lambda e: e.reciprocal(out=sm[:, 4:5], in_=sm[:, 2:3]), reads=[bsm], writes=[bsm])
    P.op("dve", lambda e: e.tensor_scalar(out=out_ap, in0=pso[:, 0:64], scalar1=sm[:, 4:5], scalar2=None, op0=ALU.mult), reads=[pbo, bsm], writes=[out_buf])


def rope_fm(k, A, ps, pb, nrows, perm, cosT, sinT, g, dst, dst_buf):
    P = k.P
    u = A["rrr"]
    A["rrr"] = u + 1
    raw, braw = A["raw"][u % 2]
    t1, bt1 = A["t1"][u % len(A["t1"])]
    cs = slice(g * 512, (g + 1) * 512)
    P.op("act", lambda e: acopy(e, out=raw[0:nrows, :], in_=ps[0:nrows, :]), reads=[pb], writes=[braw])
    ps2, pb2 = k.nxt("mm")
    P.op("pe", lambda e: e.matmul(ps2[0:nrows, :], lhsT=perm[0:nrows, 0:nrows], rhs=raw[0:nrows, :], start=True, stop=True), reads=[braw, A["b_rope"]], writes=[pb2])
    P.op("pool", lambda e: e.tensor_tensor(out=t1[0:nrows, :], in0=raw[0:nrows, :], in1=cosT[0:nrows, cs], op=ALU.mult), reads=[braw, A["b_rope"]], writes=[bt1])
    P.op("dve", lambda e: e.tensor_tensor(out=raw[0:nrows, :], in0=ps2[0:nrows, :], in1=sinT[0:nrows, cs], op=ALU.mult), reads=[pb2, A["b_rope"]], writes=[braw])
    P.op("dve", lambda e: e.tensor_tensor(out=dst, in0=t1[0:nrows, :], in1=raw[0:nrows, :], op=ALU.add), reads=[bt1, braw], writes=[dst_buf])


def emit_odd(k, l, bi):
    P, dr = k.P, k.dr
    j = l // 2
    sample = (bi == 1)
    NK = 1280 if sample else 1024
    nkt = NK // 128
    with P.phase():
        A = dict(rr=0, rrr=0)
        sinkneg = P.sb("sinkneg", [128, 8]); A["b_sink"] = P.buf()
        bcast_rows(k, sinkneg[:], j * 8, dr["sink"].tensor, 8, A["b_sink"])
        P.op("dve", lambda e: e.tensor_scalar(out=sinkneg[:], in0=sinkneg[:], scalar1=-1.0, scalar2=None, op0=ALU.mult), reads=[A["b_sink"]], writes=[A["b_sink"]])
        qanw = P.sb("qanw", [128, 2, 1]); b_qanw = P.buf()
        load_rows_fm(k, [dr["q_a_norm"][j:j + 1, :]], 2, qanw, b_qanw)
        kvnw = P.sb("kvnw", [128, 128]); b_kvnw = P.buf()
        bcast_rows(k, kvnw[:], j * 128, dr["kv_a_norm"].tensor, 128, b_kvnw)
        wqb = P.sb("wqb", [128, 2, 768], BF16); b_wqb = P.buf()
        P.dma("pool", wqb[:], dr["w_q_b"][j].rearrange("(c p) f -> p c f", p=128), writes=[b_wqb])
        wkvK = P.sb("wkvK", [128, 8, 96], BF16); b_wkv = P.buf()
        wkvV = P.sb("wkvV", [128, 8, 64], BF16)
        P.op("pool", lambda e: e.memset(wkvK[:], 0.0), writes=[b_wkv])
        wkv3 = dr["w_kv_b"][j].rearrange("p (h f) -> p h f", h=8)
        P.dma("pool", wkvK[:, :, 0:64], wkv3[:, :, 0:64], writes=[b_wkv])
        P.dma("pool", wkvV[:], wkv3[:, :, 64:128], writes=[b_wkv])
        esel = P.sb("esel", [32, 96], BF16)
        nmb = P.sb("nmb", [128, 2, 128], BF16); A["b_nm"] = P.buf()
        P.dma("pool", esel[:], dr["esel"], writes=[b_wkv])
        P.dma("pool", nmb[:, 0, :], dr["nmF"], writes=[A["b_nm"]])
        P.dma("pool", nmb[:, 1, :], dr["nmB"], writes=[A["b_nm"]])
        if sample:
            ropeD = P.sb("ropeD", [96, 2, 1024], BF16); A["b_rope"] = P.buf()
            for i in range(2):
                P.dma("pool", ropeD[:, i, :], dr["rope_tab"][2 + i, 0:96, :], writes=[A["b_rope"]])
            perms = P.sb("perms", [128, 3, 128], BF16)
            for i in range(3):
                P.dma("pool", perms[:, i, :], dr["rope_perm"][i], writes=[A["b_rope"]])
            A["raw"] = [(P.sb("rraw%d" % i, [128, 512], BF16), P.buf()) for i in range(2)]
            A["t1"] = [(P.sb("rt1%d" % i, [128, 512]), P.buf()) for i in range(1)]
        qcT = P.sb("qcT", [128, 4, 1024], BF16); b_qcT = [P.buf() for _ in range(4)]
        kdup = [P.sb("kdup%d" % g, [128, NK], BF16) for g in range(2)]; b_kdup = [P.buf() for _ in range(2)]
        vc = P.sb("vc", [128, nkt, 128], BF16); b_vc = P.buf()
        qanT = P.sb("qanT", [128, 2, 1024], BF16); b_qanT = P.buf()
        ckvT = P.sb("ckvT", [128, NK], BF16); b_ckvT = P.buf()
        kpeT = P.sb("kpeT", [32, NK], BF16); b_kpeT = P.buf()
        oall = P.sb("oall", [128, 8, 1024], BF16); b_oall = [P.buf() for _ in range(8)]
        ssm = P.sb("ossm", [128, 8, 4]); b_ssm = [P.buf() for _ in range(8)]
        from contextlib import ExitStack as _ES
        inner = _ES()
        inner.enter_context(P.phase())
        st = [(P.sb("ost%d" % i, [128, 416]), P.buf()) for i in range(2)]
        stb = [(P.sb("ostb%d" % i, [128, 416], BF16), P.buf()) for i in range(2)]
        if sample:
            rope = P.sb("ropeCK", [128, 6, 1024], BF16)
            for i in (0, 1, 4, 5):
                P.dma("pool", rope[:, i, :], dr["rope_tab"][i], writes=[A["b_rope"]])
        ost = k.opts.get("odd_stage", 99)
        if ost <= 1:
            inner.close(); return
        wv, wb = load_w(k, "w_in_odd", j, 0, 512)
        for c in range(4):
            if sample:
                proj_fm_chunk(k, wv, wb, c, lambda g, ps, pb: rope_fm(k, A, ps, pb, 128, perms[:, 0, :], rope[:, 0, :], rope[:, 1, :], g, qcT[:, c, g * 512:(g + 1) * 512], b_qcT[c]))
            else:
                proj_fm_chunk(k, wv, wb, c, lambda g, ps, pb: P.op("act", lambda e: acopy(e, out=qcT[:, c, g * 512:(g + 1) * 512], in_=ps[:]), reads=[pb], writes=[b_qcT[c]]))
        if ost <= 2:
            inner.close(); return
        wt, wb = k.wnext()
        wdup = wt[:, 0:8 * 256].rearrange("p (c f) -> p c f", c=8)
        for g in range(2):
            for dup in range(2):
                P.dma("pool", wdup[:, :, (g * 2 + dup) * 64:(g * 2 + dup + 1) * 64],
                      dr["w_in_odd"][j, :, 512 + g * 64:512 + (g + 1) * 64].rearrange("(c p) f -> p c f", p=128), writes=[wb])
        for g in range(2):
            if sample:
                proj_fm_chunk(k, wdup, wb, g, lambda gg, ps, pb: rope_fm(k, A, ps, pb, 128, perms[:, 0, :], rope[:, 0, :], rope[:, 1, :], gg, kdup[g][:, gg * 512:(gg + 1) * 512], b_kdup[g]))
            else:
                proj_fm_chunk(k, wdup, wb, g, lambda gg, ps, pb: P.op("act", lambda e: acopy(e, out=kdup[g][:, gg * 512:(gg + 1) * 512], in_=ps[:]), reads=[pb], writes=[b_kdup[g]]))
        if ost <= 3:
            inner.close(); return
        wv, wb = load_w(k, "w_in_odd", j, 512, 256)

        def evac_kv(t, ps, pb):
            s_, bs_ = st[t % 2]
            P.op("dve", lambda e: e.tensor_copy(out=s_[:, 0:256], in_=ps[:, 0:256]), reads=[pb], writes=[bs_])
            P.op("pool", lambda e: e.tensor_copy(out=vc[:, t, :], in_=s_[:, 128:256]), reads=[bs_], writes=[b_vc])
            if not sample:
                rows = slice((t % 2) * 128, (t % 2) * 128 + 128)
                P.dma("sp", dr["o_k"][t // 2, j, rows].rearrange("t g d -> t (g d)"), s_[:, 0:128], reads=[bs_])
                P.dma("sp", dr["o_v"][t // 2, j, rows].rearrange("t g d -> t (g d)"), s_[:, 128:256], reads=[bs_])
        proj_tm(k, wv, wb, 0, 256, evac_kv)
        if ost <= 4:
            inner.close(); return
        wv, wb = load_w(k, "w_in_odd", j, 768, 416)

        def evac_lat(t, ps, pb):
            s_, bs_ = st[t % 2]
            sb_, bsb_ = stb[t % 2]
            sm = ssm[:, t, :]
            tok = slice(t * 128, (t + 1) * 128)
            rows = slice((t % 2) * 128, (t % 2) * 128 + 128)
            P.op("act", lambda e: acopy(e, out=s_[:], in_=ps[:, 0:416]), reads=[pb], writes=[bs_])
            P.op("act", lambda e: e.activation(out=sb_[:, 0:256], in_=s_[:, 0:256], func=AF.Square, accum_out=sm[:, 0:1]), reads=[bs_], writes=[bsb_, b_ssm[t]])
            P.op("act", lambda e: e.activation(out=sb_[:, 256:384], in_=s_[:, 256:384], func=AF.Square, accum_out=sm[:, 1:2]), reads=[bs_], writes=[bsb_, b_ssm[t]])
            P.op("dve", lambda e: e.tensor_scalar(out=sm[:, 0:1], in0=sm[:, 0:1], scalar1=1.0 / 256, scalar2=EPS, op0=ALU.mult, op1=ALU.add), reads=[b_ssm[t]], writes=[b_ssm[t]])
            P.op("dve", lambda e: e.tensor_scalar(out=sm[:, 1:2], in0=sm[:, 1:2], scalar1=1.0 / 128, scalar2=EPS, op0=ALU.mult, op1=ALU.add), reads=[b_ssm[t]], writes=[b_ssm[t]])
            P.op("act", lambda e: e.activation(out=sm[:, 0:2], in_=sm[:, 0:2], func=AF.Sqrt), reads=[b_ssm[t]], writes=[b_ssm[t]])
            P.op("dve", lambda e: e.reciprocal(out=sm[:, 2:4], in_=sm[:, 0:2]), reads=[b_ssm[t]], writes=[b_ssm[t]])
            P.op("dve", lambda e: e.tensor_scalar(out=sb_[:, 0:256], in0=s_[:, 0:256], scalar1=sm[:, 2:3], scalar2=None, op0=ALU.mult), reads=[bs_, b_ssm[t]], writes=[bsb_])
            P.op("dve", lambda e: e.scalar_tensor_tensor(out=s_[:, 256:384], in0=s_[:, 256:384], scalar=sm[:, 3:4], in1=kvnw[:], op0=ALU.mult, op1=ALU.mult),
                 reads=[bs_, b_ssm[t], b_kvnw], writes=[bs_])
            P.op("pool", lambda e: e.tensor_copy(out=sb_[:, 256:416], in_=s_[:, 256:416]), reads=[bs_], writes=[bsb_])
            if not sample:
                P.dma("sp", dr["o_ckv"][t // 2, j, rows], s_[:, 256:384], reads=[bs_])
                P.dma("sp", dr["o_kpe"][t // 2, j, rows], s_[:, 384:416], reads=[bs_])
            def tail():
                ps2, pb2 = k.nxt("tr")
                for c in range(3):
                    P.op("pe", lambda e: e.transpose(out=ps2[:, c * 128:(c + 1) * 128], in_=sb_[:, c * 128:(c + 1) * 128], identity=k.idb[:]), reads=[bsb_, k.b_idb], writes=[pb2], inc=False)
                P.op("pe", lambda e: e.transpose(out=ps2[0:32, 384:512], in_=sb_[:, 384:416], identity=k.idb[:]), reads=[bsb_, k.b_idb], writes=[pb2])
                for c in range(2):
                    P.op("dve", lambda e: e.tensor_scalar(out=qanT[:, c, tok], in0=ps2[:, c * 128:(c + 1) * 128], scalar1=qanw[:, c, 0:1], scalar2=None, op0=ALU.mult), reads=[pb2, b_qanw], writes=[b_qanT])
                P.op("dve", lambda e: e.tensor_copy(out=ckvT[:, tok], in_=ps2[:, 256:384]), reads=[pb2], writes=[b_ckvT])
                P.op("dve", lambda e: e.tensor_copy(out=kpeT[:, tok], in_=ps2[0:32, 384:512]), reads=[pb2], writes=[b_kpeT])
            return tail
        proj_tm(k, wv, wb, 0, 416, evac_lat)
        if sample:
            for g in range(2):
                raw, braw = A["raw"][g]
                t1, bt1 = A["t1"][0]
                cs = slice(g * 512, (g + 1) * 512)
                ps2, pb2 = k.nxt("mm")
                P.op("pe", lambda e: e.matmul(ps2[0:32, :], lhsT=perms[0:32, 2, 0:32], rhs=kpeT[:, cs], start=True, stop=True), reads=[b_kpeT, A["b_rope"]], writes=[pb2])
                P.op("pool", lambda e: e.tensor_tensor(out=t1[0:32, :], in0=kpeT[:, cs], in1=rope[0:32, 4, cs], op=ALU.mult), reads=[b_kpeT, A["b_rope"]], writes=[bt1])
                P.op("dve", lambda e: e.tensor_tensor(out=raw[0:32, :], in0=ps2[0:32, :], in1=rope[0:32, 5, cs], op=ALU.mult), reads=[pb2, A["b_rope"]], writes=[braw])
                P.op("dve", lambda e: e.tensor_tensor(out=kpeT[:, cs], in0=t1[0:32, :], in1=raw[0:32, :], op=ALU.add), reads=[bt1, braw], writes=[b_kpeT])
            s_, bs_ = st[0]
            sb_, bsb_ = stb[0]
            for tt in range(2):
                rows = slice(tt * 128, (tt + 1) * 128)
                for g in range(2):
                    for dup in range(2):
                        P.dma("sp", s_[:, (g * 2 + dup) * 64:(g * 2 + dup + 1) * 64], dr["c_k"][j, rows, g, :], writes=[bs_])
                P.dma("sp", s_[:, 256:384], dr["c_v"][j, rows].rearrange("t g d -> t (g d)"), writes=[bs_])
                P.op("dve", lambda e: e.tensor_copy(out=sb_[:, 0:256], in_=s_[:, 0:256]), reads=[bs_], writes=[bsb_])
                P.op("act", lambda e: acopy(e, out=vc[:, 8 + tt, :], in_=s_[:, 256:384]), reads=[bs_], writes=[b_vc])
                ps2, pb2 = k.nxt("tr")
                for g in range(2):
                    P.op("pe", lambda e: e.transpose(out=ps2[:, g * 128:(g + 1) * 128], in_=sb_[:, g * 128:(g + 1) * 128], identity=k.idb[:]), reads=[bsb_, k.b_idb], writes=[pb2], inc=(g == 1))
                for g in range(2):
                    P.op("dve", lambda e: e.tensor_copy(out=kdup[g][:, 1024 + tt * 128:1024 + (tt + 1) * 128], in_=ps2[:, g * 128:(g + 1) * 128]), reads=[pb2], writes=[b_kdup[g]])
                s2, bs2 = st[1]
                sb2, bsb2 = stb[1]
                P.dma("sp", s2[:, 0:128], dr["c_ckv"][j, rows], writes=[bs2])
                P.dma("sp", s2[:, 128:160], dr["c_kpe"][j, rows], writes=[bs2])
                P.op("dve", lambda e: e.tensor_copy(out=sb2[:, 0:160], in_=s2[:, 0:160]), reads=[bs2], writes=[bsb2])
                ps3, pb3 = k.nxt("tr")
                P.op("pe", lambda e: e.transpose(out=ps3[:, 0:128], in_=sb2[:, 0:128], identity=k.idb[:]), reads=[bsb2, k.b_idb], writes=[pb3], inc=False)
                P.op("pe", lambda e: e.transpose(out=ps3[0:32, 128:256], in_=sb2[:, 128:160], identity=k.idb[:]), reads=[bsb2, k.b_idb], writes=[pb3])
                P.op("dve", lambda e: e.tensor_copy(out=ckvT[:, 1024 + tt * 128:1024 + (tt + 1) * 128], in_=ps3[:, 0:128]), reads=[pb3], writes=[b_ckvT])
                P.op("dve", lambda e: e.tensor_copy(out=kpeT[:, 1024 + tt * 128:1024 + (tt + 1) * 128], in_=ps3[0:32, 128:256]), reads=[pb3], writes=[b_kpeT])
        inner.close()
        if ost <= 6:
            return
        A["Ssb"] = [(P.sb("Ssb%d" % i, [128, NK]), P.buf()) for i in range(2)]
        A["Pb"] = [(P.sb("Pb%d" % i, [128, NK], BF16), P.buf()) for i in range(2)]
        A["PT"] = [(P.sb("PTa%d" % i, [128, NK], BF16), P.buf()) for i in range(2)]
        A["sm"] = [(P.sb("asm%d" % i, [128, 8]), P.buf()) for i in range(2)]
        for qt in range(8):
            tok = slice(qt * 128, (qt + 1) * 128)
            for h in range(8):
                g, c, half = h // 4, h // 2, h % 2
                rows = slice(half * 64, half * 64 + 64)
                if sample:
                    k0, k1 = max(0, qt - 1), min(7, qt + 1)
                    masks = []
                    if qt - 1 >= 0:
                        masks.append((0, nmb[:, 0, :]))
                    if qt + 1 <= 7:
                        masks.append(((k1 - k0) * 128, nmb[:, 1, :]))
                    ksegs = [(kdup[g][rows, k0 * 128:(k1 + 1) * 128], (k1 - k0 + 1) * 128, masks), (kdup[g][rows, 1024:1280], 256, [])]
                    kts = list(range(k0, k1 + 1)) + [8, 9]
                else:
                    s0 = (qt // 2) * 2
                    ksegs = [(kdup[g][rows, s0 * 128:(s0 + 2) * 128], 256, [])]
                    kts = [s0, s0 + 1]
                vlist = [vc[:, kt, g * 64:(g + 1) * 64] for kt in kts]
                attn_unit(k, A, qcT[rows, c, tok], ksegs, vlist, 64.0 ** -0.5, sinkneg[:, h:h + 1], oall[:, qt, h * 64:(h + 1) * 64], b_oall[qt],
                          [b_qcT[c], b_kdup[g], b_vc])
        attn_flush(A)
        if ost <= 7:
            return
        QdT = [(P.sb("QdT%d" % i, [96, 1024], BF16), P.buf()) for i in range(2)]
        KTh = [(P.sb("KTh%d" % i, [96, NK], BF16), P.buf()) for i in range(2)]
        Vh = [(P.sb("Vh%d" % i, [128, nkt, 64], BF16), P.buf()) for i in range(2)]
        for h in range(8):
            qd, bqd = QdT[h % 2]
            kt_, bkt = KTh[h % 2]
            for g in range(2):
                ps, pb = k.nxt("mm")
                for kc in range(2):
                    P.op("pe", lambda e: e.matmul(ps[0:96, :], lhsT=wqb[:, kc, h * 96:(h + 1) * 96], rhs=qanT[:, kc, g * 512:(g + 1) * 512], start=(kc == 0), stop=(kc == 1)),
                         reads=[b_wqb, b_qanT], writes=[pb], inc=(kc == 1))
                if sample:
                    rope_fm(k, A, ps, pb, 96, perms[:, 1, :], ropeD[:, 0, :], ropeD[:, 1, :], g, qd[:, g * 512:(g + 1) * 512], bqd)
                else:
                    P.op("act", lambda e: acopy(e, out=qd[:, g * 512:(g + 1) * 512], in_=ps[0:96, :]), reads=[pb], writes=[bqd])
            for c0 in range(0, NK, 512):
                n = min(512, NK - c0)
                ps, pb = k.nxt("mm")
                P.op("pe", lambda e: e.matmul(ps[0:96, 0:n], lhsT=wkvK[:, h, :], rhs=ckvT[:, c0:c0 + n], start=True, stop=False), reads=[b_wkv, b_ckvT], writes=[pb], inc=False)
                P.op("pe", lambda e: e.matmul(ps[0:96, 0:n], lhsT=esel[:], rhs=kpeT[:, c0:c0 + n], start=False, stop=True), reads=[b_wkv, b_kpeT], writes=[pb])
                P.op("dve", lambda e: e.tensor_copy(out=kt_[:, c0:c0 + n], in_=ps[0:96, 0:n]), reads=[pb], writes=[bkt])
            vh, bvh = Vh[h % 2]
            for kt0 in range(0, nkt, 8):
                nk_ = min(8, nkt - kt0)
                ps, pb = k.nxt("mm")
                for kk in range(nk_):
                    P.op("pe", lambda e: e.matmul(ps[:, kk * 64:(kk + 1) * 64], lhsT=ckvT[:, (kt0 + kk) * 128:(kt0 + kk + 1) * 128], rhs=wkvV[:, h, :], start=True, stop=True),
                         reads=[b_ckvT, b_wkv], writes=[pb], inc=(kk == nk_ - 1))
                P.op("act", lambda e: acopy(e, out=vh[:, kt0:kt0 + nk_, :], in_=ps[:, 0:nk_ * 64].rearrange("p (a f) -> p a f", a=nk_)), reads=[pb], writes=[bvh])
            for qt in range(8):
                tok = slice(qt * 128, (qt + 1) * 128)
                if sample:
                    ksegs = [(kt_[:, 0:512], 512, []), (kt_[:, 512:1024], 512, []), (kt_[:, 1024:1280], 256, [])]
                    kts = list(range(10))
                else:
                    s0 = (qt // 2) * 2
                    ksegs = [(kt_[:, s0 * 128:(s0 + 2) * 128], 256, [])]
                    kts = [s0, s0 + 1]
                vlist = [vh[:, kt, :] for kt in kts]
                attn_unit(k, A, qd[:, tok], ksegs, vlist, 96.0 ** -0.5, None, oall[:, qt, 512 + h * 64:512 + (h + 1) * 64], b_oall[qt], [bqd, bkt, bvh])
        attn_flush(A)
        if ost <= 8:
            return
        for qt in range(8):
            ps, pb = k.nxt("tr")
            for c in range(8):
                P.op("pe", lambda e: e.transpose(out=ps[:, c * 128:(c + 1) * 128], in_=oall[:, qt, c * 128:(c + 1) * 128], identity=k.idb[:]), reads=[b_oall[qt], k.b_idb], writes=[pb], inc=(c == 7))
            P.op("dve", lambda e: e.tensor_copy(out=k.mixT[:, 0:4, qt * 128:(qt + 1) * 128], in_=ps[:, 0:512].rearrange("p (c t) -> p c t", c=4)), reads=[pb], writes=[k.b_mixT[qt]])
            P.op("dve", lambda e: e.tensor_copy(out=k.mixT[:, 4:8, qt * 128:(qt + 1) * 128], in_=ps[:, 512:1024].rearrange("p (c t) -> p c t", c=4)), reads=[pb], writes=[k.b_mixT[qt]])
```
